# Optimizing a Trainium2 kernel written in Bass

```python
import math
import jax, jax.numpy as jnp
from jax import lax
import numpy as np

D_MODEL = 1024
BATCH = 2
SEQ = 16384
DEPTH = 2

CHUNK = 64
Q_BLOCK = 128
HEAD_DIM = 64
SB_HEADS = 8
DIFF_HEADS = 4
SB_WIDTH = SB_HEADS * HEAD_DIM
DIFF_QK_WIDTH = DIFF_HEADS * 2 * HEAD_DIM
DIFF_V_WIDTH = DIFF_HEADS * 2 * HEAD_DIM
EVEN_IN_WIDTH = 3 * SB_WIDTH + 2 * DIFF_QK_WIDTH + DIFF_V_WIDTH
POOL_GROUPS = 4
POOL_WINDOWS = (2, 4, 8, 16)
POOL_WIDTH = D_MODEL // 2
POOL_GROUP_DIM = POOL_WIDTH // POOL_GROUPS
CONV_WIDTH = D_MODEL - POOL_WIDTH
CONV_KERNEL = 31
ODD_IN_WIDTH = POOL_WIDTH + 2 * CONV_WIDTH
D_FF = -(-(-(-8 * D_MODEL // 3)) // 256) * 256
ROPE_THETA = 10000.0
EPS = 1e-6
NEG = -1e30

kernel_name = 'chunk_causal_hybrid_sb_diff_pool_conv'


def rmsnorm(x, g):
    xf = x.astype(jnp.float32)
    y = xf * lax.rsqrt(jnp.mean(xf * xf, axis=-1, keepdims=True) + EPS)
    return (y * g.astype(jnp.float32)).astype(x.dtype)


def layernorm(x, g, b):
    xf = x.astype(jnp.float32)
    mu = jnp.mean(xf, axis=-1, keepdims=True)
    var = jnp.mean(jnp.square(xf - mu), axis=-1, keepdims=True)
    y = (xf - mu) * lax.rsqrt(var + EPS) * g.astype(jnp.float32) + b.astype(jnp.float32)
    return y.astype(x.dtype)


def rope(x, pos):
    d = x.shape[-1]
    inv = jnp.power(ROPE_THETA, -jnp.arange(0, d, 2, dtype=jnp.float32) / d)
    ang = pos.astype(jnp.float32)[:, None] * inv[None, :]
    cos = jnp.cos(ang)[None, :, None, :]
    sin = jnp.sin(ang)[None, :, None, :]
    xf = x.astype(jnp.float32)
    x1, x2 = xf[..., : d // 2], xf[..., d // 2:]
    return jnp.concatenate([x1 * cos - x2 * sin, x2 * cos + x1 * sin], axis=-1).astype(x.dtype)


def stick_breaking_attention(q, k, v):
    B, T, H, d = q.shape
    nb = T // Q_BLOCK
    scale = d ** -0.5
    qb = jnp.moveaxis(q.reshape(B, nb, Q_BLOCK, H, d), 1, 0)
    key_pos = jnp.arange(T)

    def block(args):
        qblk, i = args
        z = jnp.einsum('bqhd,bkhd->bhqk', qblk, k, preferred_element_type=jnp.float32) * scale
        qpos = i * Q_BLOCK + jnp.arange(Q_BLOCK)
        strict = key_pos[None, :] < qpos[:, None]
        log_1m = jnp.where(strict, jax.nn.log_sigmoid(-z), 0.0)
        after = lax.cumsum(log_1m, axis=3, reverse=True) - log_1m
        w = jnp.where(strict, jnp.exp(jax.nn.log_sigmoid(z) + after), 0.0)
        return jnp.einsum('bhqk,bkhd->bqhd', w.astype(v.dtype), v)

    out = lax.map(block, (qb, jnp.arange(nb)))
    return jnp.moveaxis(out, 0, 1).reshape(B, T, H, d)


def differential_attention(q, k, v, lam):
    B, T, H, _, d = q.shape
    nb = T // Q_BLOCK
    scale = d ** -0.5
    qb = jnp.moveaxis(q.reshape(B, nb, Q_BLOCK, H, 2, d), 1, 0)
    key_chunk = jnp.arange(T) // CHUNK

    def block(args):
        qblk, i = args
        z = jnp.einsum('bqhmd,bkhmd->bhmqk', qblk, k, preferred_element_type=jnp.float32) * scale
        q_chunk = (i * Q_BLOCK + jnp.arange(Q_BLOCK)) // CHUNK
        mask = key_chunk[None, :] <= q_chunk[:, None]
        p = jax.nn.softmax(jnp.where(mask, z, NEG), axis=-1)
        attn = p[:, :, 0] - lam * p[:, :, 1]
        return jnp.einsum('bhqk,bkhe->bqhe', attn.astype(v.dtype), v)

    out = lax.map(block, (qb, jnp.arange(nb)))
    return jnp.moveaxis(out, 0, 1).reshape(B, T, H, 2 * d)


def even_mixer(h, w_in, lq1, lk1, lq2, lk2, subln_g, w_out, lambda_init, pos):
    B, T, _ = h.shape
    proj = h @ w_in
    cuts = np.cumsum([SB_WIDTH, SB_WIDTH, SB_WIDTH, DIFF_QK_WIDTH, DIFF_QK_WIDTH]).tolist()
    sq, sk, sv, dq, dk, dv = jnp.split(proj, cuts, axis=-1)
    sq = sq.reshape(B, T, SB_HEADS, HEAD_DIM)
    sk = sk.reshape(B, T, SB_HEADS, HEAD_DIM)
    sv = sv.reshape(B, T, SB_HEADS, HEAD_DIM)
    a_out = stick_breaking_attention(sq, sk, sv).reshape(B, T, SB_WIDTH)
    dq = rope(dq.reshape(B, T, DIFF_HEADS * 2, HEAD_DIM), pos).reshape(B, T, DIFF_HEADS, 2, HEAD_DIM)
    dk = rope(dk.reshape(B, T, DIFF_HEADS * 2, HEAD_DIM), pos).reshape(B, T, DIFF_HEADS, 2, HEAD_DIM)
    dv = dv.reshape(B, T, DIFF_HEADS, 2 * HEAD_DIM)
    lam = (jnp.exp(jnp.sum(lq1.astype(jnp.float32) * lk1.astype(jnp.float32)))
           - jnp.exp(jnp.sum(lq2.astype(jnp.float32) * lk2.astype(jnp.float32))) + lambda_init)
    b_out = differential_attention(dq, dk, dv, lam)
    b_out = (rmsnorm(b_out, subln_g) * (1.0 - lambda_init)).reshape(B, T, DIFF_V_WIDTH)
    return jnp.concatenate([a_out, b_out], axis=-1) @ w_out


def multiscale_pool_minus_identity(x):
    B, T, _ = x.shape
    t1 = jnp.arange(1, T + 1)
    outs = []
    for g, w in enumerate(POOL_WINDOWS):
        xs = x[..., g * POOL_GROUP_DIM:(g + 1) * POOL_GROUP_DIM].astype(jnp.float32)
        cs = jnp.concatenate([jnp.zeros((B, 1, POOL_GROUP_DIM), jnp.float32),
                              jnp.cumsum(xs, axis=1)], axis=1)
        lo = jnp.maximum(t1 - w, 0)
        cnt = jnp.minimum(t1, w).astype(jnp.float32)
        outs.append((cs[:, 1:] - cs[:, lo]) / cnt[None, :, None] - xs)
    return jnp.concatenate(outs, axis=-1).astype(x.dtype)


def odd_mixer(h, w_in, pool_w, pool_scale, dw_w, dw_b, cn_g, cn_b, w_out):
    B, T, _ = h.shape
    proj = h @ w_in
    xp = proj[..., :POOL_WIDTH]
    xa = proj[..., POOL_WIDTH:POOL_WIDTH + CONV_WIDTH]
    xg = proj[..., POOL_WIDTH + CONV_WIDTH:]
    pooled = multiscale_pool_minus_identity(xp).reshape(B, T, POOL_GROUPS, POOL_GROUP_DIM)
    c_out = jnp.einsum('btgi,gio->btgo', pooled, pool_w).reshape(B, T, POOL_WIDTH) * pool_scale
    u = xa * jax.nn.sigmoid(xg)
    u = lax.conv_general_dilated(u, dw_w[:, None, :].astype(u.dtype), window_strides=(1,),
                                 padding=[(CONV_KERNEL - 1, 0)],
                                 dimension_numbers=('NWC', 'WIO', 'NWC'),
                                 feature_group_count=CONV_WIDTH) + dw_b
    d_out = jax.nn.silu(layernorm(u, cn_g, cn_b))
    return jnp.concatenate([c_out, d_out], axis=-1) @ w_out


def swiglu_ffn(h, w_gate, w_up, w_down):
    return (jax.nn.silu(h @ w_gate) * (h @ w_up)) @ w_down


def setup_inputs(seed: int = 0) -> dict:
    key = jax.random.key(seed)
    ks = iter(jax.random.split(key, 40))

    def nrm(shape, scale):
        return jax.random.normal(next(ks), shape, jnp.float32) * scale

    def gain(n):
        return 1.0 + nrm((n,), 0.02)

    d = D_MODEL
    return {
        'x': nrm((BATCH, SEQ, d), 1.0),
        'mix_norm_0': gain(d),
        'w_in_0': nrm((d, EVEN_IN_WIDTH), d ** -0.5),
        'lambda_q1_0': nrm((HEAD_DIM,), 0.1),
        'lambda_k1_0': nrm((HEAD_DIM,), 0.1),
        'lambda_q2_0': nrm((HEAD_DIM,), 0.1),
        'lambda_k2_0': nrm((HEAD_DIM,), 0.1),
        'subln_0': gain(2 * HEAD_DIM),
        'w_out_0': nrm((SB_WIDTH + DIFF_V_WIDTH, d), (SB_WIDTH + DIFF_V_WIDTH) ** -0.5),
        'ffn_norm_0': gain(d),
        'w_gate_0': nrm((d, D_FF), d ** -0.5),
        'w_up_0': nrm((d, D_FF), d ** -0.5),
        'w_down_0': nrm((D_FF, d), D_FF ** -0.5),
        'mix_norm_1': gain(d),
        'w_in_1': nrm((d, ODD_IN_WIDTH), d ** -0.5),
        'pool_w_1': nrm((POOL_GROUPS, POOL_GROUP_DIM, POOL_GROUP_DIM), POOL_GROUP_DIM ** -0.5),
        'pool_scale_1': gain(POOL_WIDTH),
        'dw_w_1': nrm((CONV_KERNEL, CONV_WIDTH), CONV_KERNEL ** -0.5),
        'dw_b_1': nrm((CONV_WIDTH,), 0.01),
        'conv_norm_g_1': gain(CONV_WIDTH),
        'conv_norm_b_1': nrm((CONV_WIDTH,), 0.01),
        'w_out_1': nrm((POOL_WIDTH + CONV_WIDTH, d), (POOL_WIDTH + CONV_WIDTH) ** -0.5),
        'ffn_norm_1': gain(d),
        'w_gate_1': nrm((d, D_FF), d ** -0.5),
        'w_up_1': nrm((d, D_FF), d ** -0.5),
        'w_down_1': nrm((D_FF, d), D_FF ** -0.5),
        'final_norm': gain(d),
    }


def reference(x, mix_norm_0, w_in_0, lambda_q1_0, lambda_k1_0, lambda_q2_0, lambda_k2_0,
              subln_0, w_out_0, ffn_norm_0, w_gate_0, w_up_0, w_down_0,
              mix_norm_1, w_in_1, pool_w_1, pool_scale_1, dw_w_1, dw_b_1,
              conv_norm_g_1, conv_norm_b_1, w_out_1, ffn_norm_1, w_gate_1, w_up_1,
              w_down_1, final_norm):
    T = x.shape[1]
    pos = jnp.arange(T, dtype=jnp.int32)
    even_params = [(mix_norm_0, w_in_0, lambda_q1_0, lambda_k1_0, lambda_q2_0, lambda_k2_0,
                    subln_0, w_out_0)]
    odd_params = [(mix_norm_1, w_in_1, pool_w_1, pool_scale_1, dw_w_1, dw_b_1,
                   conv_norm_g_1, conv_norm_b_1, w_out_1)]
    ffn_params = [(ffn_norm_0, w_gate_0, w_up_0, w_down_0),
                  (ffn_norm_1, w_gate_1, w_up_1, w_down_1)]
    for i in range(DEPTH):
        if i % 2 == 0:
            p = even_params[i // 2]
            lambda_init = 0.8 - 0.6 * math.exp(-0.3 * i)
            x = x + even_mixer(rmsnorm(x, p[0]), *p[1:], lambda_init, pos)
        else:
            p = odd_params[i // 2]
            x = x + odd_mixer(rmsnorm(x, p[0]), *p[1:])
        f = ffn_params[i]
        x = x + swiglu_ffn(rmsnorm(x, f[0]), *f[1:])
    return rmsnorm(x, final_norm)
```

```python
import math
import numpy as np
import ml_dtypes
from contextlib import ExitStack
import concourse.bass as bass
import concourse.mybir as mybir
from concourse.bass_utils import run_bass_kernel_spmd

F32 = mybir.dt.float32
BF16 = mybir.dt.bfloat16
AF = mybir.ActivationFunctionType
ALU = mybir.AluOpType
NPBF = ml_dtypes.bfloat16

D = 1024
DFF = 2816
NFC = DFF // 128
BQ = 1024
HALO = 32
SLOT = BQ + HALO
SUB = 352
EPS = 1e-6
LAMBDA_INIT = 0.8 - 0.6 * math.exp(0.0)
SAME_ENGINE_SYNC = True


class DSem:
    __slots__ = ("h", "count")

    def __init__(self, h):
        self.h = h
        self.count = 0


class Res:
    __slots__ = ("name", "w", "r", "dsem")

    def __init__(self, name):
        self.name = name
        self.w = None
        self.r = []
        self.dsem = None


class Prog:
    ENGS = ("pe", "act", "dve", "pool", "sp")

    def __init__(self, nc, n_dma=48):
        self.nc = nc
        self.streams = {e: [] for e in self.ENGS}
        self.count = {e: 0 for e in self.ENGS}
        self.waited = {e: {} for e in self.ENGS}
        self.sem = {}
        for e in ("pe", "act", "dve", "pool"):
            self.sem[e] = nc.alloc_semaphore("prog_" + e)
        self.all_dsems = [DSem(nc.alloc_semaphore("dma_%d" % i)) for i in range(n_dma)]
        self.free_dsems = list(self.all_dsems)
        self.all_res = []

    def res(self, name):
        r = Res(name)
        self.all_res.append(r)
        return r

    def _get_dsem(self, res):
        if res.dsem is None:
            assert self.free_dsems, "out of DMA semaphores"
            res.dsem = self.free_dsems.pop()
        return res.dsem

    def _need(self, eng, tok, waits):
        if tok is None:
            return
        if tok[0] == "eng":
            _, e2, idx = tok
            if e2 == eng and (eng == "pe" or not SAME_ENGINE_SYNC):
                return
            key, sem, val = e2, self.sem[e2], idx
        else:
            _, ds, val = tok
            key, sem = id(ds), ds.h
        if self.waited[eng].get(key, 0) >= val:
            return
        self.waited[eng][key] = val
        for i, (s, v) in enumerate(waits):
            if s is sem:
                waits[i] = (s, max(v, val))
                return
        waits.append((sem, val))

    def _deps(self, eng, reads, writes):
        waits = []
        for r in reads:
            self._need(eng, r.w, waits)
        for w in writes:
            self._need(eng, w.w, waits)
            for t in w.r:
                self._need(eng, t, waits)
        return waits

    def _commit(self, tok, reads, writes):
        for r in reads:
            r.r.append(tok)
            if len(r.r) > 24:
                r.r = _prune(r.r)
        for w in writes:
            w.w = tok
            w.r = []

    def op(self, eng, fn, reads=(), writes=()):
        waits = self._deps(eng, reads, writes)
        self.count[eng] += 1
        tok = ("eng", eng, self.count[eng])
        self.streams[eng].append((waits, fn, (self.sem[eng], 1)))
        self._commit(tok, reads, writes)
        return tok

    def dma(self, eng, out_ap, in_ap, reads=(), writes=()):
        waits = self._deps(eng, reads, writes)
        anchor = writes[0] if writes else reads[0]
        ds = self._get_dsem(anchor)
        ds.count += 16
        tok = ("dma", ds, ds.count)
        self.streams[eng].append(
            (waits, (lambda e, o=out_ap, i=in_ap: e.dma_start(out=o, in_=i)), (ds.h, 16)))
        self._commit(tok, reads, writes)
        return tok

    def barrier(self):
        for eng in self.ENGS:
            waits = []
            for e2 in ("pe", "act", "dve", "pool"):
                if e2 != eng and self.count[e2] > 0:
                    self._need(eng, ("eng", e2, self.count[e2]), waits)
            for ds in self.all_dsems:
                if ds.count > 0:
                    self._need(eng, ("dma", ds, ds.count), waits)
            if waits:
                self.streams[eng].append((waits, None, None))
        for r in self.all_res:
            r.w = None
            r.r = []
            r.dsem = None
        self.free_dsems = list(self.all_dsems)

    def emit(self):
        nc = self.nc
        streams = self.streams
        with nc.Block() as block:
            def mk(engname):
                def body(e):
                    for waits, fn, inc in streams[engname]:
                        if fn is None:
                            for (s, v) in waits:
                                e.wait_ge(s, v)
                            continue
                        for (s, v) in waits[:-1]:
                            e.wait_ge(s, v)
                        ins = fn(e)
                        if waits:
                            ins._wait_ge(waits[-1][0], waits[-1][1])
                        ins.then_inc(inc[0], inc[1])
                return body
            if streams["pe"]:
                block.tensor(mk("pe"))
            if streams["act"]:
                block.scalar(mk("act"))
            if streams["dve"]:
                block.vector(mk("dve"))
            if streams["pool"]:
                block.gpsimd(mk("pool"))
            if streams["sp"]:
                block.sync(mk("sp"))
        self.streams = {e: [] for e in self.ENGS}


def _prune(toks):
    best = {}
    for t in toks:
        key = (t[0], t[1] if t[0] == "eng" else id(t[1]))
        if key not in best or best[key][2] < t[2]:
            best[key] = t
    return list(best.values())


class Rot:
    def __init__(self, P, nc, stack, name, shape, dtype, n, psum=False):
        self.items = []
        for i in range(n):
            t = stack.enter_context(nc.sbuf_tensor("rot_%s_%d" % (name, i), shape, dtype))
            self.items.append((t, P.res("%s%d" % (name, i))))
        self.i = 0

    def next(self):
        it = self.items[self.i % len(self.items)]
        self.i += 1
        return it


def geom(nslot):
    T = 4096 * nslot
    return dict(T=T, TL=T, NKT=T // 128, NQ=nslot * SLOT, NB=T // 1024)


def sub_info(i, s):
    LB = 1024 * (3 + 4 * i)
    q0 = LB - HALO + SUB * s
    kt_hi = (q0 + SUB - 1) // 128
    kbase = LB // 128 - 1
    return LB, q0, kt_hi, kbase


def build_masks():
    sbm, sbi, dm, di = [], {}, [], {}
    for s in range(3):
        LB, q0, kt_hi, kbase = sub_info(0, s)
        for kt in range(kbase, kt_hi + 1):
            r = kt - kbase
            kpos = (128 * kt + np.arange(128))[:, None]
            qpos = (q0 + np.arange(SUB))[None, :]
            m = (kpos < qpos)
            if not m.all():
                sbi[(s, r)] = len(sbm)
                sbm.append(m)
            m2 = (kpos // 64) <= (qpos // 64)
            if not m2.all():
                di[(s, r)] = len(dm)
                dm.append(m2)
    sbm = np.stack(sbm, 1).astype(np.float32).astype(NPBF)
    dm = np.stack(dm, 1).astype(np.float32).astype(NPBF)
    return sbm, sbi, dm, di


def build_nc(nslot, debug=False):
    G = geom(nslot)
    TL, NKT, NQ, NB = G["TL"], G["NKT"], G["NQ"], G["NB"]
    sbmask_np, sbidx, dmask_np, didx = build_masks()
    NSBM, NDM = sbmask_np.shape[1], dmask_np.shape[1]

    nc = bass.Bass("TRN2", target_bir_lowering=False)
    P = Prog(nc)

    def din(name, shape, dt=F32):
        return nc.dram_tensor(name, list(shape), dt, kind="ExternalInput").ap()

    def dscr(name, shape, dt=BF16):
        kind = "ExternalOutput" if debug else "Internal"
        return nc.dram_tensor(name, list(shape), dt, kind=kind).ap()

    xkv = din("xkv", [D, TL])
    xq = din("xq", [D, NQ])
    w_in0 = din("w_in_0", [D, 3072]); w_out0 = din("w_out_0", [D, D])
    w_g0 = din("w_gate_0", [D, DFF]); w_u0 = din("w_up_0", [D, DFF]); w_d0 = din("w_down_0", [DFF, D])
    w_in1 = din("w_in_1", [D, 1536]); w_out1 = din("w_out_1", [D, D])
    w_g1 = din("w_gate_1", [D, DFF]); w_u1 = din("w_up_1", [D, DFF]); w_d1 = din("w_down_1", [DFF, D])
    pool_w = din("pool_w_1", [4, 128, 128])
    gains = din("gains", [128, 6, 8])
    vecs = din("vecs", [128, 24])
    lamv = din("lamv", [4, 64])
    dwT = din("dwT", [128, 4, 31])
    cosk = din("cosk", [128, TL]); sink = din("sink", [128, TL])
    cosq = din("cosq", [128, NQ]); sinq = din("sinq", [128, NQ])
    rcnt = din("rcnt", [4, NQ])
    cmat = din("cmat", [128, 5, 128], BF16)
    identf = din("identf", [128, 128])
    sbmask_d = din("sbmask", [128, NSBM, SUB], BF16)
    dmask_d = din("dmask", [128, NDM, SUB], BF16)
    vones_d = din("vones", [128, 3, 128])
    vtile_d = din("vtile", [128, NKT])
    outT = nc.dram_tensor("outT", [D, nslot * BQ], F32, kind="ExternalOutput").ap()

    W0b = dscr("W0b", [8, 128, 3072]); Wo0b = dscr("Wo0b", [8, 128, D])
    Wg0b = dscr("Wg0b", [8, 128, DFF]); Wu0b = dscr("Wu0b", [8, 128, DFF]); Wd0b = dscr("Wd0b", [NFC, 128, D])
    W1b = dscr("W1b", [8, 128, 1536]); Wo1b = dscr("Wo1b", [8, 128, D])
    Wg1b = dscr("Wg1b", [8, 128, DFF]); Wu1b = dscr("Wu1b", [8, 128, DFF]); Wd1b = dscr("Wd1b", [NFC, 128, D])
    Pwb = dscr("Pwb", [4, 128, 128])
    KTs = dscr("KTs", [8, 128, TL]); Vs = dscr("Vs", [8, 128, NKT, 128])
    QTs = dscr("QTs", [8, 128, NQ]); ATs = dscr("ATs", [8, 128, NQ])
    R_W = {k: P.res(k) for k in ("W0b", "Wo0b", "Wg0b", "Wu0b", "Wd0b", "W1b", "Wo1b", "Wg1b", "Wu1b", "Wd1b", "Pwb")}
    R_KT = [P.res("KTs%d" % c) for c in range(8)]
    R_V = [P.res("Vs%d" % c) for c in range(8)]
    R_Q = [P.res("QTs%d" % c) for c in range(8)]
    R_AT = [P.res("ATs%d" % c) for c in range(8)]
    R_out = P.res("outT")

    cm = nc.alloc_sbuf_tensor("cm", [128, 5, 128], BF16); R_cm = P.res("cm")
    gn = nc.alloc_sbuf_tensor("gn", [128, 6, 8], F32); R_gn = P.res("gn")
    vc = nc.alloc_sbuf_tensor("vc", [128, 24], F32); R_vc = P.res("vc")
    idf = nc.alloc_sbuf_tensor("idf", [128, 128], F32); R_idf = P.res("idf")
    psum = nc.alloc_psum_tensor("psum", [128, 8, 512], F32)
    R_ps = [P.res("ps%d" % b) for b in range(8)]
    ONES, TRINEG, ONESNEG, ROT, IDENT = (cm[:, k, :] for k in range(5))

    P.dma("sp", cm[:], cmat, writes=[R_cm])
    P.dma("sp", gn[:], gains, writes=[R_gn])
    P.dma("sp", vc[:], vecs, writes=[R_vc])
    P.dma("sp", idf[:], identf, writes=[R_idf])

    def rstd_from_ps(ps_ap, R_psb, inv_n, lnv, R_lnv, rr, R_rr, n):
        P.op("act", lambda e: e.activation(out=lnv[:, :n], in_=ps_ap, func=AF.Ln, bias=EPS, scale=inv_n),
             reads=[R_psb], writes=[R_lnv])
        P.op("act", lambda e: e.activation(out=rr[:, :n], in_=lnv[:, :n], func=AF.Exp, scale=-0.5),
             reads=[R_lnv], writes=[R_rr])

    def norm_tile(xt, R_xt, n, sq, R_sq, psb, lnv, R_lnv, rr, R_rr, h, R_h, hoff=0, xoff=0):
        P.op("act", lambda e: e.activation(out=sq[:, :, :n], in_=xt[:, :, xoff:xoff + n], func=AF.Square),
             reads=[R_xt], writes=[R_sq])
        for kc in range(8):
            P.op("pe", lambda e, kc=kc: e.matmul(psum[:, psb, :n], lhsT=ONES, rhs=sq[:, kc, :n],
                                                  start=(kc == 0), stop=(kc == 7)),
                 reads=[R_sq, R_cm], writes=[R_ps[psb]])
        rstd_from_ps(psum[:, psb, :n], R_ps[psb], 1.0 / D, lnv, R_lnv, rr, R_rr, n)
        for kc in range(8):
            P.op("dve", lambda e, kc=kc: e.tensor_tensor(out=h[:, kc, hoff:hoff + n], in0=xt[:, kc, xoff:xoff + n],
                                                          in1=rr[:, :n], op=ALU.mult),
                 reads=[R_xt, R_rr], writes=[R_h])

    with ExitStack() as st:
        fin = Rot(P, nc, st, "fin", [128, 3072], F32, 2)
        fout = Rot(P, nc, st, "fout", [128, 3072], BF16, 2)
        jobs = []
        def add_w(src, dst, rdst, nchunk, C, gi):
            for kc in range(nchunk):
                jobs.append((src[kc * 128:(kc + 1) * 128, :], C, dst[kc], (gn[:, gi, kc:kc + 1] if gi is not None else None), rdst))
        add_w(w_in0, W0b, R_W["W0b"], 8, 3072, 0)
        bg_jobs = []
        def add_bg(src, dst, rdst, nchunk, C, gi):
            for kc in range(nchunk):
                for c0 in range(0, C, 1024):
                    cw = min(1024, C - c0)
                    bg_jobs.append((src[kc * 128:(kc + 1) * 128, c0:c0 + cw], cw, dst[kc, :, c0:c0 + cw],
                                    (gn[:, gi, kc:kc + 1] if gi is not None else None), rdst))
        add_bg(w_out0, Wo0b, R_W["Wo0b"], 8, D, None)
        add_bg(w_g0, Wg0b, R_W["Wg0b"], 8, DFF, 1)
        add_bg(w_u0, Wu0b, R_W["Wu0b"], 8, DFF, 1)
        add_bg(w_d0, Wd0b, R_W["Wd0b"], NFC, D, None)
        add_bg(w_in1, W1b, R_W["W1b"], 8, 1536, 2)
        add_bg(w_out1, Wo1b, R_W["Wo1b"], 8, D, None)
        add_bg(w_g1, Wg1b, R_W["Wg1b"], 8, DFF, 3)
        add_bg(w_u1, Wu1b, R_W["Wu1b"], 8, DFF, 3)
        add_bg(w_d1, Wd1b, R_W["Wd1b"], NFC, D, None)
        for g in range(4):
            bg_jobs.append((pool_w[g], 128, Pwb[g], None, R_W["Pwb"]))
        for n, (src, C, dst, gcol, rdst) in enumerate(jobs):
            ti, R_i = fin.next()
            to, R_o = fout.next()
            P.dma("sp", ti[:, :C], src, writes=[R_i])
            if gcol is not None:
                if n % 2 == 0:
                    P.op("dve", lambda e, ti=ti, to=to, C=C, gcol=gcol: e.tensor_scalar(
                        out=to[:, :C], in0=ti[:, :C], scalar1=gcol, scalar2=None, op0=ALU.mult),
                        reads=[R_i, R_gn], writes=[R_o])
                else:
                    P.op("act", lambda e, ti=ti, to=to, C=C, gcol=gcol: e.activation(
                        out=to[:, :C], in_=ti[:, :C], func=AF.Copy, scale=gcol),
                        reads=[R_i, R_gn], writes=[R_o])
            else:
                if n % 2 == 0:
                    P.op("dve", lambda e, ti=ti, to=to, C=C: e.tensor_copy(out=to[:, :C], in_=ti[:, :C]),
                         reads=[R_i], writes=[R_o])
                else:
                    P.op("act", lambda e, ti=ti, to=to, C=C: e.activation(out=to[:, :C], in_=ti[:, :C], func=AF.Copy),
                         reads=[R_i], writes=[R_o])
            P.dma("pool", dst, to[:, :C], reads=[R_o], writes=[rdst])
        P.barrier()
        P.emit()

    with ExitStack() as st:
        w0 = st.enter_context(nc.sbuf_tensor("s_w0", [128, 8, 3072], BF16)); R_w0 = P.res("w0")
        P.dma("sp", w0[:], W0b.rearrange("k p c -> p k c"), reads=[R_W["W0b"]], writes=[R_w0])
        xts = Rot(P, nc, st, "xt", [128, 8, 512], F32, 2)
        sqs = Rot(P, nc, st, "sq", [128, 8, 512], BF16, 1)
        lnvs = Rot(P, nc, st, "lnv", [128, 512], F32, 1)
        rrs = Rot(P, nc, st, "rr", [128, 512], F32, 1)
        hs = Rot(P, nc, st, "h", [128, 8, 512], BF16, 2)
        coss = Rot(P, nc, st, "cos", [128, 512], F32, 2)
        sins = Rot(P, nc, st, "sin", [128, 512], F32, 2)
        kbs = Rot(P, nc, st, "kb", [128, 512], BF16, 2)
        t1s = Rot(P, nc, st, "t1", [128, 512], F32, 2)
        t2s = Rot(P, nc, st, "t2", [128, 512], F32, 2)
        kst = Rot(P, nc, st, "kst", [128, 512], BF16, 3)
        vst = Rot(P, nc, st, "vst", [128, 8, 128], BF16, 2)
        xkv_v = xkv.rearrange("(k p) n -> p k n", p=128)
        xq_v = xq.rearrange("(k p) n -> p k n", p=128)
        kps_i = [0]

        def proj_T(h, R_h, n, chunks, scale, cos_t, R_cos, sin_t, R_sin, dst, R_dst, n0):
            for (c, col, rope) in chunks:
                pb = 1 + (kps_i[0] % 2); kps_i[0] += 1
                for kc in range(8):
                    P.op("pe", lambda e, kc=kc, pb=pb, col=col: e.matmul(
                        psum[:, pb, :n], lhsT=w0[:, kc, col:col + 128], rhs=h[:, kc, :n],
                        start=(kc == 0), stop=(kc == 7)), reads=[R_w0, R_h], writes=[R_ps[pb]])
                ks, R_ks = kst.next()
                if not rope:
                    P.op("act", lambda e, pb=pb, ks=ks: e.activation(out=ks[:, :n], in_=psum[:, pb, :n],
                                                                     func=AF.Copy, scale=scale),
                         reads=[R_ps[pb]], writes=[R_ks])
                else:
                    kb, R_kb = kbs.next()
                    t1, R_t1 = t1s.next()
                    t2, R_t2 = t2s.next()
                    P.op("act", lambda e, pb=pb, kb=kb: e.activation(out=kb[:, :n], in_=psum[:, pb, :n],
                                                                     func=AF.Copy, scale=scale),
                         reads=[R_ps[pb]], writes=[R_kb])
                    P.op("pe", lambda e, kb=kb: e.matmul(psum[:, 3, :n], lhsT=ROT, rhs=kb[:, :n], start=True, stop=True),
                         reads=[R_kb, R_cm], writes=[R_ps[3]])
                    P.op("dve", lambda e, kb=kb, t1=t1: e.tensor_tensor(out=t1[:, :n], in0=kb[:, :n], in1=cos_t[:, :n], op=ALU.mult),
                         reads=[R_kb, R_cos], writes=[R_t1])
                    P.op("dve", lambda e, t2=t2: e.tensor_tensor(out=t2[:, :n], in0=psum[:, 3, :n], in1=sin_t[:, :n], op=ALU.mult),
                         reads=[R_ps[3], R_sin], writes=[R_t2])
                    P.op("dve", lambda e, t1=t1, t2=t2, ks=ks: e.tensor_tensor(out=ks[:, :n], in0=t1[:, :n], in1=t2[:, :n], op=ALU.add),
                         reads=[R_t1, R_t2], writes=[R_ks])
                P.dma("pool", dst[c, :, n0:n0 + n], ks[:, :n], reads=[R_ks], writes=[R_dst[c]])

        KCH = [(c, 512 + 128 * c, False) for c in range(4)] + [(4 + c, 2048 + 128 * c, True) for c in range(4)]
        QCH = [(c, 128 * c, False) for c in range(4)] + [(4 + c, 1536 + 128 * c, True) for c in range(4)]
        vev = [0]
        for t in range(TL // 512):
            n0 = t * 512
            xt, R_xt = xts.next(); sq, R_sq = sqs.next(); lnv, R_lnv = lnvs.next(); rr, R_rr = rrs.next()
            h, R_h = hs.next(); ct, R_ct = coss.next(); sn, R_sn = sins.next()
            P.dma("sp", xt[:], xkv_v[:, :, n0:n0 + 512], writes=[R_xt])
            P.dma("sp", ct[:], cosk[:, n0:n0 + 512], writes=[R_ct])
            P.dma("sp", sn[:], sink[:, n0:n0 + 512], writes=[R_sn])
            norm_tile(xt, R_xt, 512, sq, R_sq, 0, lnv, R_lnv, rr, R_rr, h, R_h)
            proj_T(h, R_h, 512, KCH, 1.0, ct, R_ct, sn, R_sn, KTs, R_KT, n0)
            for sub in range(4):
                vs_, R_vs = vst.next()
                for half, col in ((0, 1024), (1, 2560)):
                    pb = 4 + (vev[0] % 2)
                    for kc in range(8):
                        P.op("pe", lambda e, kc=kc, pb=pb, col=col, sub=sub, h=h: e.matmul(
                            psum[:, pb, :512], lhsT=h[:, kc, sub * 128:(sub + 1) * 128], rhs=w0[:, kc, col:col + 512],
                            start=(kc == 0), stop=(kc == 7)), reads=[R_w0, R_h], writes=[R_ps[pb]])
                    dstv = vs_[:, 4 * half:4 * half + 4, :]
                    srcv = psum[:, pb, :].rearrange("p (c d) -> p c d", d=128)
                    if vev[0] % 2 == 0:
                        P.op("dve", lambda e, dstv=dstv, srcv=srcv: e.tensor_copy(out=dstv, in_=srcv),
                             reads=[R_ps[pb]], writes=[R_vs])
                    else:
                        P.op("act", lambda e, dstv=dstv, srcv=srcv: e.activation(out=dstv, in_=srcv, func=AF.Copy),
                             reads=[R_ps[pb]], writes=[R_vs])
                    vev[0] += 1
                kt = 4 * t + sub
                P.dma("pool", Vs[:, :, kt, :].rearrange("c p d -> p c d"), vs_[:], reads=[R_vs], writes=R_V)
        for t in range(NQ // SUB):
            n0 = t * SUB
            xt, R_xt = xts.next(); sq, R_sq = sqs.next(); lnv, R_lnv = lnvs.next(); rr, R_rr = rrs.next()
            h, R_h = hs.next(); ct, R_ct = coss.next(); sn, R_sn = sins.next()
            P.dma("sp", xt[:, :, :SUB], xq_v[:, :, n0:n0 + SUB], writes=[R_xt])
            P.dma("sp", ct[:, :SUB], cosq[:, n0:n0 + SUB], writes=[R_ct])
            P.dma("sp", sn[:, :SUB], sinq[:, n0:n0 + SUB], writes=[R_sn])
            norm_tile(xt, R_xt, SUB, sq, R_sq, 0, lnv, R_lnv, rr, R_rr, h, R_h)
            proj_T(h, R_h, SUB, QCH, 0.125, ct, R_ct, sn, R_sn, QTs, R_Q, n0)
        P.barrier()
        P.emit()

    with ExitStack() as st:
        KT_sb = st.enter_context(nc.sbuf_tensor("s_KT_sb", [128, TL], BF16)); R_KTsb = P.res("KT_sb")
        V_sb = st.enter_context(nc.sbuf_tensor("s_V_sb", [128, NKT, 128], BF16)); R_Vsb = P.res("V_sb")
        Qz = [st.enter_context(nc.sbuf_tensor("s_Qz%d" % hh, [128, NQ], BF16)) for hh in range(2)]; R_Qz = P.res("Qz")
        P.op("pool", lambda e: e.memset(Qz[0][:], 0.0), writes=[R_Qz])
        P.op("pool", lambda e: e.memset(Qz[1][:], 0.0), writes=[R_Qz])
        ATo = Rot(P, nc, st, "ATo", [128, NQ], BF16, 2)
        sbm = st.enter_context(nc.sbuf_tensor("s_sbm", [128, NSBM, SUB], BF16)); R_sbm = P.res("sbm")
        dmk = st.enter_context(nc.sbuf_tensor("s_dmk", [128, NDM, SUB], BF16)); R_dmk = P.res("dmk")
        vonesf = st.enter_context(nc.sbuf_tensor("s_vonesf", [128, 3, 128], F32)); R_vonesf = P.res("vonesf")
        lamt = st.enter_context(nc.sbuf_tensor("s_lamt", [128, 4, 64], F32)); R_lamt = P.res("lamt")
        lamp = st.enter_context(nc.sbuf_tensor("s_lamp", [128, 2, 64], F32)); R_lamp = P.res("lamp")
        lams = st.enter_context(nc.sbuf_tensor("s_lams", [128, 4], F32)); R_lams = P.res("lams")
        fs = Rot(P, nc, st, "f", [128, SUB], F32, 6)
        dsqs = Rot(P, nc, st, "dsq", [128, SUB], BF16, 1)
        vtile = st.enter_context(nc.sbuf_tensor("s_vtile", [128, NKT], F32)); R_vtile = P.res("vtile")
        onesf = st.enter_context(nc.sbuf_tensor("s_onesf", [128, 128], F32)); R_onesf = P.res("onesf")
        P.dma("sp", vtile[:], vtile_d, writes=[R_vtile])
        P.op("dve", lambda e: e.memset(onesf[:], 1.0), writes=[R_onesf])

        P.dma("sp", sbm[:], sbmask_d, writes=[R_sbm])
        P.dma("sp", dmk[:], dmask_d, writes=[R_dmk])
        P.dma("sp", vonesf[:], vones_d, writes=[R_vonesf])
        P.dma("sp", lamt[:].rearrange("p a d -> p (a d)"), lamv.rearrange("a d -> (a d)").rearrange("(o n) -> o n", o=1).to_broadcast([128, 256]),
              writes=[R_lamt])
        P.op("dve", lambda e: e.tensor_tensor(out=lamp[:, 0, :], in0=lamt[:, 0, :], in1=lamt[:, 1, :], op=ALU.mult),
             reads=[R_lamt], writes=[R_lamp])
        P.op("dve", lambda e: e.tensor_tensor(out=lamp[:, 1, :], in0=lamt[:, 2, :], in1=lamt[:, 3, :], op=ALU.mult),
             reads=[R_lamt], writes=[R_lamp])
        P.op("dve", lambda e: e.reduce_sum(out=lams[:, 0:2], in_=lamp[:], axis=mybir.AxisListType.X),
             reads=[R_lamp], writes=[R_lams])
        P.op("act", lambda e: e.activation(out=lams[:, 0:2], in_=lams[:, 0:2], func=AF.Exp), reads=[R_lams], writes=[R_lams])
        P.op("dve", lambda e: e.tensor_tensor(out=lams[:, 2:3], in0=lams[:, 1:2], in1=lams[:, 0:1], op=ALU.subtract),
             reads=[R_lams], writes=[R_lams])
        P.op("dve", lambda e: e.tensor_scalar(out=lams[:, 2:3], in0=lams[:, 2:3], scalar1=-LAMBDA_INIT, scalar2=None, op0=ALU.add),
             reads=[R_lams], writes=[R_lams])
        P.op("dve", lambda e: e.tensor_scalar(out=lams[:, 3:4], in0=vc[:, 0:1], scalar1=(1.0 - LAMBDA_INIT), scalar2=None, op0=ALU.mult),
             reads=[R_vc, R_lams], writes=[R_lams])
        NLAM = lams[:, 2:3]
        GSUB = lams[:, 3:4]

        bfin = Rot(P, nc, st, "bfin", [128, 1024], F32, 3)
        bfout = Rot(P, nc, st, "bfout", [128, 1024], BF16, 3)
        bg_state = dict(next=0, inflight=[])

        def bg_tick(k):
            if k % 7 != 0:
                return
            infl = bg_state["inflight"]
            if len(infl) >= 2 and infl[0]["stage"] == 2:
                j = infl.pop(0)
                P.dma("sp", j["dst"], j["to"][:, :j["C"]], reads=[j["R_o"]], writes=[j["rdst"]])
            for j in infl:
                if j["stage"] == 1:
                    ti, to, C, gcol = j["ti"], j["to"], j["C"], j["gcol"]
                    if gcol is not None:
                        P.op("dve", lambda e, ti=ti, to=to, C=C, gcol=gcol: e.tensor_scalar(
                            out=to[:, :C], in0=ti[:, :C], scalar1=gcol, scalar2=None, op0=ALU.mult),
                            reads=[j["R_i"], R_gn], writes=[j["R_o"]])
                    else:
                        P.op("dve", lambda e, ti=ti, to=to, C=C: e.tensor_copy(out=to[:, :C], in_=ti[:, :C]),
                             reads=[j["R_i"]], writes=[j["R_o"]])
                    j["stage"] = 2
                    break
            if bg_state["next"] < len(bg_jobs) and len(infl) < 3:
                src, C, dst, gcol, rdst = bg_jobs[bg_state["next"]]
                bg_state["next"] += 1
                ti, R_i = bfin.next(); to, R_o = bfout.next()
                P.dma("sp", ti[:, :C], src, writes=[R_i])
                infl.append(dict(stage=1, ti=ti, R_i=R_i, to=to, R_o=R_o, C=C, dst=dst, gcol=gcol, rdst=rdst))

        def bg_flush():
            k = 0
            while bg_state["next"] < len(bg_jobs) or bg_state["inflight"]:
                infl = bg_state["inflight"]
                if infl and infl[0]["stage"] == 2 and (len(infl) < 2 or bg_state["next"] >= len(bg_jobs)):
                    j = infl.pop(0)
                    P.dma("sp", j["dst"], j["to"][:, :j["C"]], reads=[j["R_o"]], writes=[j["rdst"]])
                    continue
                bg_tick(0)
                k += 1
                assert k < 100000

        def load_chunk(c):
            P.dma("sp", KT_sb[:], KTs[c], reads=[R_KT[c]], writes=[R_KTsb])
            P.dma("sp", V_sb[:], Vs[c], reads=[R_V[c]], writes=[R_Vsb])
            P.dma("sp", Qz[0][0:64, :], QTs[c, 0:64, :], reads=[R_Q[c]], writes=[R_Qz])
            P.dma("sp", Qz[1][64:128, :], QTs[c, 64:128, :], reads=[R_Q[c]], writes=[R_Qz])

        Srot = [Rot(P, nc, st, "Sp%d" % hh, [128, SUB], BF16, 2) for hh in range(2)]
        e2s = Rot(P, nc, st, "e2", [128, 2, SUB], F32, 2)
        sp2s = Rot(P, nc, st, "sp2", [128, 2, SUB], BF16, 3)
        wf2s = Rot(P, nc, st, "wf2", [128, 2, SUB], F32, 2)
        w2s = Rot(P, nc, st, "w2", [128, 2, SUB], BF16, 3)
        for c in range(4):
            load_chunk(c)
            at, R_at = ATo.next()
            steps = []
            for i in range(nslot):
                for s in range(3):
                    LB, q0, kt_hi, kbase = sub_info(i, s)
                    qc0 = SLOT * i + SUB * s
                    kts = list(range(kt_hi, -1, -1))
                    for n, kt in enumerate(kts):
                        steps.append(dict(qc0=qc0, kt=kt, first=(n == 0), last=(n == len(kts) - 1),
                                          mi=sbidx.get((s, kt - kbase)) if kt >= kbase else None))
            NS = len(steps)
            stt = [dict() for _ in range(NS)]
            Scur = [None, None]

            def sZ(n):
                sd = steps[n]; b = n % 3; kt, qc0 = sd["kt"], sd["qc0"]
                for hh in range(2):
                    P.op("pe", lambda e, hh=hh: e.matmul(psum[:, 2 * b + hh, :SUB], lhsT=KT_sb[:, kt * 128:(kt + 1) * 128],
                                                         rhs=Qz[hh][:, qc0:qc0 + SUB], start=True, stop=False),
                         reads=[R_KTsb, R_Qz], writes=[R_ps[2 * b + hh]])

            def sE(n):
                b = n % 3
                e_, R_e = e2s.next()
                P.op("act", lambda e: e.activation(out=e_[:], in_=psum[:, 2 * b:2 * b + 2, :SUB], func=AF.Exp),
                     reads=[R_ps[2 * b], R_ps[2 * b + 1]], writes=[R_e])
                stt[n]["e"] = (e_, R_e)

            def sSP(n):
                sd = steps[n]
                e_, R_e = stt[n]["e"]
                sp_, R_sp = sp2s.next()
                P.op("act", lambda e: e.activation(out=sp_[:], in_=e_[:], func=AF.Ln, bias=1.0, scale=1.0),
                     reads=[R_e], writes=[R_sp])
                if sd["mi"] is not None:
                    mi = sd["mi"]
                    for hh in range(2):
                        P.op("dve", lambda e, hh=hh: e.tensor_tensor(out=sp_[:, hh, :], in0=sp_[:, hh, :], in1=sbm[:, mi, :], op=ALU.mult),
                             reads=[R_sp, R_sbm], writes=[R_sp])
                stt[n]["sp"] = (sp_, R_sp)

            def sLW(n):
                sd = steps[n]; b = n % 3
                sp_, R_sp = stt[n]["sp"]
                for hh in range(2):
                    lb = 2 * b + hh
                    if sd["first"]:
                        P.op("pe", lambda e, hh=hh, lb=lb: e.matmul(psum[:, lb, :SUB], lhsT=TRINEG, rhs=sp_[:, hh, :], start=False, stop=True),
                             reads=[R_sp, R_cm], writes=[R_ps[lb]])
                    else:
                        S_, R_S = Scur[hh]
                        P.op("pe", lambda e, hh=hh, lb=lb: e.matmul(psum[:, lb, :SUB], lhsT=TRINEG, rhs=sp_[:, hh, :], start=False, stop=False),
                             reads=[R_sp, R_cm], writes=[R_ps[lb]])
                        P.op("pe", lambda e, hh=hh, lb=lb, S_=S_: e.matmul(psum[:, lb, :SUB], lhsT=ONESNEG, rhs=S_[:], start=False, stop=True),
                             reads=[R_S, R_cm], writes=[R_ps[lb]])
                if not sd["last"]:
                    for hh in range(2):
                        eng = "pool" if hh == 0 else "dve"
                        Sn, R_Sn = Srot[hh].next()
                        if sd["first"]:
                            P.op(eng, lambda e, hh=hh, Sn=Sn: e.tensor_copy(out=Sn[:], in_=sp_[:, hh, :]), reads=[R_sp], writes=[R_Sn])
                        else:
                            S_, R_S = Scur[hh]
                            P.op(eng, lambda e, hh=hh, Sn=Sn, S_=S_: e.tensor_tensor(out=Sn[:], in0=S_[:], in1=sp_[:, hh, :], op=ALU.add),
                                 reads=[R_S, R_sp], writes=[R_Sn])
                        Scur[hh] = (Sn, R_Sn)

            def sW(n):
                sd = steps[n]; b = n % 3
                wf_, R_wf = wf2s.next()
                w_, R_w = w2s.next()
                P.op("act", lambda e: e.activation(out=wf_[:], in_=psum[:, 2 * b:2 * b + 2, :SUB], func=AF.Exp),
                     reads=[R_ps[2 * b], R_ps[2 * b + 1]], writes=[R_wf])
                if sd["mi"] is not None:
                    mi = sd["mi"]
                    for hh in range(2):
                        P.op("dve", lambda e, hh=hh: e.tensor_tensor(out=w_[:, hh, :], in0=wf_[:, hh, :], in1=sbm[:, mi, :], op=ALU.mult),
                             reads=[R_wf, R_sbm], writes=[R_w])
                else:
                    P.op("dve", lambda e: e.tensor_copy(out=w_[:], in_=wf_[:]), reads=[R_wf], writes=[R_w])
                stt[n]["w"] = (w_, R_w)

            def sPV(n):
                sd = steps[n]; kt = sd["kt"]; qc0 = sd["qc0"]
                w_, R_w = stt[n]["w"]
                at_, R_at_ = at, R_at
                for hh in range(2):
                    ob = 6 + hh
                    P.op("pe", lambda e, hh=hh, ob=ob: e.matmul(psum[:, ob, :SUB], lhsT=V_sb[:, kt, :], rhs=w_[:, hh, :],
                                                                start=sd["first"], stop=sd["last"]),
                         reads=[R_Vsb, R_w], writes=[R_ps[ob]])
                    if sd["last"]:
                        pr = slice(64 * hh, 64 * hh + 64)
                        P.op("dve", lambda e, pr=pr, ob=ob: e.tensor_copy(out=at_[pr, qc0:qc0 + SUB], in_=psum[pr, ob, :SUB]),
                             reads=[R_ps[ob]], writes=[R_at_])
                stt[n].clear()

            for t in range(NS + 4):
                bg_tick(t)
                if 0 <= t - 1 < NS: sE(t - 1)
                if 0 <= t - 2 < NS: sLW(t - 2)
                if 0 <= t - 3 < NS: sW(t - 3)
                if 0 <= t - 4 < NS: sPV(t - 4)
                if 0 <= t - 1 < NS: sSP(t - 1)
                if t < NS: sZ(t)
            P.dma("pool", ATs[c], at[:], reads=[R_at], writes=[R_AT[c]])

        bg_flush()
        p2s = Rot(P, nc, st, "p2", [128, 2, SUB], BF16, 4)
        acc_rot = Rot(P, nc, st, "acc2", [128, 2, SUB], F32, 5)
        for hd in range(4):
            c = 4 + hd
            load_chunk(c)
            at, R_at = ATo.next()
            for i in range(nslot):
                for s in range(3):
                    LB, q0, kt_hi, kbase = sub_info(i, s)
                    qc0 = SLOT * i + SUB * s
                    steps = []
                    for kt in range(0, kt_hi + 1):
                        steps.append(dict(kt=kt, first=(kt == 0), last=(kt == kt_hi),
                                          mi=didx.get((s, kt - kbase)) if kt >= kbase else None))
                    NS = len(steps)
                    pst = [None] * NS
                    accs = {k: acc_rot.next() for k in ("mD", "mP", "b0", "b1", "b2")}
                    acc_eng = dict(mD="dve", mP="pool", b0="dve", b1="pool", b2="dve")
                    accstate = {}

                    def dZ(n):
                        sd = steps[n]; kt = sd["kt"]; b = n % 2; qq = qc0
                        for m in range(2):
                            P.op("pe", lambda e, m=m: e.matmul(psum[:, 2 * b + m, :SUB], lhsT=KT_sb[:, kt * 128:(kt + 1) * 128],
                                                               rhs=Qz[m][:, qq:qq + SUB], start=True, stop=True),
                                 reads=[R_KTsb, R_Qz], writes=[R_ps[2 * b + m]])

                    def dP(n):
                        sd = steps[n]; b = n % 2
                        p_, R_p = p2s.next()
                        P.op("act", lambda e: e.activation(out=p_[:], in_=psum[:, 2 * b:2 * b + 2, :SUB], func=AF.Exp),
                             reads=[R_ps[2 * b], R_ps[2 * b + 1]], writes=[R_p])
                        if sd["mi"] is not None:
                            mi = sd["mi"]
                            for m in range(2):
                                P.op("dve", lambda e, m=m: e.tensor_tensor(out=p_[:, m, :], in0=p_[:, m, :], in1=dmk[:, mi, :], op=ALU.mult),
                                     reads=[R_p, R_dmk], writes=[R_p])
                        pst[n] = (p_, R_p)

                    def dC(n):
                        sd = steps[n]; kt = sd["kt"]
                        p_, R_p = pst[n]
                        for m in range(2):
                            P.op("pe", lambda e, m=m: e.matmul(psum[:, 4 + m, :SUB], lhsT=V_sb[:, kt, :], rhs=p_[:, m, :],
                                                               start=sd["first"], stop=sd["last"]),
                                 reads=[R_Vsb, R_p], writes=[R_ps[4 + m]])
                        if kt < 24:
                            ak = "b%d" % (kt // 8)
                        else:
                            ak = "mP" if ((kt - 24) % 9) in (1, 3, 5, 7) else "mD"
                        eng = acc_eng[ak]
                        pa_, R_pa = accs[ak]
                        if ak not in accstate:
                            accstate[ak] = True
                            P.op(eng, lambda e: e.tensor_copy(out=pa_[:], in_=p_[:]), reads=[R_p], writes=[R_pa])
                        else:
                            P.op(eng, lambda e: e.tensor_tensor(out=pa_[:], in0=pa_[:], in1=p_[:], op=ALU.add),
                                 reads=[R_p, R_pa], writes=[R_pa])
                        if sd["last"]:
                            order = ["mD", "mP", "b0", "b1", "b2"]
                            for m in range(2):
                                for qi, ak2 in enumerate(order):
                                    pa2, R_pa2 = accs[ak2]
                                    lh = onesf[:] if ak2[0] == "m" else vonesf[:, int(ak2[1]), :]
                                    P.op("pe", lambda e, m=m, pa2=pa2, lh=lh, qi=qi: e.matmul(
                                        psum[:, 6 + m, :SUB], lhsT=lh, rhs=pa2[:, m, :], start=(qi == 0), stop=(qi == 4)),
                                        reads=[R_pa2, R_onesf, R_vonesf], writes=[R_ps[6 + m]])
                        pst[n] = None

                    for t in range(NS + 2):
                        if 0 <= t - 1 < NS: dP(t - 1)
                        if 0 <= t - 2 < NS: dC(t - 2)
                        if t < NS: dZ(t)
                    on = []
                    for m in range(2):
                        r_, R_r = fs.next()
                        P.op("dve", lambda e, r_=r_, m=m: e.tensor_scalar(out=r_[:], in0=psum[:, 6 + m, :SUB], scalar1=1e-30, scalar2=None, op0=ALU.max),
                             reads=[R_ps[6 + m]], writes=[R_r])
                        P.op("dve", lambda e, r_=r_: e.reciprocal(out=r_[:], in_=r_[:]), reads=[R_r], writes=[R_r])
                        o_, R_o = fs.next()
                        P.op("dve", lambda e, r_=r_, o_=o_, m=m: e.tensor_tensor(out=o_[:], in0=psum[:, 4 + m, :SUB], in1=r_[:], op=ALU.mult),
                             reads=[R_ps[4 + m], R_r], writes=[R_o])
                        on.append((o_, R_o))
                    d_, R_d = fs.next()
                    P.op("dve", lambda e, d_=d_, on=on: e.scalar_tensor_tensor(out=d_[:], in0=on[1][0][:], scalar=NLAM, in1=on[0][0][:],
                                                                                 op0=ALU.mult, op1=ALU.add),
                         reads=[on[0][1], on[1][1], R_lams], writes=[R_d])
                    dq_, R_dq = dsqs.next()
                    P.op("act", lambda e, d_=d_, dq_=dq_: e.activation(out=dq_[:], in_=d_[:], func=AF.Square), reads=[R_d], writes=[R_dq])
                    P.op("pe", lambda e, dq_=dq_: e.matmul(psum[:, 0, :SUB], lhsT=ONES, rhs=dq_[:], start=True, stop=True),
                         reads=[R_dq, R_cm], writes=[R_ps[0]])
                    ln_, R_ln = fs.next()
                    P.op("act", lambda e, ln_=ln_: e.activation(out=ln_[:], in_=psum[:, 0, :SUB], func=AF.Ln, bias=EPS, scale=1.0 / 128),
                         reads=[R_ps[0]], writes=[R_ln])
                    P.op("act", lambda e, ln_=ln_: e.activation(out=ln_[:], in_=ln_[:], func=AF.Exp, scale=-0.5),
                         reads=[R_ln], writes=[R_ln])
                    P.op("dve", lambda e, d_=d_, ln_=ln_, qc0=qc0, at=at: e.scalar_tensor_tensor(
                        out=at[:, qc0:qc0 + SUB], in0=d_[:], scalar=GSUB, in1=ln_[:], op0=ALU.mult, op1=ALU.mult),
                        reads=[R_d, R_ln, R_lams], writes=[R_at])
            P.dma("pool", ATs[c], at[:], reads=[R_at], writes=[R_AT[c]])
        P.barrier()
        P.emit()

    with ExitStack() as st:
        xres = st.enter_context(nc.sbuf_tensor("s_xres", [128, 8, SLOT], F32)); R_x = P.res("xres")
        hb = st.enter_context(nc.sbuf_tensor("s_hb", [128, 8, SLOT], BF16)); R_hb = P.res("hb")
        big = st.enter_context(nc.sbuf_tensor("s_big", [128, NFC, SLOT], BF16)); R_big = P.res("big")
        wbufs = Rot(P, nc, st, "wb", [128, NFC * 128], BF16, 4)
        sqs = Rot(P, nc, st, "sq3", [128, 8, 512], BF16, 1)
        lnvs = Rot(P, nc, st, "lnv3", [128, 512], F32, 1)
        rrs = Rot(P, nc, st, "rr3", [128, 512], F32, 1)
        sgs = Rot(P, nc, st, "sg", [128, 512], F32, 2)
        xps = Rot(P, nc, st, "xp", [128, SLOT], F32, 1)
        pa = Rot(P, nc, st, "pa", [128, SLOT], F32, 2)
        rcs = Rot(P, nc, st, "rc", [128, BQ], F32, 1)
        plb = Rot(P, nc, st, "plb", [128, BQ], BF16, 1)
        diag = st.enter_context(nc.sbuf_tensor("s_diag", [128, 31, 128], BF16)); R_diag = P.res("diag")
        dwt = st.enter_context(nc.sbuf_tensor("s_dwt", [128, 4, 31], F32)); R_dwt = P.res("dwt")
        ucf = st.enter_context(nc.sbuf_tensor("s_ucf", [128, 4, 512], F32)); R_ucf = P.res("ucf")
        ucbq = st.enter_context(nc.sbuf_tensor("s_ucbq", [128, 8, 512], BF16)); R_ucb = P.res("ucb"); R_ucq = P.res("ucq")
        ucb = ucbq[:, 0:4, :]; ucq = ucbq[:, 4:8, :]
        mts = Rot(P, nc, st, "mt", [128, 512], F32, 3)
        ots = Rot(P, nc, st, "ot", [128, 512], F32, 2)
        P.dma("sp", dwt[:], dwT, writes=[R_dwt])
        xq_v = xq.rearrange("(k p) n -> p k n", p=128)
        ATv = big[:, 0:8, :]
        psi = [0]

        def next_pb(lo=0, n=4):
            pb = lo + psi[0] % n
            psi[0] += 1
            return pb

        def linear(Wd, R_Wd, KC, col0, ncols, group, rhs, R_rhs, subs, evac):
            for g0 in range(0, ncols, group):
                gw = min(group, ncols - g0)
                wb, R_wb = wbufs.next()
                wv = wb[:, :KC * gw].rearrange("p (k c) -> p k c", c=gw)
                P.dma("sp", wv, Wd[:, :, col0 + g0:col0 + g0 + gw].rearrange("k p c -> p k c"), reads=[R_Wd], writes=[R_wb])
                for ol in range(gw // 128):
                    oc = (g0 // 128) + ol
                    for si, (n0, n) in enumerate(subs):
                        pb = next_pb()
                        for kc in range(KC):
                            P.op("pe", lambda e, kc=kc, pb=pb, ol=ol, n0=n0, n=n, wv=wv: e.matmul(
                                psum[:, pb, :n], lhsT=wv[:, kc, ol * 128:(ol + 1) * 128], rhs=rhs(kc, n0, n),
                                start=(kc == 0), stop=(kc == KC - 1)), reads=[R_wb] + R_rhs, writes=[R_ps[pb]])
                        evac(oc, si, n0, n, pb)

        def norm_slot(subs):
            for (n0, n) in subs:
                sq, R_sq = sqs.next(); lnv, R_lnv = lnvs.next(); rr, R_rr = rrs.next()
                norm_tile(xres, R_x, n, sq, R_sq, 7, lnv, R_lnv, rr, R_rr, hb, R_hb, hoff=n0, xoff=n0)

        def ffn(Wg, Wu, Wdn, kg, ku, kd, subs):
            norm_slot(subs)
            for f0 in range(0, DFF, 256):
                gw = min(256, DFF - f0)
                wg, R_wg = wbufs.next(); wu, R_wu = wbufs.next()
                wgv = wg[:, :8 * gw].rearrange("p (k c) -> p k c", c=gw)
                wuv = wu[:, :8 * gw].rearrange("p (k c) -> p k c", c=gw)
                P.dma("sp", wgv, Wg[:, :, f0:f0 + gw].rearrange("k p c -> p k c"), reads=[R_W[kg]], writes=[R_wg])
                P.dma("sp", wuv, Wu[:, :, f0:f0 + gw].rearrange("k p c -> p k c"), reads=[R_W[ku]], writes=[R_wu])
                for ol in range(gw // 128):
                    fc = f0 // 128 + ol
                    for (n0, n) in subs:
                        pg = next_pb(0, 6); pu = next_pb(0, 6)
                        for kc in range(8):
                            P.op("pe", lambda e, kc=kc, pg=pg, ol=ol, n0=n0, n=n, wgv=wgv: e.matmul(
                                psum[:, pg, :n], lhsT=wgv[:, kc, ol * 128:(ol + 1) * 128], rhs=hb[:, kc, n0:n0 + n],
                                start=(kc == 0), stop=(kc == 7)), reads=[R_wg, R_hb], writes=[R_ps[pg]])
                        for kc in range(8):
                            P.op("pe", lambda e, kc=kc, pu=pu, ol=ol, n0=n0, n=n, wuv=wuv: e.matmul(
                                psum[:, pu, :n], lhsT=wuv[:, kc, ol * 128:(ol + 1) * 128], rhs=hb[:, kc, n0:n0 + n],
                                start=(kc == 0), stop=(kc == 7)), reads=[R_wu, R_hb], writes=[R_ps[pu]])
                        sg, R_sg = sgs.next()
                        P.op("act", lambda e, sg=sg, pg=pg, n=n: e.activation(out=sg[:, :n], in_=psum[:, pg, :n], func=AF.Silu),
                             reads=[R_ps[pg]], writes=[R_sg])
                        P.op("dve", lambda e, sg=sg, pu=pu, n=n, n0=n0, fc=fc: e.tensor_tensor(
                            out=big[:, fc, n0:n0 + n], in0=sg[:, :n], in1=psum[:, pu, :n], op=ALU.mult),
                            reads=[R_sg, R_ps[pu]], writes=[R_big])
            def ev(oc, si, n0, n, pb):
                P.op("dve", lambda e: e.tensor_tensor(out=xres[:, oc, n0:n0 + n], in0=xres[:, oc, n0:n0 + n],
                                                      in1=psum[:, pb, :n], op=ALU.add),
                     reads=[R_x, R_ps[pb]], writes=[R_x])
            linear(Wdn, R_W[kd], NFC, 0, D, 128, lambda kc, n0, n: big[:, kc, n0:n0 + n], [R_big], subs, ev)

        SUBS3 = [(0, SUB), (SUB, SUB), (2 * SUB, SUB)]
        SUBS2 = [(HALO, 512), (HALO + 512, 512)]
        for i in range(nslot):
            c0 = SLOT * i
            P.dma("sp", xres[:], xq_v[:, :, c0:c0 + SLOT], writes=[R_x])
            P.dma("sp", ATv, ATs[:, :, c0:c0 + SLOT].rearrange("c p n -> p c n"), reads=R_AT, writes=[R_big])
            def ev0(oc, si, n0, n, pb):
                P.op("dve", lambda e: e.tensor_tensor(out=xres[:, oc, n0:n0 + n], in0=xres[:, oc, n0:n0 + n],
                                                      in1=psum[:, pb, :n], op=ALU.add),
                     reads=[R_x, R_ps[pb]], writes=[R_x])
            linear(Wo0b, R_W["Wo0b"], 8, 0, D, 256, lambda kc, n0, n: big[:, kc, n0:n0 + n], [R_big], SUBS3, ev0)
            ffn(Wg0b, Wu0b, Wd0b, "Wg0b", "Wu0b", "Wd0b", SUBS3)
            norm_slot(SUBS3)
            for g in range(4):
                wb, R_wb = wbufs.next()
                wv = wb[:, :8 * 128].rearrange("p (k c) -> p k c", c=128)
                P.dma("sp", wv, W1b[:, :, 128 * g:128 * g + 128].rearrange("k p c -> p k c"), reads=[R_W["W1b"]], writes=[R_wb])
                pw, R_pw = wbufs.next()
                P.dma("sp", pw[:, :128], Pwb[g], reads=[R_W["Pwb"]], writes=[R_pw])
                xp, R_xp = xps.next()
                for (n0, n) in SUBS3:
                    pb = next_pb()
                    for kc in range(8):
                        P.op("pe", lambda e, kc=kc, pb=pb, n0=n0, n=n, wv=wv: e.matmul(
                            psum[:, pb, :n], lhsT=wv[:, kc, :], rhs=hb[:, kc, n0:n0 + n], start=(kc == 0), stop=(kc == 7)),
                            reads=[R_wb, R_hb], writes=[R_ps[pb]])
                    P.op("act", lambda e, pb=pb, n0=n0, n=n, xp=xp: e.activation(out=xp[:, n0:n0 + n], in_=psum[:, pb, :n], func=AF.Copy),
                         reads=[R_ps[pb]], writes=[R_xp])
                cur, R_cur = xp, R_xp
                lo, wdt = 0, 1
                for k in range(g + 1):
                    nx, R_nx = pa.next()
                    lo2 = lo + wdt
                    P.op("dve", lambda e, cur=cur, nx=nx, lo2=lo2, wdt=wdt: e.tensor_tensor(
                        out=nx[:, lo2:SLOT], in0=cur[:, lo2:SLOT], in1=cur[:, lo2 - wdt:SLOT - wdt], op=ALU.add),
                        reads=[R_cur], writes=[R_nx])
                    cur, R_cur, lo, wdt = nx, R_nx, lo2, wdt * 2
                rc, R_rc = rcs.next()
                P.dma("sp", rc[:], rcnt[g:g + 1, c0 + HALO:c0 + SLOT].to_broadcast([128, BQ]), writes=[R_rc])
                P.op("dve", lambda e, cur=cur, rc=rc: e.tensor_tensor(out=cur[:, HALO:SLOT], in0=cur[:, HALO:SLOT], in1=rc[:], op=ALU.mult),
                     reads=[R_cur, R_rc], writes=[R_cur])
                pl, R_pl = plb.next()
                P.op("dve", lambda e, cur=cur, xp=xp, pl=pl: e.tensor_tensor(out=pl[:], in0=cur[:, HALO:SLOT], in1=xp[:, HALO:SLOT], op=ALU.subtract),
                     reads=[R_cur, R_xp], writes=[R_pl])
                for hi in range(2):
                    pb = next_pb()
                    P.op("pe", lambda e, pb=pb, hi=hi, pw=pw, pl=pl: e.matmul(psum[:, pb, :512], lhsT=pw[:, :128], rhs=pl[:, hi * 512:(hi + 1) * 512],
                                                                               start=True, stop=True),
                         reads=[R_pw, R_pl], writes=[R_ps[pb]])
                    P.op("act", lambda e, pb=pb, hi=hi, g=g: e.activation(out=big[:, 8 + g, hi * 512:(hi + 1) * 512], in_=psum[:, pb, :512],
                                                                            func=AF.Copy, scale=vc[:, 1 + g:2 + g]),
                         reads=[R_ps[pb], R_vc], writes=[R_big])
            for cc in range(4):
                wa, R_wa = wbufs.next(); wg_, R_wg_ = wbufs.next()
                wav = wa[:, :8 * 128].rearrange("p (k c) -> p k c", c=128)
                wgv = wg_[:, :8 * 128].rearrange("p (k c) -> p k c", c=128)
                P.dma("sp", wav, W1b[:, :, 512 + 128 * cc:640 + 128 * cc].rearrange("k p c -> p k c"), reads=[R_W["W1b"]], writes=[R_wa])
                P.dma("sp", wgv, W1b[:, :, 1024 + 128 * cc:1152 + 128 * cc].rearrange("k p c -> p k c"), reads=[R_W["W1b"]], writes=[R_wg_])
                for (n0, n) in SUBS3:
                    pa_ = next_pb(0, 6); pg_ = next_pb(0, 6)
                    for kc in range(8):
                        P.op("pe", lambda e, kc=kc, pa_=pa_, n0=n0, n=n, wav=wav: e.matmul(
                            psum[:, pa_, :n], lhsT=wav[:, kc, :], rhs=hb[:, kc, n0:n0 + n], start=(kc == 0), stop=(kc == 7)),
                            reads=[R_wa, R_hb], writes=[R_ps[pa_]])
                    for kc in range(8):
                        P.op("pe", lambda e, kc=kc, pg_=pg_, n0=n0, n=n, wgv=wgv: e.matmul(
                            psum[:, pg_, :n], lhsT=wgv[:, kc, :], rhs=hb[:, kc, n0:n0 + n], start=(kc == 0), stop=(kc == 7)),
                            reads=[R_wg_, R_hb], writes=[R_ps[pg_]])
                    sg, R_sg = sgs.next()
                    P.op("act", lambda e, sg=sg, pg_=pg_, n=n: e.activation(out=sg[:, :n], in_=psum[:, pg_, :n], func=AF.Sigmoid),
                         reads=[R_ps[pg_]], writes=[R_sg])
                    P.op("dve", lambda e, sg=sg, pa_=pa_, n=n, n0=n0, cc=cc: e.tensor_tensor(
                        out=big[:, cc, n0:n0 + n], in0=sg[:, :n], in1=psum[:, pa_, :n], op=ALU.mult),
                        reads=[R_sg, R_ps[pa_]], writes=[R_big])
            for hi in range(2):
                t0 = hi * 512
                for cc in range(4):
                    for j in range(31):
                        P.op("dve", lambda e, cc=cc, j=j: e.tensor_scalar(out=diag[:, j, :], in0=idf[:], scalar1=dwt[:, cc, j:j + 1],
                                                                          scalar2=None, op0=ALU.mult),
                             reads=[R_idf, R_dwt], writes=[R_diag])
                    pb = next_pb()
                    for j in range(31):
                        P.op("pe", lambda e, cc=cc, j=j, pb=pb, t0=t0: e.matmul(
                            psum[:, pb, :512], lhsT=diag[:, j, :], rhs=big[:, cc, t0 + 2 + j:t0 + 2 + j + 512],
                            start=(j == 0), stop=(j == 30)), reads=[R_diag, R_big], writes=[R_ps[pb]])
                    P.op("act", lambda e, cc=cc, pb=pb: e.activation(out=ucf[:, cc, :], in_=psum[:, pb, :512], func=AF.Identity,
                                                                      bias=vc[:, 5 + cc:6 + cc], scale=1.0),
                         reads=[R_ps[pb], R_vc], writes=[R_ucf])
                P.op("dve", lambda e: e.tensor_copy(out=ucb, in_=ucf[:]), reads=[R_ucf], writes=[R_ucb])
                P.op("act", lambda e: e.activation(out=ucq, in_=ucf[:], func=AF.Square), reads=[R_ucf], writes=[R_ucq])
                for cc in range(4):
                    P.op("pe", lambda e, cc=cc: e.matmul(psum[:, 6, :512], lhsT=ONES, rhs=ucb[:, cc, :], start=(cc == 0), stop=(cc == 3)),
                         reads=[R_ucb, R_cm], writes=[R_ps[6]])
                for cc in range(4):
                    P.op("pe", lambda e, cc=cc: e.matmul(psum[:, 7, :512], lhsT=ONES, rhs=ucq[:, cc, :], start=(cc == 0), stop=(cc == 3)),
                         reads=[R_ucq, R_cm], writes=[R_ps[7]])
                mt, R_mt = mts.next(); m2, R_m2 = mts.next(); vt, R_vt = mts.next()
                P.op("dve", lambda e, mt=mt: e.tensor_scalar(out=mt[:], in0=psum[:, 6, :512], scalar1=1.0 / 512, scalar2=None, op0=ALU.mult),
                     reads=[R_ps[6]], writes=[R_mt])
                P.op("dve", lambda e, mt=mt, m2=m2: e.tensor_tensor(out=m2[:], in0=mt[:], in1=mt[:], op=ALU.mult), reads=[R_mt], writes=[R_m2])
                P.op("dve", lambda e, m2=m2, vt=vt: e.scalar_tensor_tensor(out=vt[:], in0=psum[:, 7, :512], scalar=1.0 / 512, in1=m2[:],
                                                                            op0=ALU.mult, op1=ALU.subtract),
                     reads=[R_ps[7], R_m2], writes=[R_vt])
                P.op("act", lambda e, vt=vt: e.activation(out=vt[:], in_=vt[:], func=AF.Ln, bias=EPS, scale=1.0), reads=[R_vt], writes=[R_vt])
                P.op("act", lambda e, vt=vt: e.activation(out=vt[:], in_=vt[:], func=AF.Exp, scale=-0.5), reads=[R_vt], writes=[R_vt])
                for cc in range(4):
                    P.op("dve", lambda e, cc=cc, mt=mt: e.tensor_tensor(out=ucf[:, cc, :], in0=ucf[:, cc, :], in1=mt[:], op=ALU.subtract),
                         reads=[R_ucf, R_mt], writes=[R_ucf])
                    P.op("dve", lambda e, cc=cc, vt=vt: e.tensor_tensor(out=ucf[:, cc, :], in0=ucf[:, cc, :], in1=vt[:], op=ALU.mult),
                         reads=[R_ucf, R_vt], writes=[R_ucf])
                    P.op("act", lambda e, cc=cc, t0=t0: e.activation(out=big[:, 12 + cc, t0:t0 + 512], in_=ucf[:, cc, :], func=AF.Silu,
                                                                      bias=vc[:, 13 + cc:14 + cc], scale=vc[:, 9 + cc:10 + cc]),
                         reads=[R_ucf, R_vc], writes=[R_big])
            def ev1(oc, si, n0, n, pb):
                P.op("dve", lambda e: e.tensor_tensor(out=xres[:, oc, HALO + n0:HALO + n0 + n], in0=xres[:, oc, HALO + n0:HALO + n0 + n],
                                                      in1=psum[:, pb, :n], op=ALU.add),
                     reads=[R_x, R_ps[pb]], writes=[R_x])
            linear(Wo1b, R_W["Wo1b"], 8, 0, D, 256, lambda kc, n0, n: big[:, 8 + kc, n0:n0 + n], [R_big], [(0, 512), (512, 512)], ev1)
            ffn(Wg1b, Wu1b, Wd1b, "Wg1b", "Wu1b", "Wd1b", SUBS2)
            for (n0, n) in SUBS2:
                sq, R_sq = sqs.next(); lnv, R_lnv = lnvs.next(); rr, R_rr = rrs.next()
                P.op("act", lambda e, sq=sq, n0=n0, n=n: e.activation(out=sq[:, :, :n], in_=xres[:, :, n0:n0 + n], func=AF.Square),
                     reads=[R_x], writes=[R_sq])
                for kc in range(8):
                    P.op("pe", lambda e, kc=kc, sq=sq, n=n: e.matmul(psum[:, 7, :n], lhsT=ONES, rhs=sq[:, kc, :n], start=(kc == 0), stop=(kc == 7)),
                         reads=[R_sq, R_cm], writes=[R_ps[7]])
                rstd_from_ps(psum[:, 7, :n], R_ps[7], 1.0 / D, lnv, R_lnv, rr, R_rr, n)
                for kc in range(8):
                    ot, R_ot = ots.next()
                    P.op("dve", lambda e, kc=kc, ot=ot, rr=rr, n0=n0, n=n: e.scalar_tensor_tensor(
                        out=ot[:, :n], in0=xres[:, kc, n0:n0 + n], scalar=gn[:, 4, kc:kc + 1], in1=rr[:, :n], op0=ALU.mult, op1=ALU.mult),
                        reads=[R_x, R_rr, R_gn], writes=[R_ot])
                    o0 = BQ * i + n0 - HALO
                    P.dma("pool", outT[kc * 128:(kc + 1) * 128, o0:o0 + n], ot[:, :n], reads=[R_ot], writes=[R_out])
        P.barrier()
        P.emit()
    return nc


def _const_mats():
    ones = np.ones((128, 128), np.float32)
    p_ = np.arange(128)
    trineg = -(p_[:, None] >= p_[None, :]).astype(np.float32)
    onesneg = -ones
    rot = np.zeros((128, 128), np.float32)
    for p in range(128):
        if p % 64 < 32:
            rot[p + 32, p] = -1.0
        else:
            rot[p - 32, p] = 1.0
    ident = np.eye(128, dtype=np.float32)
    return np.stack([ones, trineg, onesneg, rot, ident], 1).astype(NPBF), ident


def _rope_tables(pos):
    inv = (10000.0 ** (-(np.arange(0, 64, 2, dtype=np.float32)) / 64.0)).astype(np.float32)
    ang = pos.astype(np.float32)[None, :] * inv[np.arange(128) % 32][:, None]
    return np.cos(ang).astype(np.float32), np.sin(ang).astype(np.float32)


def make_in_maps(inputs, nslot):
    G = geom(nslot)
    T, TL, NQ, NB = G["T"], G["TL"], G["NQ"], G["NB"]
    x = np.asarray(inputs["x"], np.float32)
    cmat, ident = _const_mats()
    sbm, _, dm, _ = build_masks()
    f = lambda k: np.ascontiguousarray(np.asarray(inputs[k], np.float32))
    gains = np.zeros((128, 6, 8), np.float32)
    for gi, k in enumerate(["mix_norm_0", "ffn_norm_0", "mix_norm_1", "ffn_norm_1", "final_norm"]):
        gains[:, gi, :] = f(k).reshape(8, 128).T
    vecs = np.zeros((128, 24), np.float32)
    vecs[:, 0] = f("subln_0")
    vecs[:, 1:5] = f("pool_scale_1").reshape(4, 128).T
    vecs[:, 5:9] = f("dw_b_1").reshape(4, 128).T
    vecs[:, 9:13] = f("conv_norm_g_1").reshape(4, 128).T
    vecs[:, 13:17] = f("conv_norm_b_1").reshape(4, 128).T
    lamv = np.stack([f("lambda_q1_0"), f("lambda_k1_0"), f("lambda_q2_0"), f("lambda_k2_0")], 0)
    dwT = np.ascontiguousarray(f("dw_w_1").T.reshape(4, 128, 31).transpose(1, 0, 2))
    shared = dict(w_in_0=f("w_in_0"), w_out_0=f("w_out_0"), w_gate_0=f("w_gate_0"), w_up_0=f("w_up_0"), w_down_0=f("w_down_0"),
                  w_in_1=f("w_in_1"), w_out_1=f("w_out_1"), w_gate_1=f("w_gate_1"), w_up_1=f("w_up_1"), w_down_1=f("w_down_1"),
                  pool_w_1=f("pool_w_1"), gains=gains, vecs=vecs, lamv=lamv, dwT=dwT, cmat=cmat, identf=ident,
                  sbmask=sbm, dmask=dm)
    maps = []
    for c in range(8):
        b, j = c // 4, c % 4
        off = 1024 * (3 - j)
        xT = x[b, :T].T
        xkv = np.zeros((D, TL), np.float32)
        xkv[:, off:] = xT[:, :TL - off]
        posk = np.maximum(np.arange(TL) - off, 0)
        cosk, sink = _rope_tables(posk)
        xq = np.zeros((D, NQ), np.float32)
        posq = np.zeros(NQ, np.int64)
        rc = np.zeros((4, NQ), np.float32)
        for i in range(nslot):
            a0 = 1024 * (j + 4 * i) - HALO
            tt = np.arange(a0, a0 + SLOT)
            valid = tt >= 0
            xq[:, SLOT * i:SLOT * (i + 1)][:, valid] = xT[:, tt[valid]]
            posq[SLOT * i:SLOT * (i + 1)] = np.maximum(tt, 0)
            for g, w in enumerate((2, 4, 8, 16)):
                rc[g, SLOT * i:SLOT * (i + 1)] = 1.0 / np.minimum(np.maximum(tt, 0) + 1, w)
        cosq, sinq = _rope_tables(posq)
        vones = np.zeros((128, 3, 128), np.float32)
        vones[:, 3 - j:, :] = 1.0
        m = dict(shared)
        vt = np.zeros((128, G["NKT"]), np.float32)
        vt[:, 8 * (3 - j):] = 1.0
        m.update(xkv=xkv, xq=xq, cosk=cosk, sink=sink, cosq=cosq, sinq=sinq, rcnt=rc, vones=vones, vtile=vt)
        maps.append(m)
    return maps


_NC_CACHE = {}


def run(inputs, nslot, debug=False):
    key = (nslot, debug)
    if key not in _NC_CACHE:
        _NC_CACHE[key] = build_nc(nslot, debug)
    nc = _NC_CACHE[key]
    maps = make_in_maps(inputs, nslot)
    res = run_bass_kernel_spmd(nc, maps, core_ids=list(range(8)))
    T = 4096 * nslot
    out = np.zeros((2, T, D), np.float32)
    for c in range(8):
        b, j = c // 4, c % 4
        oT = res.results[c]["outT"]
        for i in range(nslot):
            a0 = 1024 * (j + 4 * i)
            out[b, a0:a0 + 1024, :] = oT[:, BQ * i:BQ * (i + 1)].T
    return out, res


def kernel(**inputs):
    out, _ = run(inputs, 4)
    return out
```

```python
import math
import numpy as np
import ml_dtypes
from contextlib import ExitStack
import concourse.bass as bass
import concourse.mybir as mybir
from concourse.bass_utils import run_bass_kernel_spmd

F32 = mybir.dt.float32
BF16 = mybir.dt.bfloat16
AF = mybir.ActivationFunctionType
ALU = mybir.AluOpType
NPBF = ml_dtypes.bfloat16

D = 1024
DFF = 2816
NFC = DFF // 128
BQ = 1024
HALO = 32
SLOT = BQ + HALO
SUB = 352
EPS = 1e-6
LAMBDA_INIT = 0.8 - 0.6 * math.exp(0.0)
SAME_ENGINE_SYNC = True


class DSem:
    __slots__ = ("h", "count")

    def __init__(self, h):
        self.h = h
        self.count = 0


class Res:
    __slots__ = ("name", "w", "r", "dsem")

    def __init__(self, name):
        self.name = name
        self.w = None
        self.r = []
        self.dsem = None


class Prog:
    ENGS = ("pe", "act", "dve", "pool", "sp")

    def __init__(self, nc, n_dma=48):
        self.nc = nc
        self.streams = {e: [] for e in self.ENGS}
        self.count = {e: 0 for e in self.ENGS}
        self.waited = {e: {} for e in self.ENGS}
        self.sem = {}
        for e in ("pe", "act", "dve", "pool"):
            self.sem[e] = nc.alloc_semaphore("prog_" + e)
        self.all_dsems = [DSem(nc.alloc_semaphore("dma_%d" % i)) for i in range(n_dma)]
        self.free_dsems = list(self.all_dsems)
        self.all_res = []

    def res(self, name):
        r = Res(name)
        self.all_res.append(r)
        return r

    def _get_dsem(self, res):
        if res.dsem is None:
            assert self.free_dsems, "out of DMA semaphores"
            res.dsem = self.free_dsems.pop()
        return res.dsem

    def _need(self, eng, tok, waits):
        if tok is None:
            return
        if tok[0] == "eng":
            _, e2, idx = tok
            if e2 == eng and (eng == "pe" or not SAME_ENGINE_SYNC):
                return
            key, sem, val = e2, self.sem[e2], idx
        else:
            _, ds, val = tok
            key, sem = id(ds), ds.h
        if self.waited[eng].get(key, 0) >= val:
            return
        self.waited[eng][key] = val
        for i, (s, v) in enumerate(waits):
            if s is sem:
                waits[i] = (s, max(v, val))
                return
        waits.append((sem, val))

    def _deps(self, eng, reads, writes):
        waits = []
        for r in reads:
            self._need(eng, r.w, waits)
        for w in writes:
            self._need(eng, w.w, waits)
            for t in w.r:
                self._need(eng, t, waits)
        return waits

    def _commit(self, tok, reads, writes):
        for r in reads:
            r.r.append(tok)
            if len(r.r) > 24:
                r.r = _prune(r.r)
        for w in writes:
            w.w = tok
            w.r = []

    def op(self, eng, fn, reads=(), writes=()):
        waits = self._deps(eng, reads, writes)
        self.count[eng] += 1
        tok = ("eng", eng, self.count[eng])
        self.streams[eng].append((waits, fn, (self.sem[eng], 1)))
        self._commit(tok, reads, writes)
        return tok

    def dma(self, eng, out_ap, in_ap, reads=(), writes=()):
        waits = self._deps(eng, reads, writes)
        anchor = writes[0] if writes else reads[0]
        ds = self._get_dsem(anchor)
        ds.count += 16
        tok = ("dma", ds, ds.count)
        self.streams[eng].append(
            (waits, (lambda e, o=out_ap, i=in_ap: e.dma_start(out=o, in_=i)), (ds.h, 16)))
        self._commit(tok, reads, writes)
        return tok

    def barrier(self):
        for eng in self.ENGS:
            waits = []
            for e2 in ("pe", "act", "dve", "pool"):
                if e2 != eng and self.count[e2] > 0:
                    self._need(eng, ("eng", e2, self.count[e2]), waits)
            for ds in self.all_dsems:
                if ds.count > 0:
                    self._need(eng, ("dma", ds, ds.count), waits)
            if waits:
                self.streams[eng].append((waits, None, None))
        for r in self.all_res:
            r.w = None
            r.r = []
            r.dsem = None
        self.free_dsems = list(self.all_dsems)

    def emit(self):
        nc = self.nc
        streams = self.streams
        with nc.Block() as block:
            def mk(engname):
                def body(e):
                    for waits, fn, inc in streams[engname]:
                        if fn is None:
                            for (s, v) in waits:
                                e.wait_ge(s, v)
                            continue
                        for (s, v) in waits[:-1]:
                            e.wait_ge(s, v)
                        ins = fn(e)
                        if waits:
                            ins._wait_ge(waits[-1][0], waits[-1][1])
                        ins.then_inc(inc[0], inc[1])
                return body
            if streams["pe"]:
                block.tensor(mk("pe"))
            if streams["act"]:
                block.scalar(mk("act"))
            if streams["dve"]:
                block.vector(mk("dve"))
            if streams["pool"]:
                block.gpsimd(mk("pool"))
            if streams["sp"]:
                block.sync(mk("sp"))
        self.streams = {e: [] for e in self.ENGS}


def _prune(toks):
    best = {}
    for t in toks:
        key = (t[0], t[1] if t[0] == "eng" else id(t[1]))
        if key not in best or best[key][2] < t[2]:
            best[key] = t
    return list(best.values())


class Rot:
    def __init__(self, P, nc, stack, name, shape, dtype, n, psum=False):
        self.items = []
        for i in range(n):
            t = stack.enter_context(nc.sbuf_tensor("rot_%s_%d" % (name, i), shape, dtype))
            self.items.append((t, P.res("%s%d" % (name, i))))
        self.i = 0

    def next(self):
        it = self.items[self.i % len(self.items)]
        self.i += 1
        return it


def geom(nslot):
    T = 4096 * nslot
    return dict(T=T, TL=T, NKT=T // 128, NQ=nslot * SLOT, NB=T // 1024)


def sub_info(i, s):
    LB = 1024 * (3 + 4 * i)
    q0 = LB - HALO + SUB * s
    kt_hi = (q0 + SUB - 1) // 128
    kbase = LB // 128 - 1
    return LB, q0, kt_hi, kbase


def build_masks():
    sbm, sbi, dm, di = [], {}, [], {}
    for s in range(3):
        LB, q0, kt_hi, kbase = sub_info(0, s)
        for kt in range(kbase, kt_hi + 1):
            r = kt - kbase
            kpos = (128 * kt + np.arange(128))[:, None]
            qpos = (q0 + np.arange(SUB))[None, :]
            m = (kpos < qpos)
            if not m.all():
                sbi[(s, r)] = len(sbm)
                sbm.append(m)
            m2 = (kpos // 64) <= (qpos // 64)
            if not m2.all():
                di[(s, r)] = len(dm)
                dm.append(m2)
    sbm = np.stack(sbm, 1).astype(np.float32).astype(NPBF)
    dm = np.stack(dm, 1).astype(np.float32).astype(NPBF)
    return sbm, sbi, dm, di


def build_nc(nslot, debug=False):
    G = geom(nslot)
    TL, NKT, NQ, NB = G["TL"], G["NKT"], G["NQ"], G["NB"]
    sbmask_np, sbidx, dmask_np, didx = build_masks()
    NSBM, NDM = sbmask_np.shape[1], dmask_np.shape[1]

    nc = bass.Bass("TRN2", target_bir_lowering=False)
    P = Prog(nc)

    def din(name, shape, dt=F32):
        return nc.dram_tensor(name, list(shape), dt, kind="ExternalInput").ap()

    def dscr(name, shape, dt=BF16):
        kind = "ExternalOutput" if debug else "Internal"
        return nc.dram_tensor(name, list(shape), dt, kind=kind).ap()

    xkv = din("xkv", [D, TL])
    xq = din("xq", [D, NQ])
    w_in0 = din("w_in_0", [D, 3072]); w_out0 = din("w_out_0", [D, D])
    w_g0 = din("w_gate_0", [D, DFF]); w_u0 = din("w_up_0", [D, DFF]); w_d0 = din("w_down_0", [DFF, D])
    w_in1 = din("w_in_1", [D, 1536]); w_out1 = din("w_out_1", [D, D])
    w_g1 = din("w_gate_1", [D, DFF]); w_u1 = din("w_up_1", [D, DFF]); w_d1 = din("w_down_1", [DFF, D])
    pool_w = din("pool_w_1", [4, 128, 128])
    gains = din("gains", [128, 6, 8])
    vecs = din("vecs", [128, 24])
    lamv = din("lamv", [4, 64])
    dwT = din("dwT", [128, 4, 31])
    cosk = din("cosk", [128, TL]); sink = din("sink", [128, TL])
    cosq = din("cosq", [128, NQ]); sinq = din("sinq", [128, NQ])
    rcnt = din("rcnt", [4, NQ])
    cmat = din("cmat", [128, 5, 128], BF16)
    identf = din("identf", [128, 128])
    sbmask_d = din("sbmask", [128, NSBM, SUB], BF16)
    dmask_d = din("dmask", [128, NDM, SUB], BF16)
    vones_d = din("vones", [128, 3, 128])
    vtile_d = din("vtile", [128, NKT])
    outT = nc.dram_tensor("outT", [D, nslot * BQ], F32, kind="ExternalOutput").ap()

    W0b = dscr("W0b", [8, 128, 3072]); Wo0b = dscr("Wo0b", [8, 128, D])
    Wg0b = dscr("Wg0b", [8, 128, DFF]); Wu0b = dscr("Wu0b", [8, 128, DFF]); Wd0b = dscr("Wd0b", [NFC, 128, D])
    W1b = dscr("W1b", [8, 128, 1536]); Wo1b = dscr("Wo1b", [8, 128, D])
    Wg1b = dscr("Wg1b", [8, 128, DFF]); Wu1b = dscr("Wu1b", [8, 128, DFF]); Wd1b = dscr("Wd1b", [NFC, 128, D])
    Pwb = dscr("Pwb", [4, 128, 128])
    KTs = dscr("KTs", [8, 128, TL]); Vs = dscr("Vs", [8, 128, NKT, 128])
    QTs = dscr("QTs", [8, 128, NQ]); ATs = dscr("ATs", [8, 128, NQ])
    R_W = {k: P.res(k) for k in ("W0b", "Wo0b", "Wg0b", "Wu0b", "Wd0b", "W1b", "Wo1b", "Wg1b", "Wu1b", "Wd1b", "Pwb")}
    R_KT = [P.res("KTs%d" % c) for c in range(8)]
    R_V = [P.res("Vs%d" % c) for c in range(8)]
    R_Q = [P.res("QTs%d" % c) for c in range(8)]
    R_AT = [P.res("ATs%d" % c) for c in range(8)]
    R_out = P.res("outT")

    cm = nc.alloc_sbuf_tensor("cm", [128, 5, 128], BF16); R_cm = P.res("cm")
    gn = nc.alloc_sbuf_tensor("gn", [128, 6, 8], F32); R_gn = P.res("gn")
    vc = nc.alloc_sbuf_tensor("vc", [128, 24], F32); R_vc = P.res("vc")
    idf = nc.alloc_sbuf_tensor("idf", [128, 128], F32); R_idf = P.res("idf")
    psum = nc.alloc_psum_tensor("psum", [128, 8, 512], F32)
    R_ps = [P.res("ps%d" % b) for b in range(8)]
    ONES, TRINEG, ONESNEG, ROT, IDENT = (cm[:, k, :] for k in range(5))

    P.dma("sp", cm[:], cmat, writes=[R_cm])
    P.dma("sp", gn[:], gains, writes=[R_gn])
    P.dma("sp", vc[:], vecs, writes=[R_vc])
    P.dma("sp", idf[:], identf, writes=[R_idf])

    def rstd_from_ps(ps_ap, R_psb, inv_n, lnv, R_lnv, rr, R_rr, n):
        P.op("act", lambda e: e.activation(out=lnv[:, :n], in_=ps_ap, func=AF.Ln, bias=EPS, scale=inv_n),
             reads=[R_psb], writes=[R_lnv])
        P.op("act", lambda e: e.activation(out=rr[:, :n], in_=lnv[:, :n], func=AF.Exp, scale=-0.5),
             reads=[R_lnv], writes=[R_rr])

    def norm_tile(xt, R_xt, n, sq, R_sq, psb, lnv, R_lnv, rr, R_rr, h, R_h, hoff=0, xoff=0):
        P.op("act", lambda e: e.activation(out=sq[:, :, :n], in_=xt[:, :, xoff:xoff + n], func=AF.Square),
             reads=[R_xt], writes=[R_sq])
        for kc in range(8):
            P.op("pe", lambda e, kc=kc: e.matmul(psum[:, psb, :n], lhsT=ONES, rhs=sq[:, kc, :n],
                                                  start=(kc == 0), stop=(kc == 7)),
                 reads=[R_sq, R_cm], writes=[R_ps[psb]])
        rstd_from_ps(psum[:, psb, :n], R_ps[psb], 1.0 / D, lnv, R_lnv, rr, R_rr, n)
        for kc in range(8):
            P.op("dve", lambda e, kc=kc: e.tensor_tensor(out=h[:, kc, hoff:hoff + n], in0=xt[:, kc, xoff:xoff + n],
                                                          in1=rr[:, :n], op=ALU.mult),
                 reads=[R_xt, R_rr], writes=[R_h])

    with ExitStack() as st:
        fin = Rot(P, nc, st, "fin", [128, 3072], F32, 2)
        fout = Rot(P, nc, st, "fout", [128, 3072], BF16, 2)
        jobs = []
        def add_w(src, dst, rdst, nchunk, C, gi):
            for kc in range(nchunk):
                jobs.append((src[kc * 128:(kc + 1) * 128, :], C, dst[kc], (gn[:, gi, kc:kc + 1] if gi is not None else None), rdst))
        add_w(w_in0, W0b, R_W["W0b"], 8, 3072, 0)
        bg_jobs = []
        def add_bg(src, dst, rdst, nchunk, C, gi):
            for kc in range(nchunk):
                for c0 in range(0, C, 1024):
                    cw = min(1024, C - c0)
                    bg_jobs.append((src[kc * 128:(kc + 1) * 128, c0:c0 + cw], cw, dst[kc, :, c0:c0 + cw],
                                    (gn[:, gi, kc:kc + 1] if gi is not None else None), rdst))
        add_bg(w_out0, Wo0b, R_W["Wo0b"], 8, D, None)
        add_bg(w_g0, Wg0b, R_W["Wg0b"], 8, DFF, 1)
        add_bg(w_u0, Wu0b, R_W["Wu0b"], 8, DFF, 1)
        add_bg(w_d0, Wd0b, R_W["Wd0b"], NFC, D, None)
        add_bg(w_in1, W1b, R_W["W1b"], 8, 1536, 2)
        add_bg(w_out1, Wo1b, R_W["Wo1b"], 8, D, None)
        add_bg(w_g1, Wg1b, R_W["Wg1b"], 8, DFF, 3)
        add_bg(w_u1, Wu1b, R_W["Wu1b"], 8, DFF, 3)
        add_bg(w_d1, Wd1b, R_W["Wd1b"], NFC, D, None)
        for g in range(4):
            bg_jobs.append((pool_w[g], 128, Pwb[g], None, R_W["Pwb"]))
        for n, (src, C, dst, gcol, rdst) in enumerate(jobs):
            ti, R_i = fin.next()
            to, R_o = fout.next()
            P.dma("sp", ti[:, :C], src, writes=[R_i])
            if gcol is not None:
                if n % 2 == 0:
                    P.op("dve", lambda e, ti=ti, to=to, C=C, gcol=gcol: e.tensor_scalar(
                        out=to[:, :C], in0=ti[:, :C], scalar1=gcol, scalar2=None, op0=ALU.mult),
                        reads=[R_i, R_gn], writes=[R_o])
                else:
                    P.op("act", lambda e, ti=ti, to=to, C=C, gcol=gcol: e.activation(
                        out=to[:, :C], in_=ti[:, :C], func=AF.Copy, scale=gcol),
                        reads=[R_i, R_gn], writes=[R_o])
            else:
                if n % 2 == 0:
                    P.op("dve", lambda e, ti=ti, to=to, C=C: e.tensor_copy(out=to[:, :C], in_=ti[:, :C]),
                         reads=[R_i], writes=[R_o])
                else:
                    P.op("act", lambda e, ti=ti, to=to, C=C: e.activation(out=to[:, :C], in_=ti[:, :C], func=AF.Copy),
                         reads=[R_i], writes=[R_o])
            P.dma("pool", dst, to[:, :C], reads=[R_o], writes=[rdst])
        P.barrier()
        P.emit()

    with ExitStack() as st:
        w0 = st.enter_context(nc.sbuf_tensor("s_w0", [128, 8, 3072], BF16)); R_w0 = P.res("w0")
        P.dma("sp", w0[:], W0b.rearrange("k p c -> p k c"), reads=[R_W["W0b"]], writes=[R_w0])
        xts = Rot(P, nc, st, "xt", [128, 8, 512], F32, 2)
        sqs = Rot(P, nc, st, "sq", [128, 8, 512], BF16, 1)
        lnvs = Rot(P, nc, st, "lnv", [128, 512], F32, 1)
        rrs = Rot(P, nc, st, "rr", [128, 512], F32, 1)
        hs = Rot(P, nc, st, "h", [128, 8, 512], BF16, 2)
        coss = Rot(P, nc, st, "cos", [128, 512], F32, 2)
        sins = Rot(P, nc, st, "sin", [128, 512], F32, 2)
        kbs = Rot(P, nc, st, "kb", [128, 512], BF16, 2)
        t1s = Rot(P, nc, st, "t1", [128, 512], F32, 2)
        t2s = Rot(P, nc, st, "t2", [128, 512], F32, 2)
        kst = Rot(P, nc, st, "kst", [128, 512], BF16, 3)
        vst = Rot(P, nc, st, "vst", [128, 8, 128], BF16, 2)
        xkv_v = xkv.rearrange("(k p) n -> p k n", p=128)
        xq_v = xq.rearrange("(k p) n -> p k n", p=128)
        kps_i = [0]

        def proj_T(h, R_h, n, chunks, scale, cos_t, R_cos, sin_t, R_sin, dst, R_dst, n0):
            for (c, col, rope) in chunks:
                pb = 1 + (kps_i[0] % 2); kps_i[0] += 1
                for kc in range(8):
                    P.op("pe", lambda e, kc=kc, pb=pb, col=col: e.matmul(
                        psum[:, pb, :n], lhsT=w0[:, kc, col:col + 128], rhs=h[:, kc, :n],
                        start=(kc == 0), stop=(kc == 7)), reads=[R_w0, R_h], writes=[R_ps[pb]])
                ks, R_ks = kst.next()
                if not rope:
                    P.op("act", lambda e, pb=pb, ks=ks: e.activation(out=ks[:, :n], in_=psum[:, pb, :n],
                                                                     func=AF.Copy, scale=scale),
                         reads=[R_ps[pb]], writes=[R_ks])
                else:
                    kb, R_kb = kbs.next()
                    t1, R_t1 = t1s.next()
                    t2, R_t2 = t2s.next()
                    P.op("act", lambda e, pb=pb, kb=kb: e.activation(out=kb[:, :n], in_=psum[:, pb, :n],
                                                                     func=AF.Copy, scale=scale),
                         reads=[R_ps[pb]], writes=[R_kb])
                    P.op("pe", lambda e, kb=kb: e.matmul(psum[:, 3, :n], lhsT=ROT, rhs=kb[:, :n], start=True, stop=True),
                         reads=[R_kb, R_cm], writes=[R_ps[3]])
                    P.op("dve", lambda e, kb=kb, t1=t1: e.tensor_tensor(out=t1[:, :n], in0=kb[:, :n], in1=cos_t[:, :n], op=ALU.mult),
                         reads=[R_kb, R_cos], writes=[R_t1])
                    P.op("dve", lambda e, t2=t2: e.tensor_tensor(out=t2[:, :n], in0=psum[:, 3, :n], in1=sin_t[:, :n], op=ALU.mult),
                         reads=[R_ps[3], R_sin], writes=[R_t2])
                    P.op("dve", lambda e, t1=t1, t2=t2, ks=ks: e.tensor_tensor(out=ks[:, :n], in0=t1[:, :n], in1=t2[:, :n], op=ALU.add),
                         reads=[R_t1, R_t2], writes=[R_ks])
                P.dma("pool", dst[c, :, n0:n0 + n], ks[:, :n], reads=[R_ks], writes=[R_dst[c]])

        KCH = [(c, 512 + 128 * c, False) for c in range(4)] + [(4 + c, 2048 + 128 * c, True) for c in range(4)]
        QCH = [(c, 128 * c, False) for c in range(4)] + [(4 + c, 1536 + 128 * c, True) for c in range(4)]
        vev = [0]
        for t in range(TL // 512):
            n0 = t * 512
            xt, R_xt = xts.next(); sq, R_sq = sqs.next(); lnv, R_lnv = lnvs.next(); rr, R_rr = rrs.next()
            h, R_h = hs.next(); ct, R_ct = coss.next(); sn, R_sn = sins.next()
            P.dma("sp", xt[:], xkv_v[:, :, n0:n0 + 512], writes=[R_xt])
            P.dma("sp", ct[:], cosk[:, n0:n0 + 512], writes=[R_ct])
            P.dma("sp", sn[:], sink[:, n0:n0 + 512], writes=[R_sn])
            norm_tile(xt, R_xt, 512, sq, R_sq, 0, lnv, R_lnv, rr, R_rr, h, R_h)
            proj_T(h, R_h, 512, KCH, 1.0, ct, R_ct, sn, R_sn, KTs, R_KT, n0)
            for sub in range(4):
                vs_, R_vs = vst.next()
                for half, col in ((0, 1024), (1, 2560)):
                    pb = 4 + (vev[0] % 2)
                    for kc in range(8):
                        P.op("pe", lambda e, kc=kc, pb=pb, col=col, sub=sub, h=h: e.matmul(
                            psum[:, pb, :512], lhsT=h[:, kc, sub * 128:(sub + 1) * 128], rhs=w0[:, kc, col:col + 512],
                            start=(kc == 0), stop=(kc == 7)), reads=[R_w0, R_h], writes=[R_ps[pb]])
                    dstv = vs_[:, 4 * half:4 * half + 4, :]
                    srcv = psum[:, pb, :].rearrange("p (c d) -> p c d", d=128)
                    if vev[0] % 2 == 0:
                        P.op("dve", lambda e, dstv=dstv, srcv=srcv: e.tensor_copy(out=dstv, in_=srcv),
                             reads=[R_ps[pb]], writes=[R_vs])
                    else:
                        P.op("act", lambda e, dstv=dstv, srcv=srcv: e.activation(out=dstv, in_=srcv, func=AF.Copy),
                             reads=[R_ps[pb]], writes=[R_vs])
                    vev[0] += 1
                kt = 4 * t + sub
                P.dma("pool", Vs[:, :, kt, :].rearrange("c p d -> p c d"), vs_[:], reads=[R_vs], writes=R_V)
        for t in range(NQ // SUB):
            n0 = t * SUB
            xt, R_xt = xts.next(); sq, R_sq = sqs.next(); lnv, R_lnv = lnvs.next(); rr, R_rr = rrs.next()
            h, R_h = hs.next(); ct, R_ct = coss.next(); sn, R_sn = sins.next()
            P.dma("sp", xt[:, :, :SUB], xq_v[:, :, n0:n0 + SUB], writes=[R_xt])
            P.dma("sp", ct[:, :SUB], cosq[:, n0:n0 + SUB], writes=[R_ct])
            P.dma("sp", sn[:, :SUB], sinq[:, n0:n0 + SUB], writes=[R_sn])
            norm_tile(xt, R_xt, SUB, sq, R_sq, 0, lnv, R_lnv, rr, R_rr, h, R_h)
            proj_T(h, R_h, SUB, QCH, 0.125, ct, R_ct, sn, R_sn, QTs, R_Q, n0)
        P.barrier()
        P.emit()

    with ExitStack() as st:
        KT_sb = st.enter_context(nc.sbuf_tensor("s_KT_sb", [128, TL], BF16)); R_KTsb = P.res("KT_sb")
        V_sb = st.enter_context(nc.sbuf_tensor("s_V_sb", [128, NKT, 128], BF16)); R_Vsb = P.res("V_sb")
        Qz = [st.enter_context(nc.sbuf_tensor("s_Qz%d" % hh, [128, NQ], BF16)) for hh in range(2)]; R_Qz = P.res("Qz")
        P.op("pool", lambda e: e.memset(Qz[0][:], 0.0), writes=[R_Qz])
        P.op("pool", lambda e: e.memset(Qz[1][:], 0.0), writes=[R_Qz])
        ATo = Rot(P, nc, st, "ATo", [128, NQ], BF16, 2)
        sbm = st.enter_context(nc.sbuf_tensor("s_sbm", [128, NSBM, SUB], BF16)); R_sbm = P.res("sbm")
        dmk = st.enter_context(nc.sbuf_tensor("s_dmk", [128, NDM, SUB], BF16)); R_dmk = P.res("dmk")
        vonesf = st.enter_context(nc.sbuf_tensor("s_vonesf", [128, 3, 128], F32)); R_vonesf = P.res("vonesf")
        lamt = st.enter_context(nc.sbuf_tensor("s_lamt", [128, 4, 64], F32)); R_lamt = P.res("lamt")
        lamp = st.enter_context(nc.sbuf_tensor("s_lamp", [128, 2, 64], F32)); R_lamp = P.res("lamp")
        lams = st.enter_context(nc.sbuf_tensor("s_lams", [128, 4], F32)); R_lams = P.res("lams")
        fs = Rot(P, nc, st, "f", [128, SUB], F32, 6)
        dsqs = Rot(P, nc, st, "dsq", [128, SUB], BF16, 1)
        vtile = st.enter_context(nc.sbuf_tensor("s_vtile", [128, NKT], F32)); R_vtile = P.res("vtile")
        onesf = st.enter_context(nc.sbuf_tensor("s_onesf", [128, 128], F32)); R_onesf = P.res("onesf")
        P.dma("sp", vtile[:], vtile_d, writes=[R_vtile])
        P.op("dve", lambda e: e.memset(onesf[:], 1.0), writes=[R_onesf])

        P.dma("sp", sbm[:], sbmask_d, writes=[R_sbm])
        P.dma("sp", dmk[:], dmask_d, writes=[R_dmk])
        P.dma("sp", vonesf[:], vones_d, writes=[R_vonesf])
        P.dma("sp", lamt[:].rearrange("p a d -> p (a d)"), lamv.rearrange("a d -> (a d)").rearrange("(o n) -> o n", o=1).to_broadcast([128, 256]),
              writes=[R_lamt])
        P.op("dve", lambda e: e.tensor_tensor(out=lamp[:, 0, :], in0=lamt[:, 0, :], in1=lamt[:, 1, :], op=ALU.mult),
             reads=[R_lamt], writes=[R_lamp])
        P.op("dve", lambda e: e.tensor_tensor(out=lamp[:, 1, :], in0=lamt[:, 2, :], in1=lamt[:, 3, :], op=ALU.mult),
             reads=[R_lamt], writes=[R_lamp])
        P.op("dve", lambda e: e.reduce_sum(out=lams[:, 0:2], in_=lamp[:], axis=mybir.AxisListType.X),
             reads=[R_lamp], writes=[R_lams])
        P.op("act", lambda e: e.activation(out=lams[:, 0:2], in_=lams[:, 0:2], func=AF.Exp), reads=[R_lams], writes=[R_lams])
        P.op("dve", lambda e: e.tensor_tensor(out=lams[:, 2:3], in0=lams[:, 1:2], in1=lams[:, 0:1], op=ALU.subtract),
             reads=[R_lams], writes=[R_lams])
        P.op("dve", lambda e: e.tensor_scalar(out=lams[:, 2:3], in0=lams[:, 2:3], scalar1=-LAMBDA_INIT, scalar2=None, op0=ALU.add),
             reads=[R_lams], writes=[R_lams])
        P.op("dve", lambda e: e.tensor_scalar(out=lams[:, 3:4], in0=vc[:, 0:1], scalar1=(1.0 - LAMBDA_INIT), scalar2=None, op0=ALU.mult),
             reads=[R_vc, R_lams], writes=[R_lams])
        NLAM = lams[:, 2:3]
        GSUB = lams[:, 3:4]

        bfin = Rot(P, nc, st, "bfin", [128, 1024], F32, 3)
        bfout = Rot(P, nc, st, "bfout", [128, 1024], BF16, 3)
        bg_state = dict(next=0, inflight=[])

        def bg_tick(k):
            if k % 7 != 0:
                return
            infl = bg_state["inflight"]
            if len(infl) >= 2 and infl[0]["stage"] == 2:
                j = infl.pop(0)
                P.dma("sp", j["dst"], j["to"][:, :j["C"]], reads=[j["R_o"]], writes=[j["rdst"]])
            for j in infl:
                if j["stage"] == 1:
                    ti, to, C, gcol = j["ti"], j["to"], j["C"], j["gcol"]
                    if gcol is not None:
                        P.op("dve", lambda e, ti=ti, to=to, C=C, gcol=gcol: e.tensor_scalar(
                            out=to[:, :C], in0=ti[:, :C], scalar1=gcol, scalar2=None, op0=ALU.mult),
                            reads=[j["R_i"], R_gn], writes=[j["R_o"]])
                    else:
                        P.op("dve", lambda e, ti=ti, to=to, C=C: e.tensor_copy(out=to[:, :C], in_=ti[:, :C]),
                             reads=[j["R_i"]], writes=[j["R_o"]])
                    j["stage"] = 2
                    break
            if bg_state["next"] < len(bg_jobs) and len(infl) < 3:
                src, C, dst, gcol, rdst = bg_jobs[bg_state["next"]]
                bg_state["next"] += 1
                ti, R_i = bfin.next(); to, R_o = bfout.next()
                P.dma("sp", ti[:, :C], src, writes=[R_i])
                infl.append(dict(stage=1, ti=ti, R_i=R_i, to=to, R_o=R_o, C=C, dst=dst, gcol=gcol, rdst=rdst))

        def bg_flush():
            k = 0
            while bg_state["next"] < len(bg_jobs) or bg_state["inflight"]:
                infl = bg_state["inflight"]
                if infl and infl[0]["stage"] == 2 and (len(infl) < 2 or bg_state["next"] >= len(bg_jobs)):
                    j = infl.pop(0)
                    P.dma("sp", j["dst"], j["to"][:, :j["C"]], reads=[j["R_o"]], writes=[j["rdst"]])
                    continue
                bg_tick(0)
                k += 1
                assert k < 100000

        def load_chunk(c):
            P.dma("sp", KT_sb[:], KTs[c], reads=[R_KT[c]], writes=[R_KTsb])
            P.dma("sp", V_sb[:], Vs[c], reads=[R_V[c]], writes=[R_Vsb])
            P.dma("sp", Qz[0][0:64, :], QTs[c, 0:64, :], reads=[R_Q[c]], writes=[R_Qz])
            P.dma("sp", Qz[1][64:128, :], QTs[c, 64:128, :], reads=[R_Q[c]], writes=[R_Qz])

        Srot = [Rot(P, nc, st, "Sp%d" % hh, [128, SUB], BF16, 2) for hh in range(2)]
        e2s = Rot(P, nc, st, "e2", [128, 2, SUB], F32, 2)
        sp2s = Rot(P, nc, st, "sp2", [128, 2, SUB], BF16, 3)
        wf2s = Rot(P, nc, st, "wf2", [128, 2, SUB], F32, 2)
        w2s = Rot(P, nc, st, "w2", [128, 2, SUB], BF16, 3)
        for c in range(4):
            load_chunk(c)
            at, R_at = ATo.next()
            steps = []
            for i in range(nslot):
                for s in range(3):
                    LB, q0, kt_hi, kbase = sub_info(i, s)
                    qc0 = SLOT * i + SUB * s
                    kts = list(range(kt_hi, -1, -1))
                    for n, kt in enumerate(kts):
                        steps.append(dict(qc0=qc0, kt=kt, first=(n == 0), last=(n == len(kts) - 1),
                                          mi=sbidx.get((s, kt - kbase)) if kt >= kbase else None))
            NS = len(steps)
            stt = [dict() for _ in range(NS)]
            Scur = [None, None]

            def sZ(n):
                sd = steps[n]; b = n % 3; kt, qc0 = sd["kt"], sd["qc0"]
                for hh in range(2):
                    P.op("pe", lambda e, hh=hh: e.matmul(psum[:, 2 * b + hh, :SUB], lhsT=KT_sb[:, kt * 128:(kt + 1) * 128],
                                                         rhs=Qz[hh][:, qc0:qc0 + SUB], start=True, stop=False),
                         reads=[R_KTsb, R_Qz], writes=[R_ps[2 * b + hh]])

            def sE(n):
                b = n % 3
                e_, R_e = e2s.next()
                P.op("act", lambda e: e.activation(out=e_[:], in_=psum[:, 2 * b:2 * b + 2, :SUB], func=AF.Exp),
                     reads=[R_ps[2 * b], R_ps[2 * b + 1]], writes=[R_e])
                stt[n]["e"] = (e_, R_e)

            def sSP(n):
                sd = steps[n]
                e_, R_e = stt[n]["e"]
                sp_, R_sp = sp2s.next()
                P.op("act", lambda e: e.activation(out=sp_[:], in_=e_[:], func=AF.Ln, bias=1.0, scale=1.0),
                     reads=[R_e], writes=[R_sp])
                if sd["mi"] is not None:
                    mi = sd["mi"]
                    for hh in range(2):
                        P.op("dve", lambda e, hh=hh: e.tensor_tensor(out=sp_[:, hh, :], in0=sp_[:, hh, :], in1=sbm[:, mi, :], op=ALU.mult),
                             reads=[R_sp, R_sbm], writes=[R_sp])
                stt[n]["sp"] = (sp_, R_sp)

            def sLW(n):
                sd = steps[n]; b = n % 3
                sp_, R_sp = stt[n]["sp"]
                for hh in range(2):
                    lb = 2 * b + hh
                    if sd["first"]:
                        P.op("pe", lambda e, hh=hh, lb=lb: e.matmul(psum[:, lb, :SUB], lhsT=TRINEG, rhs=sp_[:, hh, :], start=False, stop=True),
                             reads=[R_sp, R_cm], writes=[R_ps[lb]])
                    else:
                        S_, R_S = Scur[hh]
                        P.op("pe", lambda e, hh=hh, lb=lb: e.matmul(psum[:, lb, :SUB], lhsT=TRINEG, rhs=sp_[:, hh, :], start=False, stop=False),
                             reads=[R_sp, R_cm], writes=[R_ps[lb]])
                        P.op("pe", lambda e, hh=hh, lb=lb, S_=S_: e.matmul(psum[:, lb, :SUB], lhsT=ONESNEG, rhs=S_[:], start=False, stop=True),
                             reads=[R_S, R_cm], writes=[R_ps[lb]])
                if not sd["last"]:
                    for hh in range(2):
                        eng = "pool" if hh == 0 else "dve"
                        Sn, R_Sn = Srot[hh].next()
                        if sd["first"]:
                            P.op(eng, lambda e, hh=hh, Sn=Sn: e.tensor_copy(out=Sn[:], in_=sp_[:, hh, :]), reads=[R_sp], writes=[R_Sn])
                        else:
                            S_, R_S = Scur[hh]
                            P.op(eng, lambda e, hh=hh, Sn=Sn, S_=S_: e.tensor_tensor(out=Sn[:], in0=S_[:], in1=sp_[:, hh, :], op=ALU.add),
                                 reads=[R_S, R_sp], writes=[R_Sn])
                        Scur[hh] = (Sn, R_Sn)

            def sW(n):
                sd = steps[n]; b = n % 3
                wf_, R_wf = wf2s.next()
                w_, R_w = w2s.next()
                P.op("act", lambda e: e.activation(out=wf_[:], in_=psum[:, 2 * b:2 * b + 2, :SUB], func=AF.Exp),
                     reads=[R_ps[2 * b], R_ps[2 * b + 1]], writes=[R_wf])
                if sd["mi"] is not None:
                    mi = sd["mi"]
                    for hh in range(2):
                        P.op("dve", lambda e, hh=hh: e.tensor_tensor(out=w_[:, hh, :], in0=wf_[:, hh, :], in1=sbm[:, mi, :], op=ALU.mult),
                             reads=[R_wf, R_sbm], writes=[R_w])
                else:
                    P.op("dve", lambda e: e.tensor_copy(out=w_[:], in_=wf_[:]), reads=[R_wf], writes=[R_w])
                stt[n]["w"] = (w_, R_w)

            def sPV(n):
                sd = steps[n]; kt = sd["kt"]; qc0 = sd["qc0"]
                w_, R_w = stt[n]["w"]
                at_, R_at_ = at, R_at
                for hh in range(2):
                    ob = 6 + hh
                    P.op("pe", lambda e, hh=hh, ob=ob: e.matmul(psum[:, ob, :SUB], lhsT=V_sb[:, kt, :], rhs=w_[:, hh, :],
                                                                start=sd["first"], stop=sd["last"]),
                         reads=[R_Vsb, R_w], writes=[R_ps[ob]])
                    if sd["last"]:
                        pr = slice(64 * hh, 64 * hh + 64)
                        P.op("dve", lambda e, pr=pr, ob=ob: e.tensor_copy(out=at_[pr, qc0:qc0 + SUB], in_=psum[pr, ob, :SUB]),
                             reads=[R_ps[ob]], writes=[R_at_])
                stt[n].clear()

            for t in range(NS + 4):
                bg_tick(t)
                if 0 <= t - 1 < NS: sE(t - 1)
                if 0 <= t - 2 < NS: sLW(t - 2)
                if 0 <= t - 3 < NS: sW(t - 3)
                if 0 <= t - 4 < NS: sPV(t - 4)
                if 0 <= t - 1 < NS: sSP(t - 1)
                if t < NS: sZ(t)
            P.dma("pool", ATs[c], at[:], reads=[R_at], writes=[R_AT[c]])

        bg_flush()
        p2s = Rot(P, nc, st, "p2", [128, 2, SUB], BF16, 6)
        acc_rot = Rot(P, nc, st, "acc2", [128, 2, SUB], F32, 4)
        psum2s = Rot(P, nc, st, "psm2", [128, 2, SUB], BF16, 2)
        for hd in range(4):
            c = 4 + hd
            load_chunk(c)
            at, R_at = ATo.next()
            for i in range(nslot):
                for s in range(3):
                    LB, q0, kt_hi, kbase = sub_info(i, s)
                    qc0 = SLOT * i + SUB * s
                    steps = []
                    for kt in range(0, kt_hi + 1):
                        steps.append(dict(kt=kt, first=(kt == 0), last=(kt == kt_hi),
                                          mi=didx.get((s, kt - kbase)) if kt >= kbase else None))
                    NS = len(steps)
                    pst = [None] * NS
                    accs = {k: acc_rot.next() for k in ("mD", "b0", "b1", "b2")}
                    accstate = {}
                    held = {}

                    def dZ(n):
                        sd = steps[n]; kt = sd["kt"]; b = n % 2; qq = qc0
                        for m in range(2):
                            P.op("pe", lambda e, m=m: e.matmul(psum[:, 2 * b + m, :SUB], lhsT=KT_sb[:, kt * 128:(kt + 1) * 128],
                                                               rhs=Qz[m][:, qq:qq + SUB], start=True, stop=True),
                                 reads=[R_KTsb, R_Qz], writes=[R_ps[2 * b + m]])

                    def dP(n):
                        sd = steps[n]; b = n % 2
                        p_, R_p = p2s.next()
                        P.op("act", lambda e: e.activation(out=p_[:], in_=psum[:, 2 * b:2 * b + 2, :SUB], func=AF.Exp),
                             reads=[R_ps[2 * b], R_ps[2 * b + 1]], writes=[R_p])
                        if sd["mi"] is not None:
                            mi = sd["mi"]
                            for m in range(2):
                                P.op("dve", lambda e, m=m: e.tensor_tensor(out=p_[:, m, :], in0=p_[:, m, :], in1=dmk[:, mi, :], op=ALU.mult),
                                     reads=[R_p, R_dmk], writes=[R_p])
                        pst[n] = (p_, R_p)

                    def dC(n):
                        sd = steps[n]; kt = sd["kt"]
                        p_, R_p = pst[n]
                        for m in range(2):
                            P.op("pe", lambda e, m=m: e.matmul(psum[:, 4 + m, :SUB], lhsT=V_sb[:, kt, :], rhs=p_[:, m, :],
                                                               start=sd["first"], stop=sd["last"]),
                                 reads=[R_Vsb, R_p], writes=[R_ps[4 + m]])
                        ak = ("b%d" % (kt // 8)) if kt < 24 else "mD"
                        pa_, R_pa = accs[ak]
                        src_, R_src = None, None
                        if kt % 2 == 0 and not sd["last"]:
                            held["p"] = (p_, R_p)
                        elif kt % 2 == 1:
                            hp_, R_hp = held.pop("p")
                            t_, R_t = psum2s.next()
                            P.op("dve", lambda e: e.tensor_tensor(out=t_[:], in0=hp_[:], in1=p_[:], op=ALU.add),
                                 reads=[R_hp, R_p], writes=[R_t])
                            src_, R_src = t_, R_t
                        else:
                            src_, R_src = p_, R_p
                        if src_ is not None:
                            if ak not in accstate:
                                accstate[ak] = True
                                P.op("dve", lambda e: e.tensor_copy(out=pa_[:], in_=src_[:]), reads=[R_src], writes=[R_pa])
                            else:
                                P.op("dve", lambda e: e.tensor_tensor(out=pa_[:], in0=pa_[:], in1=src_[:], op=ALU.add),
                                     reads=[R_src, R_pa], writes=[R_pa])
                        if sd["last"]:
                            order = ["mD", "b0", "b1", "b2"]
                            for m in range(2):
                                for qi, ak2 in enumerate(order):
                                    pa2, R_pa2 = accs[ak2]
                                    lh = onesf[:] if ak2[0] == "m" else vonesf[:, int(ak2[1]), :]
                                    P.op("pe", lambda e, m=m, pa2=pa2, lh=lh, qi=qi: e.matmul(
                                        psum[:, 6 + m, :SUB], lhsT=lh, rhs=pa2[:, m, :], start=(qi == 0), stop=(qi == 3)),
                                        reads=[R_pa2, R_onesf, R_vonesf], writes=[R_ps[6 + m]])
                        pst[n] = None

                    for t in range(NS + 2):
                        if 0 <= t - 1 < NS: dP(t - 1)
                        if 0 <= t - 2 < NS: dC(t - 2)
                        if t < NS: dZ(t)
                    on = []
                    for m in range(2):
                        r_, R_r = fs.next()
                        P.op("dve", lambda e, r_=r_, m=m: e.tensor_scalar(out=r_[:], in0=psum[:, 6 + m, :SUB], scalar1=1e-30, scalar2=None, op0=ALU.max),
                             reads=[R_ps[6 + m]], writes=[R_r])
                        P.op("dve", lambda e, r_=r_: e.reciprocal(out=r_[:], in_=r_[:]), reads=[R_r], writes=[R_r])
                        o_, R_o = fs.next()
                        P.op("dve", lambda e, r_=r_, o_=o_, m=m: e.tensor_tensor(out=o_[:], in0=psum[:, 4 + m, :SUB], in1=r_[:], op=ALU.mult),
                             reads=[R_ps[4 + m], R_r], writes=[R_o])
                        on.append((o_, R_o))
                    d_, R_d = fs.next()
                    P.op("dve", lambda e, d_=d_, on=on: e.scalar_tensor_tensor(out=d_[:], in0=on[1][0][:], scalar=NLAM, in1=on[0][0][:],
                                                                                 op0=ALU.mult, op1=ALU.add),
                         reads=[on[0][1], on[1][1], R_lams], writes=[R_d])
                    dq_, R_dq = dsqs.next()
                    P.op("act", lambda e, d_=d_, dq_=dq_: e.activation(out=dq_[:], in_=d_[:], func=AF.Square), reads=[R_d], writes=[R_dq])
                    P.op("pe", lambda e, dq_=dq_: e.matmul(psum[:, 0, :SUB], lhsT=ONES, rhs=dq_[:], start=True, stop=True),
                         reads=[R_dq, R_cm], writes=[R_ps[0]])
                    ln_, R_ln = fs.next()
                    P.op("act", lambda e, ln_=ln_: e.activation(out=ln_[:], in_=psum[:, 0, :SUB], func=AF.Ln, bias=EPS, scale=1.0 / 128),
                         reads=[R_ps[0]], writes=[R_ln])
                    P.op("act", lambda e, ln_=ln_: e.activation(out=ln_[:], in_=ln_[:], func=AF.Exp, scale=-0.5),
                         reads=[R_ln], writes=[R_ln])
                    P.op("dve", lambda e, d_=d_, ln_=ln_, qc0=qc0, at=at: e.scalar_tensor_tensor(
                        out=at[:, qc0:qc0 + SUB], in0=d_[:], scalar=GSUB, in1=ln_[:], op0=ALU.mult, op1=ALU.mult),
                        reads=[R_d, R_ln, R_lams], writes=[R_at])
            P.dma("pool", ATs[c], at[:], reads=[R_at], writes=[R_AT[c]])
        P.barrier()
        P.emit()

    with ExitStack() as st:
        xres = st.enter_context(nc.sbuf_tensor("s_xres", [128, 8, SLOT], F32)); R_x = P.res("xres")
        hb = st.enter_context(nc.sbuf_tensor("s_hb", [128, 8, SLOT], BF16)); R_hb = P.res("hb")
        big = st.enter_context(nc.sbuf_tensor("s_big", [128, NFC, SLOT], BF16)); R_big = P.res("big")
        wbufs = Rot(P, nc, st, "wb", [128, NFC * 128], BF16, 4)
        sqs = Rot(P, nc, st, "sq3", [128, 8, 512], BF16, 1)
        lnvs = Rot(P, nc, st, "lnv3", [128, 512], F32, 1)
        rrs = Rot(P, nc, st, "rr3", [128, 512], F32, 1)
        sgs = Rot(P, nc, st, "sg", [128, 512], F32, 2)
        xps = Rot(P, nc, st, "xp", [128, SLOT], F32, 1)
        pa = Rot(P, nc, st, "pa", [128, SLOT], F32, 2)
        rcs = Rot(P, nc, st, "rc", [128, BQ], F32, 1)
        plb = Rot(P, nc, st, "plb", [128, BQ], BF16, 1)
        diags = Rot(P, nc, st, "diag", [128, 31, 128], BF16, 2)
        dwt = st.enter_context(nc.sbuf_tensor("s_dwt", [128, 4, 31], F32)); R_dwt = P.res("dwt")
        ucf = st.enter_context(nc.sbuf_tensor("s_ucf", [128, 4, 512], F32)); R_ucf = P.res("ucf")
        ucbq = st.enter_context(nc.sbuf_tensor("s_ucbq", [128, 8, 512], BF16)); R_ucb = P.res("ucb"); R_ucq = P.res("ucq")
        ucb = ucbq[:, 0:4, :]; ucq = ucbq[:, 4:8, :]
        mts = Rot(P, nc, st, "mt", [128, 512], F32, 3)
        ots = Rot(P, nc, st, "ot", [128, 512], F32, 2)
        P.dma("sp", dwt[:], dwT, writes=[R_dwt])
        xq_v = xq.rearrange("(k p) n -> p k n", p=128)
        ATv = big[:, 0:8, :]
        psi = [0]

        def next_pb(lo=0, n=4):
            pb = lo + psi[0] % n
            psi[0] += 1
            return pb

        def linear(Wd, R_Wd, KC, col0, ncols, group, rhs, R_rhs, subs, evac):
            for g0 in range(0, ncols, group):
                gw = min(group, ncols - g0)
                wb, R_wb = wbufs.next()
                wv = wb[:, :KC * gw].rearrange("p (k c) -> p k c", c=gw)
                P.dma("sp", wv, Wd[:, :, col0 + g0:col0 + g0 + gw].rearrange("k p c -> p k c"), reads=[R_Wd], writes=[R_wb])
                for ol in range(gw // 128):
                    oc = (g0 // 128) + ol
                    for si, (n0, n) in enumerate(subs):
                        pb = next_pb()
                        for kc in range(KC):
                            P.op("pe", lambda e, kc=kc, pb=pb, ol=ol, n0=n0, n=n, wv=wv: e.matmul(
                                psum[:, pb, :n], lhsT=wv[:, kc, ol * 128:(ol + 1) * 128], rhs=rhs(kc, n0, n),
                                start=(kc == 0), stop=(kc == KC - 1)), reads=[R_wb] + R_rhs, writes=[R_ps[pb]])
                        evac(oc, si, n0, n, pb)

        def norm_slot(subs):
            for (n0, n) in subs:
                sq, R_sq = sqs.next(); lnv, R_lnv = lnvs.next(); rr, R_rr = rrs.next()
                norm_tile(xres, R_x, n, sq, R_sq, 7, lnv, R_lnv, rr, R_rr, hb, R_hb, hoff=n0, xoff=n0)

        def ffn(Wg, Wu, Wdn, kg, ku, kd, subs):
            norm_slot(subs)
            for f0 in range(0, DFF, 256):
                gw = min(256, DFF - f0)
                wg, R_wg = wbufs.next(); wu, R_wu = wbufs.next()
                wgv = wg[:, :8 * gw].rearrange("p (k c) -> p k c", c=gw)
                wuv = wu[:, :8 * gw].rearrange("p (k c) -> p k c", c=gw)
                P.dma("sp", wgv, Wg[:, :, f0:f0 + gw].rearrange("k p c -> p k c"), reads=[R_W[kg]], writes=[R_wg])
                P.dma("sp", wuv, Wu[:, :, f0:f0 + gw].rearrange("k p c -> p k c"), reads=[R_W[ku]], writes=[R_wu])
                for ol in range(gw // 128):
                    fc = f0 // 128 + ol
                    for (n0, n) in subs:
                        pg = next_pb(0, 6); pu = next_pb(0, 6)
                        for kc in range(8):
                            P.op("pe", lambda e, kc=kc, pg=pg, ol=ol, n0=n0, n=n, wgv=wgv: e.matmul(
                                psum[:, pg, :n], lhsT=wgv[:, kc, ol * 128:(ol + 1) * 128], rhs=hb[:, kc, n0:n0 + n],
                                start=(kc == 0), stop=(kc == 7)), reads=[R_wg, R_hb], writes=[R_ps[pg]])
                        for kc in range(8):
                            P.op("pe", lambda e, kc=kc, pu=pu, ol=ol, n0=n0, n=n, wuv=wuv: e.matmul(
                                psum[:, pu, :n], lhsT=wuv[:, kc, ol * 128:(ol + 1) * 128], rhs=hb[:, kc, n0:n0 + n],
                                start=(kc == 0), stop=(kc == 7)), reads=[R_wu, R_hb], writes=[R_ps[pu]])
                        sg, R_sg = sgs.next()
                        P.op("act", lambda e, sg=sg, pg=pg, n=n: e.activation(out=sg[:, :n], in_=psum[:, pg, :n], func=AF.Silu),
                             reads=[R_ps[pg]], writes=[R_sg])
                        P.op("dve", lambda e, sg=sg, pu=pu, n=n, n0=n0, fc=fc: e.tensor_tensor(
                            out=big[:, fc, n0:n0 + n], in0=sg[:, :n], in1=psum[:, pu, :n], op=ALU.mult),
                            reads=[R_sg, R_ps[pu]], writes=[R_big])
            def ev(oc, si, n0, n, pb):
                P.op("dve", lambda e: e.tensor_tensor(out=xres[:, oc, n0:n0 + n], in0=xres[:, oc, n0:n0 + n],
                                                      in1=psum[:, pb, :n], op=ALU.add),
                     reads=[R_x, R_ps[pb]], writes=[R_x])
            linear(Wdn, R_W[kd], NFC, 0, D, 128, lambda kc, n0, n: big[:, kc, n0:n0 + n], [R_big], subs, ev)

        SUBS3 = [(0, SUB), (SUB, SUB), (2 * SUB, SUB)]
        SUBS2 = [(HALO, 512), (HALO + 512, 512)]
        for i in range(nslot):
            c0 = SLOT * i
            P.dma("sp", xres[:], xq_v[:, :, c0:c0 + SLOT], writes=[R_x])
            P.dma("sp", ATv, ATs[:, :, c0:c0 + SLOT].rearrange("c p n -> p c n"), reads=R_AT, writes=[R_big])
            def ev0(oc, si, n0, n, pb):
                P.op("dve", lambda e: e.tensor_tensor(out=xres[:, oc, n0:n0 + n], in0=xres[:, oc, n0:n0 + n],
                                                      in1=psum[:, pb, :n], op=ALU.add),
                     reads=[R_x, R_ps[pb]], writes=[R_x])
            linear(Wo0b, R_W["Wo0b"], 8, 0, D, 256, lambda kc, n0, n: big[:, kc, n0:n0 + n], [R_big], SUBS3, ev0)
            ffn(Wg0b, Wu0b, Wd0b, "Wg0b", "Wu0b", "Wd0b", SUBS3)
            norm_slot(SUBS3)
            for g in range(4):
                wb, R_wb = wbufs.next()
                wv = wb[:, :8 * 128].rearrange("p (k c) -> p k c", c=128)
                P.dma("sp", wv, W1b[:, :, 128 * g:128 * g + 128].rearrange("k p c -> p k c"), reads=[R_W["W1b"]], writes=[R_wb])
                pw, R_pw = wbufs.next()
                P.dma("sp", pw[:, :128], Pwb[g], reads=[R_W["Pwb"]], writes=[R_pw])
                xp, R_xp = xps.next()
                for (n0, n) in SUBS3:
                    pb = next_pb()
                    for kc in range(8):
                        P.op("pe", lambda e, kc=kc, pb=pb, n0=n0, n=n, wv=wv: e.matmul(
                            psum[:, pb, :n], lhsT=wv[:, kc, :], rhs=hb[:, kc, n0:n0 + n], start=(kc == 0), stop=(kc == 7)),
                            reads=[R_wb, R_hb], writes=[R_ps[pb]])
                    P.op("act", lambda e, pb=pb, n0=n0, n=n, xp=xp: e.activation(out=xp[:, n0:n0 + n], in_=psum[:, pb, :n], func=AF.Copy),
                         reads=[R_ps[pb]], writes=[R_xp])
                cur, R_cur = xp, R_xp
                lo, wdt = 0, 1
                for k in range(g + 1):
                    nx, R_nx = pa.next()
                    lo2 = lo + wdt
                    P.op("dve", lambda e, cur=cur, nx=nx, lo2=lo2, wdt=wdt: e.tensor_tensor(
                        out=nx[:, lo2:SLOT], in0=cur[:, lo2:SLOT], in1=cur[:, lo2 - wdt:SLOT - wdt], op=ALU.add),
                        reads=[R_cur], writes=[R_nx])
                    cur, R_cur, lo, wdt = nx, R_nx, lo2, wdt * 2
                rc, R_rc = rcs.next()
                P.dma("sp", rc[:], rcnt[g:g + 1, c0 + HALO:c0 + SLOT].to_broadcast([128, BQ]), writes=[R_rc])
                P.op("dve", lambda e, cur=cur, rc=rc: e.tensor_tensor(out=cur[:, HALO:SLOT], in0=cur[:, HALO:SLOT], in1=rc[:], op=ALU.mult),
                     reads=[R_cur, R_rc], writes=[R_cur])
                pl, R_pl = plb.next()
                P.op("dve", lambda e, cur=cur, xp=xp, pl=pl: e.tensor_tensor(out=pl[:], in0=cur[:, HALO:SLOT], in1=xp[:, HALO:SLOT], op=ALU.subtract),
                     reads=[R_cur, R_xp], writes=[R_pl])
                for hi in range(2):
                    pb = next_pb()
                    P.op("pe", lambda e, pb=pb, hi=hi, pw=pw, pl=pl: e.matmul(psum[:, pb, :512], lhsT=pw[:, :128], rhs=pl[:, hi * 512:(hi + 1) * 512],
                                                                               start=True, stop=True),
                         reads=[R_pw, R_pl], writes=[R_ps[pb]])
                    P.op("act", lambda e, pb=pb, hi=hi, g=g: e.activation(out=big[:, 8 + g, hi * 512:(hi + 1) * 512], in_=psum[:, pb, :512],
                                                                            func=AF.Copy, scale=vc[:, 1 + g:2 + g]),
                         reads=[R_ps[pb], R_vc], writes=[R_big])
            for cc in range(4):
                wa, R_wa = wbufs.next(); wg_, R_wg_ = wbufs.next()
                wav = wa[:, :8 * 128].rearrange("p (k c) -> p k c", c=128)
                wgv = wg_[:, :8 * 128].rearrange("p (k c) -> p k c", c=128)
                P.dma("sp", wav, W1b[:, :, 512 + 128 * cc:640 + 128 * cc].rearrange("k p c -> p k c"), reads=[R_W["W1b"]], writes=[R_wa])
                P.dma("sp", wgv, W1b[:, :, 1024 + 128 * cc:1152 + 128 * cc].rearrange("k p c -> p k c"), reads=[R_W["W1b"]], writes=[R_wg_])
                for (n0, n) in SUBS3:
                    pa_ = next_pb(0, 6); pg_ = next_pb(0, 6)
                    for kc in range(8):
                        P.op("pe", lambda e, kc=kc, pa_=pa_, n0=n0, n=n, wav=wav: e.matmul(
                            psum[:, pa_, :n], lhsT=wav[:, kc, :], rhs=hb[:, kc, n0:n0 + n], start=(kc == 0), stop=(kc == 7)),
                            reads=[R_wa, R_hb], writes=[R_ps[pa_]])
                    for kc in range(8):
                        P.op("pe", lambda e, kc=kc, pg_=pg_, n0=n0, n=n, wgv=wgv: e.matmul(
                            psum[:, pg_, :n], lhsT=wgv[:, kc, :], rhs=hb[:, kc, n0:n0 + n], start=(kc == 0), stop=(kc == 7)),
                            reads=[R_wg_, R_hb], writes=[R_ps[pg_]])
                    sg, R_sg = sgs.next()
                    P.op("act", lambda e, sg=sg, pg_=pg_, n=n: e.activation(out=sg[:, :n], in_=psum[:, pg_, :n], func=AF.Sigmoid),
                         reads=[R_ps[pg_]], writes=[R_sg])
                    P.op("dve", lambda e, sg=sg, pa_=pa_, n=n, n0=n0, cc=cc: e.tensor_tensor(
                        out=big[:, cc, n0:n0 + n], in0=sg[:, :n], in1=psum[:, pa_, :n], op=ALU.mult),
                        reads=[R_sg, R_ps[pa_]], writes=[R_big])
            for hi in range(2):
                t0 = hi * 512
                for cc in range(4):
                    diag, R_diag = diags.next()
                    for j in range(31):
                        P.op("dve", lambda e, cc=cc, j=j, diag=diag: e.tensor_scalar(out=diag[:, j, :], in0=idf[:], scalar1=dwt[:, cc, j:j + 1],
                                                                          scalar2=None, op0=ALU.mult),
                             reads=[R_idf, R_dwt], writes=[R_diag])
                    pb = next_pb()
                    for j in range(31):
                        P.op("pe", lambda e, cc=cc, j=j, pb=pb, t0=t0, diag=diag: e.matmul(
                            psum[:, pb, :512], lhsT=diag[:, j, :], rhs=big[:, cc, t0 + 2 + j:t0 + 2 + j + 512],
                            start=(j == 0), stop=(j == 30)), reads=[R_diag, R_big], writes=[R_ps[pb]])
                    P.op("act", lambda e, cc=cc, pb=pb: e.activation(out=ucf[:, cc, :], in_=psum[:, pb, :512], func=AF.Identity,
                                                                      bias=vc[:, 5 + cc:6 + cc], scale=1.0),
                         reads=[R_ps[pb], R_vc], writes=[R_ucf])
                P.op("dve", lambda e: e.tensor_copy(out=ucb, in_=ucf[:]), reads=[R_ucf], writes=[R_ucb])
                P.op("act", lambda e: e.activation(out=ucq, in_=ucf[:], func=AF.Square), reads=[R_ucf], writes=[R_ucq])
                for cc in range(4):
                    P.op("pe", lambda e, cc=cc: e.matmul(psum[:, 6, :512], lhsT=ONES, rhs=ucb[:, cc, :], start=(cc == 0), stop=(cc == 3)),
                         reads=[R_ucb, R_cm], writes=[R_ps[6]])
                for cc in range(4):
                    P.op("pe", lambda e, cc=cc: e.matmul(psum[:, 7, :512], lhsT=ONES, rhs=ucq[:, cc, :], start=(cc == 0), stop=(cc == 3)),
                         reads=[R_ucq, R_cm], writes=[R_ps[7]])
                mt, R_mt = mts.next(); m2, R_m2 = mts.next(); vt, R_vt = mts.next()
                P.op("dve", lambda e, mt=mt: e.tensor_scalar(out=mt[:], in0=psum[:, 6, :512], scalar1=1.0 / 512, scalar2=None, op0=ALU.mult),
                     reads=[R_ps[6]], writes=[R_mt])
                P.op("dve", lambda e, mt=mt, m2=m2: e.tensor_tensor(out=m2[:], in0=mt[:], in1=mt[:], op=ALU.mult), reads=[R_mt], writes=[R_m2])
                P.op("dve", lambda e, m2=m2, vt=vt: e.scalar_tensor_tensor(out=vt[:], in0=psum[:, 7, :512], scalar=1.0 / 512, in1=m2[:],
                                                                            op0=ALU.mult, op1=ALU.subtract),
                     reads=[R_ps[7], R_m2], writes=[R_vt])
                P.op("act", lambda e, vt=vt: e.activation(out=vt[:], in_=vt[:], func=AF.Ln, bias=EPS, scale=1.0), reads=[R_vt], writes=[R_vt])
                P.op("act", lambda e, vt=vt: e.activation(out=vt[:], in_=vt[:], func=AF.Exp, scale=-0.5), reads=[R_vt], writes=[R_vt])
                for cc in range(4):
                    P.op("dve", lambda e, cc=cc, mt=mt: e.tensor_tensor(out=ucf[:, cc, :], in0=ucf[:, cc, :], in1=mt[:], op=ALU.subtract),
                         reads=[R_ucf, R_mt], writes=[R_ucf])
                    P.op("dve", lambda e, cc=cc, vt=vt: e.tensor_tensor(out=ucf[:, cc, :], in0=ucf[:, cc, :], in1=vt[:], op=ALU.mult),
                         reads=[R_ucf, R_vt], writes=[R_ucf])
                    P.op("act", lambda e, cc=cc, t0=t0: e.activation(out=big[:, 12 + cc, t0:t0 + 512], in_=ucf[:, cc, :], func=AF.Silu,
                                                                      bias=vc[:, 13 + cc:14 + cc], scale=vc[:, 9 + cc:10 + cc]),
                         reads=[R_ucf, R_vc], writes=[R_big])
            def ev1(oc, si, n0, n, pb):
                P.op("dve", lambda e: e.tensor_tensor(out=xres[:, oc, HALO + n0:HALO + n0 + n], in0=xres[:, oc, HALO + n0:HALO + n0 + n],
                                                      in1=psum[:, pb, :n], op=ALU.add),
                     reads=[R_x, R_ps[pb]], writes=[R_x])
            linear(Wo1b, R_W["Wo1b"], 8, 0, D, 256, lambda kc, n0, n: big[:, 8 + kc, n0:n0 + n], [R_big], [(0, 512), (512, 512)], ev1)
            ffn(Wg1b, Wu1b, Wd1b, "Wg1b", "Wu1b", "Wd1b", SUBS2)
            for (n0, n) in SUBS2:
                sq, R_sq = sqs.next(); lnv, R_lnv = lnvs.next(); rr, R_rr = rrs.next()
                P.op("act", lambda e, sq=sq, n0=n0, n=n: e.activation(out=sq[:, :, :n], in_=xres[:, :, n0:n0 + n], func=AF.Square),
                     reads=[R_x], writes=[R_sq])
                for kc in range(8):
                    P.op("pe", lambda e, kc=kc, sq=sq, n=n: e.matmul(psum[:, 7, :n], lhsT=ONES, rhs=sq[:, kc, :n], start=(kc == 0), stop=(kc == 7)),
                         reads=[R_sq, R_cm], writes=[R_ps[7]])
                rstd_from_ps(psum[:, 7, :n], R_ps[7], 1.0 / D, lnv, R_lnv, rr, R_rr, n)
                for kc in range(8):
                    ot, R_ot = ots.next()
                    P.op("dve", lambda e, kc=kc, ot=ot, rr=rr, n0=n0, n=n: e.scalar_tensor_tensor(
                        out=ot[:, :n], in0=xres[:, kc, n0:n0 + n], scalar=gn[:, 4, kc:kc + 1], in1=rr[:, :n], op0=ALU.mult, op1=ALU.mult),
                        reads=[R_x, R_rr, R_gn], writes=[R_ot])
                    o0 = BQ * i + n0 - HALO
                    P.dma("pool", outT[kc * 128:(kc + 1) * 128, o0:o0 + n], ot[:, :n], reads=[R_ot], writes=[R_out])
        P.barrier()
        P.emit()
    return nc


def _const_mats():
    ones = np.ones((128, 128), np.float32)
    p_ = np.arange(128)
    trineg = -(p_[:, None] >= p_[None, :]).astype(np.float32)
    onesneg = -ones
    rot = np.zeros((128, 128), np.float32)
    for p in range(128):
        if p % 64 < 32:
            rot[p + 32, p] = -1.0
        else:
            rot[p - 32, p] = 1.0
    ident = np.eye(128, dtype=np.float32)
    return np.stack([ones, trineg, onesneg, rot, ident], 1).astype(NPBF), ident


def _rope_tables(pos):
    inv = (10000.0 ** (-(np.arange(0, 64, 2, dtype=np.float32)) / 64.0)).astype(np.float32)
    ang = pos.astype(np.float32)[None, :] * inv[np.arange(128) % 32][:, None]
    return np.cos(ang).astype(np.float32), np.sin(ang).astype(np.float32)


def make_in_maps(inputs, nslot):
    G = geom(nslot)
    T, TL, NQ, NB = G["T"], G["TL"], G["NQ"], G["NB"]
    x = np.asarray(inputs["x"], np.float32)
    cmat, ident = _const_mats()
    sbm, _, dm, _ = build_masks()
    f = lambda k: np.ascontiguousarray(np.asarray(inputs[k], np.float32))
    gains = np.zeros((128, 6, 8), np.float32)
    for gi, k in enumerate(["mix_norm_0", "ffn_norm_0", "mix_norm_1", "ffn_norm_1", "final_norm"]):
        gains[:, gi, :] = f(k).reshape(8, 128).T
    vecs = np.zeros((128, 24), np.float32)
    vecs[:, 0] = f("subln_0")
    vecs[:, 1:5] = f("pool_scale_1").reshape(4, 128).T
    vecs[:, 5:9] = f("dw_b_1").reshape(4, 128).T
    vecs[:, 9:13] = f("conv_norm_g_1").reshape(4, 128).T
    vecs[:, 13:17] = f("conv_norm_b_1").reshape(4, 128).T
    lamv = np.stack([f("lambda_q1_0"), f("lambda_k1_0"), f("lambda_q2_0"), f("lambda_k2_0")], 0)
    dwT = np.ascontiguousarray(f("dw_w_1").T.reshape(4, 128, 31).transpose(1, 0, 2))
    shared = dict(w_in_0=f("w_in_0"), w_out_0=f("w_out_0"), w_gate_0=f("w_gate_0"), w_up_0=f("w_up_0"), w_down_0=f("w_down_0"),
                  w_in_1=f("w_in_1"), w_out_1=f("w_out_1"), w_gate_1=f("w_gate_1"), w_up_1=f("w_up_1"), w_down_1=f("w_down_1"),
                  pool_w_1=f("pool_w_1"), gains=gains, vecs=vecs, lamv=lamv, dwT=dwT, cmat=cmat, identf=ident,
                  sbmask=sbm, dmask=dm)
    maps = []
    for c in range(8):
        b, j = c // 4, c % 4
        off = 1024 * (3 - j)
        xT = x[b, :T].T
        xkv = np.zeros((D, TL), np.float32)
        xkv[:, off:] = xT[:, :TL - off]
        posk = np.maximum(np.arange(TL) - off, 0)
        cosk, sink = _rope_tables(posk)
        xq = np.zeros((D, NQ), np.float32)
        posq = np.zeros(NQ, np.int64)
        rc = np.zeros((4, NQ), np.float32)
        for i in range(nslot):
            a0 = 1024 * (j + 4 * i) - HALO
            tt = np.arange(a0, a0 + SLOT)
            valid = tt >= 0
            xq[:, SLOT * i:SLOT * (i + 1)][:, valid] = xT[:, tt[valid]]
            posq[SLOT * i:SLOT * (i + 1)] = np.maximum(tt, 0)
            for g, w in enumerate((2, 4, 8, 16)):
                rc[g, SLOT * i:SLOT * (i + 1)] = 1.0 / np.minimum(np.maximum(tt, 0) + 1, w)
        cosq, sinq = _rope_tables(posq)
        vones = np.zeros((128, 3, 128), np.float32)
        vones[:, 3 - j:, :] = 1.0
        m = dict(shared)
        vt = np.zeros((128, G["NKT"]), np.float32)
        vt[:, 8 * (3 - j):] = 1.0
        m.update(xkv=xkv, xq=xq, cosk=cosk, sink=sink, cosq=cosq, sinq=sinq, rcnt=rc, vones=vones, vtile=vt)
        maps.append(m)
    return maps


_NC_CACHE = {}


def run(inputs, nslot, debug=False):
    key = (nslot, debug)
    if key not in _NC_CACHE:
        _NC_CACHE[key] = build_nc(nslot, debug)
    nc = _NC_CACHE[key]
    maps = make_in_maps(inputs, nslot)
    res = run_bass_kernel_spmd(nc, maps, core_ids=list(range(8)))
    T = 4096 * nslot
    out = np.zeros((2, T, D), np.float32)
    for c in range(8):
        b, j = c // 4, c % 4
        oT = res.results[c]["outT"]
        for i in range(nslot):
            a0 = 1024 * (j + 4 * i)
            out[b, a0:a0 + 1024, :] = oT[:, BQ * i:BQ * (i + 1)].T
    return out, res


def kernel(**inputs):
    out, _ = run(inputs, 4)
    return out
```

```python
import math
import numpy as np
import ml_dtypes
from contextlib import ExitStack
import concourse.bass as bass
import concourse.mybir as mybir
from concourse.bass_utils import run_bass_kernel_spmd

F32 = mybir.dt.float32
BF16 = mybir.dt.bfloat16
AF = mybir.ActivationFunctionType
ALU = mybir.AluOpType
NPBF = ml_dtypes.bfloat16

D = 1024
DFF = 2816
NFC = DFF // 128
BQ = 1024
HALO = 32
SLOT = BQ + HALO
SUB = 352
EPS = 1e-6
LAMBDA_INIT = 0.8 - 0.6 * math.exp(0.0)
SAME_ENGINE_SYNC = True


class DSem:
    __slots__ = ("h", "count")

    def __init__(self, h):
        self.h = h
        self.count = 0


class Res:
    __slots__ = ("name", "w", "r", "dsem")

    def __init__(self, name):
        self.name = name
        self.w = None
        self.r = []
        self.dsem = None


class Prog:
    ENGS = ("pe", "act", "dve", "pool", "sp")

    def __init__(self, nc, n_dma=48):
        self.nc = nc
        self.streams = {e: [] for e in self.ENGS}
        self.count = {e: 0 for e in self.ENGS}
        self.waited = {e: {} for e in self.ENGS}
        self.sem = {}
        for e in ("pe", "act", "dve", "pool"):
            self.sem[e] = nc.alloc_semaphore("prog_" + e)
        self.all_dsems = [DSem(nc.alloc_semaphore("dma_%d" % i)) for i in range(n_dma)]
        self.free_dsems = list(self.all_dsems)
        self.all_res = []

    def res(self, name):
        r = Res(name)
        self.all_res.append(r)
        return r

    def _get_dsem(self, res):
        if res.dsem is None:
            assert self.free_dsems, "out of DMA semaphores"
            res.dsem = self.free_dsems.pop()
        return res.dsem

    def _need(self, eng, tok, waits):
        if tok is None:
            return
        if tok[0] == "eng":
            _, e2, idx = tok
            if e2 == eng and (eng == "pe" or not SAME_ENGINE_SYNC):
                return
            key, sem, val = e2, self.sem[e2], idx
        else:
            _, ds, val = tok
            key, sem = id(ds), ds.h
        if self.waited[eng].get(key, 0) >= val:
            return
        self.waited[eng][key] = val
        for i, (s, v) in enumerate(waits):
            if s is sem:
                waits[i] = (s, max(v, val))
                return
        waits.append((sem, val))

    def _deps(self, eng, reads, writes):
        waits = []
        for r in reads:
            self._need(eng, r.w, waits)
        for w in writes:
            self._need(eng, w.w, waits)
            for t in w.r:
                self._need(eng, t, waits)
        return waits

    def _commit(self, tok, reads, writes):
        for r in reads:
            r.r.append(tok)
            if len(r.r) > 24:
                r.r = _prune(r.r)
        for w in writes:
            w.w = tok
            w.r = []

    def op(self, eng, fn, reads=(), writes=()):
        waits = self._deps(eng, reads, writes)
        self.count[eng] += 1
        tok = ("eng", eng, self.count[eng])
        self.streams[eng].append((waits, fn, (self.sem[eng], 1)))
        self._commit(tok, reads, writes)
        return tok

    def dma(self, eng, out_ap, in_ap, reads=(), writes=()):
        waits = self._deps(eng, reads, writes)
        anchor = writes[0] if writes else reads[0]
        ds = self._get_dsem(anchor)
        ds.count += 16
        tok = ("dma", ds, ds.count)
        self.streams[eng].append(
            (waits, (lambda e, o=out_ap, i=in_ap: e.dma_start(out=o, in_=i)), (ds.h, 16)))
        self._commit(tok, reads, writes)
        return tok

    def barrier(self):
        for eng in self.ENGS:
            waits = []
            for e2 in ("pe", "act", "dve", "pool"):
                if e2 != eng and self.count[e2] > 0:
                    self._need(eng, ("eng", e2, self.count[e2]), waits)
            for ds in self.all_dsems:
                if ds.count > 0:
                    self._need(eng, ("dma", ds, ds.count), waits)
            if waits:
                self.streams[eng].append((waits, None, None))
        for r in self.all_res:
            r.w = None
            r.r = []
            r.dsem = None
        self.free_dsems = list(self.all_dsems)

    def emit(self):
        nc = self.nc
        streams = self.streams
        with nc.Block() as block:
            def mk(engname):
                def body(e):
                    for waits, fn, inc in streams[engname]:
                        if fn is None:
                            for (s, v) in waits:
                                e.wait_ge(s, v)
                            continue
                        for (s, v) in waits[:-1]:
                            e.wait_ge(s, v)
                        ins = fn(e)
                        if waits:
                            ins._wait_ge(waits[-1][0], waits[-1][1])
                        ins.then_inc(inc[0], inc[1])
                return body
            if streams["pe"]:
                block.tensor(mk("pe"))
            if streams["act"]:
                block.scalar(mk("act"))
            if streams["dve"]:
                block.vector(mk("dve"))
            if streams["pool"]:
                block.gpsimd(mk("pool"))
            if streams["sp"]:
                block.sync(mk("sp"))
        self.streams = {e: [] for e in self.ENGS}


def _prune(toks):
    best = {}
    for t in toks:
        key = (t[0], t[1] if t[0] == "eng" else id(t[1]))
        if key not in best or best[key][2] < t[2]:
            best[key] = t
    return list(best.values())


class Rot:
    def __init__(self, P, nc, stack, name, shape, dtype, n, psum=False):
        self.items = []
        for i in range(n):
            t = stack.enter_context(nc.sbuf_tensor("rot_%s_%d" % (name, i), shape, dtype))
            self.items.append((t, P.res("%s%d" % (name, i))))
        self.i = 0

    def next(self):
        it = self.items[self.i % len(self.items)]
        self.i += 1
        return it


def geom(nslot):
    T = 4096 * nslot
    return dict(T=T, TL=T, NKT=T // 128, NQ=nslot * SLOT, NB=T // 1024)


def sub_info(i, s):
    LB = 1024 * (3 + 4 * i)
    q0 = LB - HALO + SUB * s
    kt_hi = (q0 + SUB - 1) // 128
    kbase = LB // 128 - 1
    return LB, q0, kt_hi, kbase


def build_masks():
    sbm, sbi, dm, di = [], {}, [], {}
    for s in range(3):
        LB, q0, kt_hi, kbase = sub_info(0, s)
        for kt in range(kbase, kt_hi + 1):
            r = kt - kbase
            kpos = (128 * kt + np.arange(128))[:, None]
            qpos = (q0 + np.arange(SUB))[None, :]
            m = (kpos < qpos)
            if not m.all():
                sbi[(s, r)] = len(sbm)
                sbm.append(m)
            m2 = (kpos // 64) <= (qpos // 64)
            if not m2.all():
                di[(s, r)] = len(dm)
                dm.append(m2)
    sbm = np.stack(sbm, 1).astype(np.float32).astype(NPBF)
    dm = np.stack(dm, 1).astype(np.float32).astype(NPBF)
    return sbm, sbi, dm, di


def build_nc(nslot, debug=False):
    G = geom(nslot)
    TL, NKT, NQ, NB = G["TL"], G["NKT"], G["NQ"], G["NB"]
    sbmask_np, sbidx, dmask_np, didx = build_masks()
    NSBM, NDM = sbmask_np.shape[1], dmask_np.shape[1]

    nc = bass.Bass("TRN2", target_bir_lowering=False)
    P = Prog(nc)

    def din(name, shape, dt=F32):
        return nc.dram_tensor(name, list(shape), dt, kind="ExternalInput").ap()

    def dscr(name, shape, dt=BF16):
        kind = "ExternalOutput" if debug else "Internal"
        return nc.dram_tensor(name, list(shape), dt, kind=kind).ap()

    xkv = din("xkv", [D, TL])
    xq = din("xq", [D, NQ])
    w_in0 = din("w_in_0", [D, 3072]); w_out0 = din("w_out_0", [D, D])
    w_g0 = din("w_gate_0", [D, DFF]); w_u0 = din("w_up_0", [D, DFF]); w_d0 = din("w_down_0", [DFF, D])
    w_in1 = din("w_in_1", [D, 1536]); w_out1 = din("w_out_1", [D, D])
    w_g1 = din("w_gate_1", [D, DFF]); w_u1 = din("w_up_1", [D, DFF]); w_d1 = din("w_down_1", [DFF, D])
    pool_w = din("pool_w_1", [4, 128, 128])
    gains = din("gains", [128, 6, 8])
    vecs = din("vecs", [128, 24])
    lamv = din("lamv", [4, 64])
    dwT = din("dwT", [128, 4, 31])
    cosk = din("cosk", [128, TL]); sink = din("sink", [128, TL])
    cosq = din("cosq", [128, NQ]); sinq = din("sinq", [128, NQ])
    rcnt = din("rcnt", [4, NQ])
    cmat = din("cmat", [128, 5, 128], BF16)
    identf = din("identf", [128, 128])
    sbmask_d = din("sbmask", [128, NSBM, SUB], BF16)
    dmask_d = din("dmask", [128, NDM, SUB], BF16)
    vones_d = din("vones", [128, 3, 128])
    vtile_d = din("vtile", [128, NKT])
    outT = nc.dram_tensor("outT", [D, nslot * BQ], F32, kind="ExternalOutput").ap()

    W0b = dscr("W0b", [8, 128, 3072]); Wo0b = dscr("Wo0b", [8, 128, D])
    Wg0b = dscr("Wg0b", [8, 128, DFF]); Wu0b = dscr("Wu0b", [8, 128, DFF]); Wd0b = dscr("Wd0b", [NFC, 128, D])
    W1b = dscr("W1b", [8, 128, 1536]); Wo1b = dscr("Wo1b", [8, 128, D])
    Wg1b = dscr("Wg1b", [8, 128, DFF]); Wu1b = dscr("Wu1b", [8, 128, DFF]); Wd1b = dscr("Wd1b", [NFC, 128, D])
    Pwb = dscr("Pwb", [4, 128, 128])
    KTs = dscr("KTs", [8, 128, TL]); Vs = dscr("Vs", [8, 128, NKT, 128])
    QTs = dscr("QTs", [8, 128, NQ]); ATs = dscr("ATs", [8, 128, NQ])
    R_W = {k: P.res(k) for k in ("W0b", "Wo0b", "Wg0b", "Wu0b", "Wd0b", "W1b", "Wo1b", "Wg1b", "Wu1b", "Wd1b", "Pwb")}
    R_KT = [P.res("KTs%d" % c) for c in range(8)]
    R_V = [P.res("Vs%d" % c) for c in range(8)]
    R_Q = [P.res("QTs%d" % c) for c in range(8)]
    R_AT = [P.res("ATs%d" % c) for c in range(8)]
    R_out = P.res("outT")

    cm = nc.alloc_sbuf_tensor("cm", [128, 5, 128], BF16); R_cm = P.res("cm")
    gn = nc.alloc_sbuf_tensor("gn", [128, 6, 8], F32); R_gn = P.res("gn")
    vc = nc.alloc_sbuf_tensor("vc", [128, 24], F32); R_vc = P.res("vc")
    idf = nc.alloc_sbuf_tensor("idf", [128, 128], F32); R_idf = P.res("idf")
    psum = nc.alloc_psum_tensor("psum", [128, 8, 512], F32)
    R_ps = [P.res("ps%d" % b) for b in range(8)]
    ONES, TRINEG, ONESNEG, ROT, IDENT = (cm[:, k, :] for k in range(5))

    P.dma("sp", cm[:], cmat, writes=[R_cm])
    P.dma("sp", gn[:], gains, writes=[R_gn])
    P.dma("sp", vc[:], vecs, writes=[R_vc])
    P.dma("sp", idf[:], identf, writes=[R_idf])

    def rstd_from_ps(ps_ap, R_psb, inv_n, lnv, R_lnv, rr, R_rr, n):
        P.op("act", lambda e: e.activation(out=lnv[:, :n], in_=ps_ap, func=AF.Ln, bias=EPS, scale=inv_n),
             reads=[R_psb], writes=[R_lnv])
        P.op("act", lambda e: e.activation(out=rr[:, :n], in_=lnv[:, :n], func=AF.Exp, scale=-0.5),
             reads=[R_lnv], writes=[R_rr])

    def norm_tile(xt, R_xt, n, sq, R_sq, psb, lnv, R_lnv, rr, R_rr, h, R_h, hoff=0, xoff=0):
        P.op("act", lambda e: e.activation(out=sq[:, :, :n], in_=xt[:, :, xoff:xoff + n], func=AF.Square),
             reads=[R_xt], writes=[R_sq])
        for kc in range(8):
            P.op("pe", lambda e, kc=kc: e.matmul(psum[:, psb, :n], lhsT=ONES, rhs=sq[:, kc, :n],
                                                  start=(kc == 0), stop=(kc == 7)),
                 reads=[R_sq, R_cm], writes=[R_ps[psb]])
        rstd_from_ps(psum[:, psb, :n], R_ps[psb], 1.0 / D, lnv, R_lnv, rr, R_rr, n)
        for kc in range(8):
            P.op("dve", lambda e, kc=kc: e.tensor_tensor(out=h[:, kc, hoff:hoff + n], in0=xt[:, kc, xoff:xoff + n],
                                                          in1=rr[:, :n], op=ALU.mult),
                 reads=[R_xt, R_rr], writes=[R_h])

    with ExitStack() as st:
        fin = Rot(P, nc, st, "fin", [128, 3072], F32, 2)
        fout = Rot(P, nc, st, "fout", [128, 3072], BF16, 2)
        jobs = []
        def add_w(src, dst, rdst, nchunk, C, gi):
            for kc in range(nchunk):
                jobs.append((src[kc * 128:(kc + 1) * 128, :], C, dst[kc], (gn[:, gi, kc:kc + 1] if gi is not None else None), rdst))
        add_w(w_in0, W0b, R_W["W0b"], 8, 3072, 0)
        bg_jobs = []
        def add_bg(src, dst, rdst, nchunk, C, gi):
            for kc in range(nchunk):
                for c0 in range(0, C, 1024):
                    cw = min(1024, C - c0)
                    bg_jobs.append((src[kc * 128:(kc + 1) * 128, c0:c0 + cw], cw, dst[kc, :, c0:c0 + cw],
                                    (gn[:, gi, kc:kc + 1] if gi is not None else None), rdst))
        add_bg(w_out0, Wo0b, R_W["Wo0b"], 8, D, None)
        add_bg(w_g0, Wg0b, R_W["Wg0b"], 8, DFF, 1)
        add_bg(w_u0, Wu0b, R_W["Wu0b"], 8, DFF, 1)
        add_bg(w_d0, Wd0b, R_W["Wd0b"], NFC, D, None)
        add_bg(w_in1, W1b, R_W["W1b"], 8, 1536, 2)
        add_bg(w_out1, Wo1b, R_W["Wo1b"], 8, D, None)
        add_bg(w_g1, Wg1b, R_W["Wg1b"], 8, DFF, 3)
        add_bg(w_u1, Wu1b, R_W["Wu1b"], 8, DFF, 3)
        add_bg(w_d1, Wd1b, R_W["Wd1b"], NFC, D, None)
        for g in range(4):
            bg_jobs.append((pool_w[g], 128, Pwb[g], None, R_W["Pwb"]))
        for n, (src, C, dst, gcol, rdst) in enumerate(jobs):
            ti, R_i = fin.next()
            to, R_o = fout.next()
            P.dma("sp", ti[:, :C], src, writes=[R_i])
            if gcol is not None:
                if n % 2 == 0:
                    P.op("dve", lambda e, ti=ti, to=to, C=C, gcol=gcol: e.tensor_scalar(
                        out=to[:, :C], in0=ti[:, :C], scalar1=gcol, scalar2=None, op0=ALU.mult),
                        reads=[R_i, R_gn], writes=[R_o])
                else:
                    P.op("act", lambda e, ti=ti, to=to, C=C, gcol=gcol: e.activation(
                        out=to[:, :C], in_=ti[:, :C], func=AF.Copy, scale=gcol),
                        reads=[R_i, R_gn], writes=[R_o])
            else:
                if n % 2 == 0:
                    P.op("dve", lambda e, ti=ti, to=to, C=C: e.tensor_copy(out=to[:, :C], in_=ti[:, :C]),
                         reads=[R_i], writes=[R_o])
                else:
                    P.op("act", lambda e, ti=ti, to=to, C=C: e.activation(out=to[:, :C], in_=ti[:, :C], func=AF.Copy),
                         reads=[R_i], writes=[R_o])
            P.dma("pool", dst, to[:, :C], reads=[R_o], writes=[rdst])
        P.barrier()
        P.emit()

    with ExitStack() as st:
        w0 = st.enter_context(nc.sbuf_tensor("s_w0", [128, 8, 3072], BF16)); R_w0 = P.res("w0")
        P.dma("sp", w0[:], W0b.rearrange("k p c -> p k c"), reads=[R_W["W0b"]], writes=[R_w0])
        xts = Rot(P, nc, st, "xt", [128, 8, 512], F32, 2)
        sqs = Rot(P, nc, st, "sq", [128, 8, 512], BF16, 2)
        lnvs = Rot(P, nc, st, "lnv", [128, 512], F32, 2)
        rrs = Rot(P, nc, st, "rr", [128, 512], F32, 2)
        hs = Rot(P, nc, st, "h", [128, 8, 512], BF16, 2)
        coss = Rot(P, nc, st, "cos", [128, 512], F32, 2)
        sins = Rot(P, nc, st, "sin", [128, 512], F32, 2)
        kbs = Rot(P, nc, st, "kb", [128, 512], BF16, 2)
        t1s = Rot(P, nc, st, "t1", [128, 512], F32, 2)
        t2s = Rot(P, nc, st, "t2", [128, 512], F32, 2)
        kst = Rot(P, nc, st, "kst", [128, 512], BF16, 4)
        vst = Rot(P, nc, st, "vst", [128, 8, 128], BF16, 3)
        xkv_v = xkv.rearrange("(k p) n -> p k n", p=128)
        xq_v = xq.rearrange("(k p) n -> p k n", p=128)
        kps_i = [0]

        def proj_T(h, R_h, n, chunks, scale, cos_t, R_cos, sin_t, R_sin, dst, R_dst, n0):
            for (c, col, rope) in chunks:
                pb = 1 + (kps_i[0] % 2); kps_i[0] += 1
                for kc in range(8):
                    P.op("pe", lambda e, kc=kc, pb=pb, col=col: e.matmul(
                        psum[:, pb, :n], lhsT=w0[:, kc, col:col + 128], rhs=h[:, kc, :n],
                        start=(kc == 0), stop=(kc == 7)), reads=[R_w0, R_h], writes=[R_ps[pb]])
                ks, R_ks = kst.next()
                if not rope:
                    P.op("act", lambda e, pb=pb, ks=ks: e.activation(out=ks[:, :n], in_=psum[:, pb, :n],
                                                                     func=AF.Copy, scale=scale),
                         reads=[R_ps[pb]], writes=[R_ks])
                else:
                    kb, R_kb = kbs.next()
                    t1, R_t1 = t1s.next()
                    t2, R_t2 = t2s.next()
                    P.op("act", lambda e, pb=pb, kb=kb: e.activation(out=kb[:, :n], in_=psum[:, pb, :n],
                                                                     func=AF.Copy, scale=scale),
                         reads=[R_ps[pb]], writes=[R_kb])
                    P.op("pe", lambda e, kb=kb: e.matmul(psum[:, 3, :n], lhsT=ROT, rhs=kb[:, :n], start=True, stop=True),
                         reads=[R_kb, R_cm], writes=[R_ps[3]])
                    P.op("dve", lambda e, kb=kb, t1=t1: e.tensor_tensor(out=t1[:, :n], in0=kb[:, :n], in1=cos_t[:, :n], op=ALU.mult),
                         reads=[R_kb, R_cos], writes=[R_t1])
                    P.op("dve", lambda e, t2=t2: e.tensor_tensor(out=t2[:, :n], in0=psum[:, 3, :n], in1=sin_t[:, :n], op=ALU.mult),
                         reads=[R_ps[3], R_sin], writes=[R_t2])
                    P.op("dve", lambda e, t1=t1, t2=t2, ks=ks: e.tensor_tensor(out=ks[:, :n], in0=t1[:, :n], in1=t2[:, :n], op=ALU.add),
                         reads=[R_t1, R_t2], writes=[R_ks])
                P.dma("pool", dst[c, :, n0:n0 + n], ks[:, :n], reads=[R_ks], writes=[R_dst[c]])

        KCH = [(c, 512 + 128 * c, False) for c in range(4)] + [(4 + c, 2048 + 128 * c, True) for c in range(4)]
        QCH = [(c, 128 * c, False) for c in range(4)] + [(4 + c, 1536 + 128 * c, True) for c in range(4)]
        vev = [0]
        for t in range(TL // 512):
            n0 = t * 512
            xt, R_xt = xts.next(); sq, R_sq = sqs.next(); lnv, R_lnv = lnvs.next(); rr, R_rr = rrs.next()
            h, R_h = hs.next(); ct, R_ct = coss.next(); sn, R_sn = sins.next()
            P.dma("sp", xt[:], xkv_v[:, :, n0:n0 + 512], writes=[R_xt])
            P.dma("sp", ct[:], cosk[:, n0:n0 + 512], writes=[R_ct])
            P.dma("sp", sn[:], sink[:, n0:n0 + 512], writes=[R_sn])
            norm_tile(xt, R_xt, 512, sq, R_sq, 0, lnv, R_lnv, rr, R_rr, h, R_h)
            proj_T(h, R_h, 512, KCH, 1.0, ct, R_ct, sn, R_sn, KTs, R_KT, n0)
            for sub in range(4):
                vs_, R_vs = vst.next()
                for half, col in ((0, 1024), (1, 2560)):
                    pb = 4 + (vev[0] % 4)
                    for kc in range(8):
                        P.op("pe", lambda e, kc=kc, pb=pb, col=col, sub=sub, h=h: e.matmul(
                            psum[:, pb, :512], lhsT=h[:, kc, sub * 128:(sub + 1) * 128], rhs=w0[:, kc, col:col + 512],
                            start=(kc == 0), stop=(kc == 7)), reads=[R_w0, R_h], writes=[R_ps[pb]])
                    dstv = vs_[:, 4 * half:4 * half + 4, :]
                    srcv = psum[:, pb, :].rearrange("p (c d) -> p c d", d=128)
                    if vev[0] % 2 == 0:
                        P.op("dve", lambda e, dstv=dstv, srcv=srcv: e.tensor_copy(out=dstv, in_=srcv),
                             reads=[R_ps[pb]], writes=[R_vs])
                    else:
                        P.op("act", lambda e, dstv=dstv, srcv=srcv: e.activation(out=dstv, in_=srcv, func=AF.Copy),
                             reads=[R_ps[pb]], writes=[R_vs])
                    vev[0] += 1
                kt = 4 * t + sub
                P.dma("pool", Vs[:, :, kt, :].rearrange("c p d -> p c d"), vs_[:], reads=[R_vs], writes=R_V)
        for t in range(NQ // SUB):
            n0 = t * SUB
            xt, R_xt = xts.next(); sq, R_sq = sqs.next(); lnv, R_lnv = lnvs.next(); rr, R_rr = rrs.next()
            h, R_h = hs.next(); ct, R_ct = coss.next(); sn, R_sn = sins.next()
            P.dma("sp", xt[:, :, :SUB], xq_v[:, :, n0:n0 + SUB], writes=[R_xt])
            P.dma("sp", ct[:, :SUB], cosq[:, n0:n0 + SUB], writes=[R_ct])
            P.dma("sp", sn[:, :SUB], sinq[:, n0:n0 + SUB], writes=[R_sn])
            norm_tile(xt, R_xt, SUB, sq, R_sq, 0, lnv, R_lnv, rr, R_rr, h, R_h)
            proj_T(h, R_h, SUB, QCH, 0.125, ct, R_ct, sn, R_sn, QTs, R_Q, n0)
        P.barrier()
        P.emit()

    with ExitStack() as st:
        KT_sb = st.enter_context(nc.sbuf_tensor("s_KT_sb", [128, TL], BF16)); R_KTsb = P.res("KT_sb")
        V_sb = st.enter_context(nc.sbuf_tensor("s_V_sb", [128, NKT, 128], BF16)); R_Vsb = P.res("V_sb")
        Qz = [st.enter_context(nc.sbuf_tensor("s_Qz%d" % hh, [128, NQ], BF16)) for hh in range(2)]; R_Qz = P.res("Qz")
        P.op("pool", lambda e: e.memset(Qz[0][:], 0.0), writes=[R_Qz])
        P.op("pool", lambda e: e.memset(Qz[1][:], 0.0), writes=[R_Qz])
        ATo = Rot(P, nc, st, "ATo", [128, NQ], BF16, 2)
        sbm = st.enter_context(nc.sbuf_tensor("s_sbm", [128, NSBM, SUB], BF16)); R_sbm = P.res("sbm")
        dmk = st.enter_context(nc.sbuf_tensor("s_dmk", [128, NDM, SUB], BF16)); R_dmk = P.res("dmk")
        vonesf = st.enter_context(nc.sbuf_tensor("s_vonesf", [128, 3, 128], F32)); R_vonesf = P.res("vonesf")
        lamt = st.enter_context(nc.sbuf_tensor("s_lamt", [128, 4, 64], F32)); R_lamt = P.res("lamt")
        lamp = st.enter_context(nc.sbuf_tensor("s_lamp", [128, 2, 64], F32)); R_lamp = P.res("lamp")
        lams = st.enter_context(nc.sbuf_tensor("s_lams", [128, 4], F32)); R_lams = P.res("lams")
        fs = Rot(P, nc, st, "f", [128, SUB], F32, 6)
        dsqs = Rot(P, nc, st, "dsq", [128, SUB], BF16, 1)
        vtile = st.enter_context(nc.sbuf_tensor("s_vtile", [128, NKT], F32)); R_vtile = P.res("vtile")
        onesf = st.enter_context(nc.sbuf_tensor("s_onesf", [128, 128], F32)); R_onesf = P.res("onesf")
        P.dma("sp", vtile[:], vtile_d, writes=[R_vtile])
        P.op("dve", lambda e: e.memset(onesf[:], 1.0), writes=[R_onesf])

        P.dma("sp", sbm[:], sbmask_d, writes=[R_sbm])
        P.dma("sp", dmk[:], dmask_d, writes=[R_dmk])
        P.dma("sp", vonesf[:], vones_d, writes=[R_vonesf])
        P.dma("sp", lamt[:].rearrange("p a d -> p (a d)"), lamv.rearrange("a d -> (a d)").rearrange("(o n) -> o n", o=1).to_broadcast([128, 256]),
              writes=[R_lamt])
        P.op("dve", lambda e: e.tensor_tensor(out=lamp[:, 0, :], in0=lamt[:, 0, :], in1=lamt[:, 1, :], op=ALU.mult),
             reads=[R_lamt], writes=[R_lamp])
        P.op("dve", lambda e: e.tensor_tensor(out=lamp[:, 1, :], in0=lamt[:, 2, :], in1=lamt[:, 3, :], op=ALU.mult),
             reads=[R_lamt], writes=[R_lamp])
        P.op("dve", lambda e: e.reduce_sum(out=lams[:, 0:2], in_=lamp[:], axis=mybir.AxisListType.X),
             reads=[R_lamp], writes=[R_lams])
        P.op("act", lambda e: e.activation(out=lams[:, 0:2], in_=lams[:, 0:2], func=AF.Exp), reads=[R_lams], writes=[R_lams])
        P.op("dve", lambda e: e.tensor_tensor(out=lams[:, 2:3], in0=lams[:, 1:2], in1=lams[:, 0:1], op=ALU.subtract),
             reads=[R_lams], writes=[R_lams])
        P.op("dve", lambda e: e.tensor_scalar(out=lams[:, 2:3], in0=lams[:, 2:3], scalar1=-LAMBDA_INIT, scalar2=None, op0=ALU.add),
             reads=[R_lams], writes=[R_lams])
        P.op("dve", lambda e: e.tensor_scalar(out=lams[:, 3:4], in0=vc[:, 0:1], scalar1=(1.0 - LAMBDA_INIT), scalar2=None, op0=ALU.mult),
             reads=[R_vc, R_lams], writes=[R_lams])
        NLAM = lams[:, 2:3]
        GSUB = lams[:, 3:4]

        bfin = Rot(P, nc, st, "bfin", [128, 1024], F32, 3)
        bfout = Rot(P, nc, st, "bfout", [128, 1024], BF16, 3)
        bg_state = dict(next=0, inflight=[])

        def bg_tick(k):
            if k % 7 != 0:
                return
            infl = bg_state["inflight"]
            if len(infl) >= 2 and infl[0]["stage"] == 2:
                j = infl.pop(0)
                P.dma("sp", j["dst"], j["to"][:, :j["C"]], reads=[j["R_o"]], writes=[j["rdst"]])
            for j in infl:
                if j["stage"] == 1:
                    ti, to, C, gcol = j["ti"], j["to"], j["C"], j["gcol"]
                    if gcol is not None:
                        P.op("dve", lambda e, ti=ti, to=to, C=C, gcol=gcol: e.tensor_scalar(
                            out=to[:, :C], in0=ti[:, :C], scalar1=gcol, scalar2=None, op0=ALU.mult),
                            reads=[j["R_i"], R_gn], writes=[j["R_o"]])
                    else:
                        P.op("dve", lambda e, ti=ti, to=to, C=C: e.tensor_copy(out=to[:, :C], in_=ti[:, :C]),
                             reads=[j["R_i"]], writes=[j["R_o"]])
                    j["stage"] = 2
                    break
            if bg_state["next"] < len(bg_jobs) and len(infl) < 3:
                src, C, dst, gcol, rdst = bg_jobs[bg_state["next"]]
                bg_state["next"] += 1
                ti, R_i = bfin.next(); to, R_o = bfout.next()
                P.dma("sp", ti[:, :C], src, writes=[R_i])
                infl.append(dict(stage=1, ti=ti, R_i=R_i, to=to, R_o=R_o, C=C, dst=dst, gcol=gcol, rdst=rdst))

        def bg_flush():
            k = 0
            while bg_state["next"] < len(bg_jobs) or bg_state["inflight"]:
                infl = bg_state["inflight"]
                if infl and infl[0]["stage"] == 2 and (len(infl) < 2 or bg_state["next"] >= len(bg_jobs)):
                    j = infl.pop(0)
                    P.dma("sp", j["dst"], j["to"][:, :j["C"]], reads=[j["R_o"]], writes=[j["rdst"]])
                    continue
                bg_tick(0)
                k += 1
                assert k < 100000

        def load_chunk(c):
            P.dma("sp", KT_sb[:], KTs[c], reads=[R_KT[c]], writes=[R_KTsb])
            P.dma("sp", V_sb[:], Vs[c], reads=[R_V[c]], writes=[R_Vsb])
            P.dma("sp", Qz[0][0:64, :], QTs[c, 0:64, :], reads=[R_Q[c]], writes=[R_Qz])
            P.dma("sp", Qz[1][64:128, :], QTs[c, 64:128, :], reads=[R_Q[c]], writes=[R_Qz])

        Srot = [Rot(P, nc, st, "Sp%d" % hh, [128, SUB], BF16, 2) for hh in range(2)]
        e2s = Rot(P, nc, st, "e2", [128, 2, SUB], F32, 2)
        sp2s = Rot(P, nc, st, "sp2", [128, 2, SUB], BF16, 3)
        wf2s = Rot(P, nc, st, "wf2", [128, 2, SUB], F32, 2)
        w2s = Rot(P, nc, st, "w2", [128, 2, SUB], BF16, 3)
        for c in range(4):
            load_chunk(c)
            at, R_at = ATo.next()
            steps = []
            for i in range(nslot):
                for s in range(3):
                    LB, q0, kt_hi, kbase = sub_info(i, s)
                    qc0 = SLOT * i + SUB * s
                    kts = list(range(kt_hi, -1, -1))
                    for n, kt in enumerate(kts):
                        steps.append(dict(qc0=qc0, kt=kt, first=(n == 0), last=(n == len(kts) - 1),
                                          mi=sbidx.get((s, kt - kbase)) if kt >= kbase else None))
            NS = len(steps)
            stt = [dict() for _ in range(NS)]
            Scur = [None, None]

            def sZ(n):
                sd = steps[n]; b = n % 3; kt, qc0 = sd["kt"], sd["qc0"]
                for hh in range(2):
                    P.op("pe", lambda e, hh=hh: e.matmul(psum[:, 2 * b + hh, :SUB], lhsT=KT_sb[:, kt * 128:(kt + 1) * 128],
                                                         rhs=Qz[hh][:, qc0:qc0 + SUB], start=True, stop=False),
                         reads=[R_KTsb, R_Qz], writes=[R_ps[2 * b + hh]])

            def sE(n):
                b = n % 3
                e_, R_e = e2s.next()
                P.op("act", lambda e: e.activation(out=e_[:], in_=psum[:, 2 * b:2 * b + 2, :SUB], func=AF.Exp),
                     reads=[R_ps[2 * b], R_ps[2 * b + 1]], writes=[R_e])
                stt[n]["e"] = (e_, R_e)

            def sSP(n):
                sd = steps[n]
                e_, R_e = stt[n]["e"]
                sp_, R_sp = sp2s.next()
                P.op("act", lambda e: e.activation(out=sp_[:], in_=e_[:], func=AF.Ln, bias=1.0, scale=1.0),
                     reads=[R_e], writes=[R_sp])
                if sd["mi"] is not None:
                    mi = sd["mi"]
                    for hh in range(2):
                        P.op("dve", lambda e, hh=hh: e.tensor_tensor(out=sp_[:, hh, :], in0=sp_[:, hh, :], in1=sbm[:, mi, :], op=ALU.mult),
                             reads=[R_sp, R_sbm], writes=[R_sp])
                stt[n]["sp"] = (sp_, R_sp)

            def sLW(n):
                sd = steps[n]; b = n % 3
                sp_, R_sp = stt[n]["sp"]
                for hh in range(2):
                    lb = 2 * b + hh
                    if sd["first"]:
                        P.op("pe", lambda e, hh=hh, lb=lb: e.matmul(psum[:, lb, :SUB], lhsT=TRINEG, rhs=sp_[:, hh, :], start=False, stop=True),
                             reads=[R_sp, R_cm], writes=[R_ps[lb]])
                    else:
                        S_, R_S = Scur[hh]
                        P.op("pe", lambda e, hh=hh, lb=lb: e.matmul(psum[:, lb, :SUB], lhsT=TRINEG, rhs=sp_[:, hh, :], start=False, stop=False),
                             reads=[R_sp, R_cm], writes=[R_ps[lb]])
                        P.op("pe", lambda e, hh=hh, lb=lb, S_=S_: e.matmul(psum[:, lb, :SUB], lhsT=ONESNEG, rhs=S_[:], start=False, stop=True),
                             reads=[R_S, R_cm], writes=[R_ps[lb]])
                if not sd["last"]:
                    for hh in range(2):
                        eng = "pool" if hh == 0 else "dve"
                        Sn, R_Sn = Srot[hh].next()
                        if sd["first"]:
                            P.op(eng, lambda e, hh=hh, Sn=Sn: e.tensor_copy(out=Sn[:], in_=sp_[:, hh, :]), reads=[R_sp], writes=[R_Sn])
                        else:
                            S_, R_S = Scur[hh]
                            P.op(eng, lambda e, hh=hh, Sn=Sn, S_=S_: e.tensor_tensor(out=Sn[:], in0=S_[:], in1=sp_[:, hh, :], op=ALU.add),
                                 reads=[R_S, R_sp], writes=[R_Sn])
                        Scur[hh] = (Sn, R_Sn)

            def sW(n):
                sd = steps[n]; b = n % 3
                wf_, R_wf = wf2s.next()
                w_, R_w = w2s.next()
                P.op("act", lambda e: e.activation(out=wf_[:], in_=psum[:, 2 * b:2 * b + 2, :SUB], func=AF.Exp),
                     reads=[R_ps[2 * b], R_ps[2 * b + 1]], writes=[R_wf])
                if sd["mi"] is not None:
                    mi = sd["mi"]
                    for hh in range(2):
                        P.op("dve", lambda e, hh=hh: e.tensor_tensor(out=w_[:, hh, :], in0=wf_[:, hh, :], in1=sbm[:, mi, :], op=ALU.mult),
                             reads=[R_wf, R_sbm], writes=[R_w])
                else:
                    P.op("dve", lambda e: e.tensor_copy(out=w_[:], in_=wf_[:]), reads=[R_wf], writes=[R_w])
                stt[n]["w"] = (w_, R_w)

            def sPV(n):
                sd = steps[n]; kt = sd["kt"]; qc0 = sd["qc0"]
                w_, R_w = stt[n]["w"]
                at_, R_at_ = at, R_at
                for hh in range(2):
                    ob = 6 + hh
                    P.op("pe", lambda e, hh=hh, ob=ob: e.matmul(psum[:, ob, :SUB], lhsT=V_sb[:, kt, :], rhs=w_[:, hh, :],
                                                                start=sd["first"], stop=sd["last"]),
                         reads=[R_Vsb, R_w], writes=[R_ps[ob]])
                    if sd["last"]:
                        pr = slice(64 * hh, 64 * hh + 64)
                        P.op("dve", lambda e, pr=pr, ob=ob: e.tensor_copy(out=at_[pr, qc0:qc0 + SUB], in_=psum[pr, ob, :SUB]),
                             reads=[R_ps[ob]], writes=[R_at_])
                stt[n].clear()

            for t in range(NS + 4):
                bg_tick(t)
                if 0 <= t - 1 < NS: sE(t - 1)
                if 0 <= t - 2 < NS: sLW(t - 2)
                if 0 <= t - 3 < NS: sW(t - 3)
                if 0 <= t - 4 < NS: sPV(t - 4)
                if 0 <= t - 1 < NS: sSP(t - 1)
                if t < NS: sZ(t)
            P.dma("pool", ATs[c], at[:], reads=[R_at], writes=[R_AT[c]])

        bg_flush()
        p2s = Rot(P, nc, st, "p2", [128, 2, SUB], BF16, 6)
        acc_rot = Rot(P, nc, st, "acc2", [128, 2, SUB], F32, 4)
        psum2s = Rot(P, nc, st, "psm2", [128, 2, SUB], BF16, 2)
        for hd in range(4):
            c = 4 + hd
            load_chunk(c)
            at, R_at = ATo.next()
            for i in range(nslot):
                for s in range(3):
                    LB, q0, kt_hi, kbase = sub_info(i, s)
                    qc0 = SLOT * i + SUB * s
                    steps = []
                    for kt in range(0, kt_hi + 1):
                        steps.append(dict(kt=kt, first=(kt == 0), last=(kt == kt_hi),
                                          mi=didx.get((s, kt - kbase)) if kt >= kbase else None))
                    NS = len(steps)
                    pst = [None] * NS
                    accs = {k: acc_rot.next() for k in ("mD", "b0", "b1", "b2")}
                    accstate = {}
                    held = {}

                    def dZ(n):
                        sd = steps[n]; kt = sd["kt"]; b = n % 2; qq = qc0
                        for m in range(2):
                            P.op("pe", lambda e, m=m: e.matmul(psum[:, 2 * b + m, :SUB], lhsT=KT_sb[:, kt * 128:(kt + 1) * 128],
                                                               rhs=Qz[m][:, qq:qq + SUB], start=True, stop=True),
                                 reads=[R_KTsb, R_Qz], writes=[R_ps[2 * b + m]])

                    def dP(n):
                        sd = steps[n]; b = n % 2
                        p_, R_p = p2s.next()
                        P.op("act", lambda e: e.activation(out=p_[:], in_=psum[:, 2 * b:2 * b + 2, :SUB], func=AF.Exp),
                             reads=[R_ps[2 * b], R_ps[2 * b + 1]], writes=[R_p])
                        if sd["mi"] is not None:
                            mi = sd["mi"]
                            for m in range(2):
                                P.op("dve", lambda e, m=m: e.tensor_tensor(out=p_[:, m, :], in0=p_[:, m, :], in1=dmk[:, mi, :], op=ALU.mult),
                                     reads=[R_p, R_dmk], writes=[R_p])
                        pst[n] = (p_, R_p)

                    def dC(n):
                        sd = steps[n]; kt = sd["kt"]
                        p_, R_p = pst[n]
                        for m in range(2):
                            P.op("pe", lambda e, m=m: e.matmul(psum[:, 4 + m, :SUB], lhsT=V_sb[:, kt, :], rhs=p_[:, m, :],
                                                               start=sd["first"], stop=sd["last"]),
                                 reads=[R_Vsb, R_p], writes=[R_ps[4 + m]])
                        ak = ("b%d" % (kt // 8)) if kt < 24 else "mD"
                        pa_, R_pa = accs[ak]
                        src_, R_src = None, None
                        if kt % 2 == 0 and not sd["last"]:
                            held["p"] = (p_, R_p)
                        elif kt % 2 == 1:
                            hp_, R_hp = held.pop("p")
                            t_, R_t = psum2s.next()
                            P.op("dve", lambda e: e.tensor_tensor(out=t_[:], in0=hp_[:], in1=p_[:], op=ALU.add),
                                 reads=[R_hp, R_p], writes=[R_t])
                            src_, R_src = t_, R_t
                        else:
                            src_, R_src = p_, R_p
                        if src_ is not None:
                            if ak not in accstate:
                                accstate[ak] = True
                                P.op("dve", lambda e: e.tensor_copy(out=pa_[:], in_=src_[:]), reads=[R_src], writes=[R_pa])
                            else:
                                P.op("dve", lambda e: e.tensor_tensor(out=pa_[:], in0=pa_[:], in1=src_[:], op=ALU.add),
                                     reads=[R_src, R_pa], writes=[R_pa])
                        if sd["last"]:
                            order = ["mD", "b0", "b1", "b2"]
                            for m in range(2):
                                for qi, ak2 in enumerate(order):
                                    pa2, R_pa2 = accs[ak2]
                                    lh = onesf[:] if ak2[0] == "m" else vonesf[:, int(ak2[1]), :]
                                    P.op("pe", lambda e, m=m, pa2=pa2, lh=lh, qi=qi: e.matmul(
                                        psum[:, 6 + m, :SUB], lhsT=lh, rhs=pa2[:, m, :], start=(qi == 0), stop=(qi == 3)),
                                        reads=[R_pa2, R_onesf, R_vonesf], writes=[R_ps[6 + m]])
                        pst[n] = None

                    for t in range(NS + 2):
                        if 0 <= t - 1 < NS: dP(t - 1)
                        if 0 <= t - 2 < NS: dC(t - 2)
                        if t < NS: dZ(t)
                    on = []
                    for m in range(2):
                        r_, R_r = fs.next()
                        P.op("dve", lambda e, r_=r_, m=m: e.tensor_scalar(out=r_[:], in0=psum[:, 6 + m, :SUB], scalar1=1e-30, scalar2=None, op0=ALU.max),
                             reads=[R_ps[6 + m]], writes=[R_r])
                        P.op("dve", lambda e, r_=r_: e.reciprocal(out=r_[:], in_=r_[:]), reads=[R_r], writes=[R_r])
                        o_, R_o = fs.next()
                        P.op("dve", lambda e, r_=r_, o_=o_, m=m: e.tensor_tensor(out=o_[:], in0=psum[:, 4 + m, :SUB], in1=r_[:], op=ALU.mult),
                             reads=[R_ps[4 + m], R_r], writes=[R_o])
                        on.append((o_, R_o))
                    d_, R_d = fs.next()
                    P.op("dve", lambda e, d_=d_, on=on: e.scalar_tensor_tensor(out=d_[:], in0=on[1][0][:], scalar=NLAM, in1=on[0][0][:],
                                                                                 op0=ALU.mult, op1=ALU.add),
                         reads=[on[0][1], on[1][1], R_lams], writes=[R_d])
                    dq_, R_dq = dsqs.next()
                    P.op("act", lambda e, d_=d_, dq_=dq_: e.activation(out=dq_[:], in_=d_[:], func=AF.Square), reads=[R_d], writes=[R_dq])
                    P.op("pe", lambda e, dq_=dq_: e.matmul(psum[:, 0, :SUB], lhsT=ONES, rhs=dq_[:], start=True, stop=True),
                         reads=[R_dq, R_cm], writes=[R_ps[0]])
                    ln_, R_ln = fs.next()
                    P.op("act", lambda e, ln_=ln_: e.activation(out=ln_[:], in_=psum[:, 0, :SUB], func=AF.Ln, bias=EPS, scale=1.0 / 128),
                         reads=[R_ps[0]], writes=[R_ln])
                    P.op("act", lambda e, ln_=ln_: e.activation(out=ln_[:], in_=ln_[:], func=AF.Exp, scale=-0.5),
                         reads=[R_ln], writes=[R_ln])
                    P.op("dve", lambda e, d_=d_, ln_=ln_, qc0=qc0, at=at: e.scalar_tensor_tensor(
                        out=at[:, qc0:qc0 + SUB], in0=d_[:], scalar=GSUB, in1=ln_[:], op0=ALU.mult, op1=ALU.mult),
                        reads=[R_d, R_ln, R_lams], writes=[R_at])
            P.dma("pool", ATs[c], at[:], reads=[R_at], writes=[R_AT[c]])
        P.barrier()
        P.emit()

    with ExitStack() as st:
        xres = st.enter_context(nc.sbuf_tensor("s_xres", [128, 8, SLOT], F32)); R_x = P.res("xres")
        hb = st.enter_context(nc.sbuf_tensor("s_hb", [128, 8, SLOT], BF16)); R_hb = P.res("hb")
        big = st.enter_context(nc.sbuf_tensor("s_big", [128, NFC, SLOT], BF16)); R_big = P.res("big")
        wbufs = Rot(P, nc, st, "wb", [128, NFC * 128], BF16, 4)
        sqs = Rot(P, nc, st, "sq3", [128, 8, 512], BF16, 1)
        lnvs = Rot(P, nc, st, "lnv3", [128, 512], F32, 1)
        rrs = Rot(P, nc, st, "rr3", [128, 512], F32, 1)
        sgs = Rot(P, nc, st, "sg", [128, 512], F32, 2)
        xps = Rot(P, nc, st, "xp", [128, SLOT], F32, 1)
        pa = Rot(P, nc, st, "pa", [128, SLOT], F32, 2)
        rcs = Rot(P, nc, st, "rc", [128, BQ], F32, 1)
        plb = Rot(P, nc, st, "plb", [128, BQ], BF16, 1)
        diags = Rot(P, nc, st, "diag", [128, 31, 128], BF16, 2)
        dwt = st.enter_context(nc.sbuf_tensor("s_dwt", [128, 4, 31], F32)); R_dwt = P.res("dwt")
        ucf = st.enter_context(nc.sbuf_tensor("s_ucf", [128, 4, 512], F32)); R_ucf = P.res("ucf")
        ucbq = st.enter_context(nc.sbuf_tensor("s_ucbq", [128, 8, 512], BF16)); R_ucb = P.res("ucb"); R_ucq = P.res("ucq")
        ucb = ucbq[:, 0:4, :]; ucq = ucbq[:, 4:8, :]
        mts = Rot(P, nc, st, "mt", [128, 512], F32, 3)
        ots = Rot(P, nc, st, "ot", [128, 512], F32, 2)
        P.dma("sp", dwt[:], dwT, writes=[R_dwt])
        xq_v = xq.rearrange("(k p) n -> p k n", p=128)
        ATv = big[:, 0:8, :]
        psi = [0]

        def next_pb(lo=0, n=4):
            pb = lo + psi[0] % n
            psi[0] += 1
            return pb

        def linear(Wd, R_Wd, KC, col0, ncols, group, rhs, R_rhs, subs, evac):
            for g0 in range(0, ncols, group):
                gw = min(group, ncols - g0)
                wb, R_wb = wbufs.next()
                wv = wb[:, :KC * gw].rearrange("p (k c) -> p k c", c=gw)
                P.dma("sp", wv, Wd[:, :, col0 + g0:col0 + g0 + gw].rearrange("k p c -> p k c"), reads=[R_Wd], writes=[R_wb])
                for ol in range(gw // 128):
                    oc = (g0 // 128) + ol
                    for si, (n0, n) in enumerate(subs):
                        pb = next_pb()
                        for kc in range(KC):
                            P.op("pe", lambda e, kc=kc, pb=pb, ol=ol, n0=n0, n=n, wv=wv: e.matmul(
                                psum[:, pb, :n], lhsT=wv[:, kc, ol * 128:(ol + 1) * 128], rhs=rhs(kc, n0, n),
                                start=(kc == 0), stop=(kc == KC - 1)), reads=[R_wb] + R_rhs, writes=[R_ps[pb]])
                        evac(oc, si, n0, n, pb)

        def norm_slot(subs):
            for (n0, n) in subs:
                sq, R_sq = sqs.next(); lnv, R_lnv = lnvs.next(); rr, R_rr = rrs.next()
                norm_tile(xres, R_x, n, sq, R_sq, 7, lnv, R_lnv, rr, R_rr, hb, R_hb, hoff=n0, xoff=n0)

        def ffn(Wg, Wu, Wdn, kg, ku, kd, subs):
            norm_slot(subs)
            for f0 in range(0, DFF, 256):
                gw = min(256, DFF - f0)
                wg, R_wg = wbufs.next(); wu, R_wu = wbufs.next()
                wgv = wg[:, :8 * gw].rearrange("p (k c) -> p k c", c=gw)
                wuv = wu[:, :8 * gw].rearrange("p (k c) -> p k c", c=gw)
                P.dma("sp", wgv, Wg[:, :, f0:f0 + gw].rearrange("k p c -> p k c"), reads=[R_W[kg]], writes=[R_wg])
                P.dma("sp", wuv, Wu[:, :, f0:f0 + gw].rearrange("k p c -> p k c"), reads=[R_W[ku]], writes=[R_wu])
                for ol in range(gw // 128):
                    fc = f0 // 128 + ol
                    for (n0, n) in subs:
                        pg = next_pb(0, 6); pu = next_pb(0, 6)
                        for kc in range(8):
                            P.op("pe", lambda e, kc=kc, pg=pg, ol=ol, n0=n0, n=n, wgv=wgv: e.matmul(
                                psum[:, pg, :n], lhsT=wgv[:, kc, ol * 128:(ol + 1) * 128], rhs=hb[:, kc, n0:n0 + n],
                                start=(kc == 0), stop=(kc == 7)), reads=[R_wg, R_hb], writes=[R_ps[pg]])
                        for kc in range(8):
                            P.op("pe", lambda e, kc=kc, pu=pu, ol=ol, n0=n0, n=n, wuv=wuv: e.matmul(
                                psum[:, pu, :n], lhsT=wuv[:, kc, ol * 128:(ol + 1) * 128], rhs=hb[:, kc, n0:n0 + n],
                                start=(kc == 0), stop=(kc == 7)), reads=[R_wu, R_hb], writes=[R_ps[pu]])
                        sg, R_sg = sgs.next()
                        P.op("act", lambda e, sg=sg, pg=pg, n=n: e.activation(out=sg[:, :n], in_=psum[:, pg, :n], func=AF.Silu),
                             reads=[R_ps[pg]], writes=[R_sg])
                        P.op("dve", lambda e, sg=sg, pu=pu, n=n, n0=n0, fc=fc: e.tensor_tensor(
                            out=big[:, fc, n0:n0 + n], in0=sg[:, :n], in1=psum[:, pu, :n], op=ALU.mult),
                            reads=[R_sg, R_ps[pu]], writes=[R_big])
            def ev(oc, si, n0, n, pb):
                P.op("dve", lambda e: e.tensor_tensor(out=xres[:, oc, n0:n0 + n], in0=xres[:, oc, n0:n0 + n],
                                                      in1=psum[:, pb, :n], op=ALU.add),
                     reads=[R_x, R_ps[pb]], writes=[R_x])
            linear(Wdn, R_W[kd], NFC, 0, D, 128, lambda kc, n0, n: big[:, kc, n0:n0 + n], [R_big], subs, ev)

        SUBS3 = [(0, SUB), (SUB, SUB), (2 * SUB, SUB)]
        SUBS2 = [(HALO, 512), (HALO + 512, 512)]
        for i in range(nslot):
            c0 = SLOT * i
            P.dma("sp", xres[:], xq_v[:, :, c0:c0 + SLOT], writes=[R_x])
            P.dma("sp", ATv, ATs[:, :, c0:c0 + SLOT].rearrange("c p n -> p c n"), reads=R_AT, writes=[R_big])
            def ev0(oc, si, n0, n, pb):
                P.op("dve", lambda e: e.tensor_tensor(out=xres[:, oc, n0:n0 + n], in0=xres[:, oc, n0:n0 + n],
                                                      in1=psum[:, pb, :n], op=ALU.add),
                     reads=[R_x, R_ps[pb]], writes=[R_x])
            linear(Wo0b, R_W["Wo0b"], 8, 0, D, 256, lambda kc, n0, n: big[:, kc, n0:n0 + n], [R_big], SUBS3, ev0)
            ffn(Wg0b, Wu0b, Wd0b, "Wg0b", "Wu0b", "Wd0b", SUBS3)
            norm_slot(SUBS3)
            for g in range(4):
                wb, R_wb = wbufs.next()
                wv = wb[:, :8 * 128].rearrange("p (k c) -> p k c", c=128)
                P.dma("sp", wv, W1b[:, :, 128 * g:128 * g + 128].rearrange("k p c -> p k c"), reads=[R_W["W1b"]], writes=[R_wb])
                pw, R_pw = wbufs.next()
                P.dma("sp", pw[:, :128], Pwb[g], reads=[R_W["Pwb"]], writes=[R_pw])
                xp, R_xp = xps.next()
                for (n0, n) in SUBS3:
                    pb = next_pb()
                    for kc in range(8):
                        P.op("pe", lambda e, kc=kc, pb=pb, n0=n0, n=n, wv=wv: e.matmul(
                            psum[:, pb, :n], lhsT=wv[:, kc, :], rhs=hb[:, kc, n0:n0 + n], start=(kc == 0), stop=(kc == 7)),
                            reads=[R_wb, R_hb], writes=[R_ps[pb]])
                    P.op("act", lambda e, pb=pb, n0=n0, n=n, xp=xp: e.activation(out=xp[:, n0:n0 + n], in_=psum[:, pb, :n], func=AF.Copy),
                         reads=[R_ps[pb]], writes=[R_xp])
                cur, R_cur = xp, R_xp
                lo, wdt = 0, 1
                for k in range(g + 1):
                    nx, R_nx = pa.next()
                    lo2 = lo + wdt
                    P.op("dve", lambda e, cur=cur, nx=nx, lo2=lo2, wdt=wdt: e.tensor_tensor(
                        out=nx[:, lo2:SLOT], in0=cur[:, lo2:SLOT], in1=cur[:, lo2 - wdt:SLOT - wdt], op=ALU.add),
                        reads=[R_cur], writes=[R_nx])
                    cur, R_cur, lo, wdt = nx, R_nx, lo2, wdt * 2
                rc, R_rc = rcs.next()
                P.dma("sp", rc[:], rcnt[g:g + 1, c0 + HALO:c0 + SLOT].to_broadcast([128, BQ]), writes=[R_rc])
                P.op("dve", lambda e, cur=cur, rc=rc: e.tensor_tensor(out=cur[:, HALO:SLOT], in0=cur[:, HALO:SLOT], in1=rc[:], op=ALU.mult),
                     reads=[R_cur, R_rc], writes=[R_cur])
                pl, R_pl = plb.next()
                P.op("dve", lambda e, cur=cur, xp=xp, pl=pl: e.tensor_tensor(out=pl[:], in0=cur[:, HALO:SLOT], in1=xp[:, HALO:SLOT], op=ALU.subtract),
                     reads=[R_cur, R_xp], writes=[R_pl])
                for hi in range(2):
                    pb = next_pb()
                    P.op("pe", lambda e, pb=pb, hi=hi, pw=pw, pl=pl: e.matmul(psum[:, pb, :512], lhsT=pw[:, :128], rhs=pl[:, hi * 512:(hi + 1) * 512],
                                                                               start=True, stop=True),
                         reads=[R_pw, R_pl], writes=[R_ps[pb]])
                    P.op("act", lambda e, pb=pb, hi=hi, g=g: e.activation(out=big[:, 8 + g, hi * 512:(hi + 1) * 512], in_=psum[:, pb, :512],
                                                                            func=AF.Copy, scale=vc[:, 1 + g:2 + g]),
                         reads=[R_ps[pb], R_vc], writes=[R_big])
            for cc in range(4):
                wa, R_wa = wbufs.next(); wg_, R_wg_ = wbufs.next()
                wav = wa[:, :8 * 128].rearrange("p (k c) -> p k c", c=128)
                wgv = wg_[:, :8 * 128].rearrange("p (k c) -> p k c", c=128)
                P.dma("sp", wav, W1b[:, :, 512 + 128 * cc:640 + 128 * cc].rearrange("k p c -> p k c"), reads=[R_W["W1b"]], writes=[R_wa])
                P.dma("sp", wgv, W1b[:, :, 1024 + 128 * cc:1152 + 128 * cc].rearrange("k p c -> p k c"), reads=[R_W["W1b"]], writes=[R_wg_])
                for (n0, n) in SUBS3:
                    pa_ = next_pb(0, 6); pg_ = next_pb(0, 6)
                    for kc in range(8):
                        P.op("pe", lambda e, kc=kc, pa_=pa_, n0=n0, n=n, wav=wav: e.matmul(
                            psum[:, pa_, :n], lhsT=wav[:, kc, :], rhs=hb[:, kc, n0:n0 + n], start=(kc == 0), stop=(kc == 7)),
                            reads=[R_wa, R_hb], writes=[R_ps[pa_]])
                    for kc in range(8):
                        P.op("pe", lambda e, kc=kc, pg_=pg_, n0=n0, n=n, wgv=wgv: e.matmul(
                            psum[:, pg_, :n], lhsT=wgv[:, kc, :], rhs=hb[:, kc, n0:n0 + n], start=(kc == 0), stop=(kc == 7)),
                            reads=[R_wg_, R_hb], writes=[R_ps[pg_]])
                    sg, R_sg = sgs.next()
                    P.op("act", lambda e, sg=sg, pg_=pg_, n=n: e.activation(out=sg[:, :n], in_=psum[:, pg_, :n], func=AF.Sigmoid),
                         reads=[R_ps[pg_]], writes=[R_sg])
                    P.op("dve", lambda e, sg=sg, pa_=pa_, n=n, n0=n0, cc=cc: e.tensor_tensor(
                        out=big[:, cc, n0:n0 + n], in0=sg[:, :n], in1=psum[:, pa_, :n], op=ALU.mult),
                        reads=[R_sg, R_ps[pa_]], writes=[R_big])
            for hi in range(2):
                t0 = hi * 512
                for cc in range(4):
                    diag, R_diag = diags.next()
                    for j in range(31):
                        P.op("dve", lambda e, cc=cc, j=j, diag=diag: e.tensor_scalar(out=diag[:, j, :], in0=idf[:], scalar1=dwt[:, cc, j:j + 1],
                                                                          scalar2=None, op0=ALU.mult),
                             reads=[R_idf, R_dwt], writes=[R_diag])
                    pb = next_pb()
                    for j in range(31):
                        P.op("pe", lambda e, cc=cc, j=j, pb=pb, t0=t0, diag=diag: e.matmul(
                            psum[:, pb, :512], lhsT=diag[:, j, :], rhs=big[:, cc, t0 + 2 + j:t0 + 2 + j + 512],
                            start=(j == 0), stop=(j == 30)), reads=[R_diag, R_big], writes=[R_ps[pb]])
                    P.op("act", lambda e, cc=cc, pb=pb: e.activation(out=ucf[:, cc, :], in_=psum[:, pb, :512], func=AF.Identity,
                                                                      bias=vc[:, 5 + cc:6 + cc], scale=1.0),
                         reads=[R_ps[pb], R_vc], writes=[R_ucf])
                P.op("dve", lambda e: e.tensor_copy(out=ucb, in_=ucf[:]), reads=[R_ucf], writes=[R_ucb])
                P.op("act", lambda e: e.activation(out=ucq, in_=ucf[:], func=AF.Square), reads=[R_ucf], writes=[R_ucq])
                for cc in range(4):
                    P.op("pe", lambda e, cc=cc: e.matmul(psum[:, 6, :512], lhsT=ONES, rhs=ucb[:, cc, :], start=(cc == 0), stop=(cc == 3)),
                         reads=[R_ucb, R_cm], writes=[R_ps[6]])
                for cc in range(4):
                    P.op("pe", lambda e, cc=cc: e.matmul(psum[:, 7, :512], lhsT=ONES, rhs=ucq[:, cc, :], start=(cc == 0), stop=(cc == 3)),
                         reads=[R_ucq, R_cm], writes=[R_ps[7]])
                mt, R_mt = mts.next(); m2, R_m2 = mts.next(); vt, R_vt = mts.next()
                P.op("dve", lambda e, mt=mt: e.tensor_scalar(out=mt[:], in0=psum[:, 6, :512], scalar1=1.0 / 512, scalar2=None, op0=ALU.mult),
                     reads=[R_ps[6]], writes=[R_mt])
                P.op("dve", lambda e, mt=mt, m2=m2: e.tensor_tensor(out=m2[:], in0=mt[:], in1=mt[:], op=ALU.mult), reads=[R_mt], writes=[R_m2])
                P.op("dve", lambda e, m2=m2, vt=vt: e.scalar_tensor_tensor(out=vt[:], in0=psum[:, 7, :512], scalar=1.0 / 512, in1=m2[:],
                                                                            op0=ALU.mult, op1=ALU.subtract),
                     reads=[R_ps[7], R_m2], writes=[R_vt])
                P.op("act", lambda e, vt=vt: e.activation(out=vt[:], in_=vt[:], func=AF.Ln, bias=EPS, scale=1.0), reads=[R_vt], writes=[R_vt])
                P.op("act", lambda e, vt=vt: e.activation(out=vt[:], in_=vt[:], func=AF.Exp, scale=-0.5), reads=[R_vt], writes=[R_vt])
                for cc in range(4):
                    P.op("dve", lambda e, cc=cc, mt=mt: e.tensor_tensor(out=ucf[:, cc, :], in0=ucf[:, cc, :], in1=mt[:], op=ALU.subtract),
                         reads=[R_ucf, R_mt], writes=[R_ucf])
                    P.op("dve", lambda e, cc=cc, vt=vt: e.tensor_tensor(out=ucf[:, cc, :], in0=ucf[:, cc, :], in1=vt[:], op=ALU.mult),
                         reads=[R_ucf, R_vt], writes=[R_ucf])
                    P.op("act", lambda e, cc=cc, t0=t0: e.activation(out=big[:, 12 + cc, t0:t0 + 512], in_=ucf[:, cc, :], func=AF.Silu,
                                                                      bias=vc[:, 13 + cc:14 + cc], scale=vc[:, 9 + cc:10 + cc]),
                         reads=[R_ucf, R_vc], writes=[R_big])
            def ev1(oc, si, n0, n, pb):
                P.op("dve", lambda e: e.tensor_tensor(out=xres[:, oc, HALO + n0:HALO + n0 + n], in0=xres[:, oc, HALO + n0:HALO + n0 + n],
                                                      in1=psum[:, pb, :n], op=ALU.add),
                     reads=[R_x, R_ps[pb]], writes=[R_x])
            linear(Wo1b, R_W["Wo1b"], 8, 0, D, 256, lambda kc, n0, n: big[:, 8 + kc, n0:n0 + n], [R_big], [(0, 512), (512, 512)], ev1)
            ffn(Wg1b, Wu1b, Wd1b, "Wg1b", "Wu1b", "Wd1b", SUBS2)
            for (n0, n) in SUBS2:
                sq, R_sq = sqs.next(); lnv, R_lnv = lnvs.next(); rr, R_rr = rrs.next()
                P.op("act", lambda e, sq=sq, n0=n0, n=n: e.activation(out=sq[:, :, :n], in_=xres[:, :, n0:n0 + n], func=AF.Square),
                     reads=[R_x], writes=[R_sq])
                for kc in range(8):
                    P.op("pe", lambda e, kc=kc, sq=sq, n=n: e.matmul(psum[:, 7, :n], lhsT=ONES, rhs=sq[:, kc, :n], start=(kc == 0), stop=(kc == 7)),
                         reads=[R_sq, R_cm], writes=[R_ps[7]])
                rstd_from_ps(psum[:, 7, :n], R_ps[7], 1.0 / D, lnv, R_lnv, rr, R_rr, n)
                for kc in range(8):
                    ot, R_ot = ots.next()
                    P.op("dve", lambda e, kc=kc, ot=ot, rr=rr, n0=n0, n=n: e.scalar_tensor_tensor(
                        out=ot[:, :n], in0=xres[:, kc, n0:n0 + n], scalar=gn[:, 4, kc:kc + 1], in1=rr[:, :n], op0=ALU.mult, op1=ALU.mult),
                        reads=[R_x, R_rr, R_gn], writes=[R_ot])
                    o0 = BQ * i + n0 - HALO
                    P.dma("pool", outT[kc * 128:(kc + 1) * 128, o0:o0 + n], ot[:, :n], reads=[R_ot], writes=[R_out])
        P.barrier()
        P.emit()
    return nc


def _const_mats():
    ones = np.ones((128, 128), np.float32)
    p_ = np.arange(128)
    trineg = -(p_[:, None] >= p_[None, :]).astype(np.float32)
    onesneg = -ones
    rot = np.zeros((128, 128), np.float32)
    for p in range(128):
        if p % 64 < 32:
            rot[p + 32, p] = -1.0
        else:
            rot[p - 32, p] = 1.0
    ident = np.eye(128, dtype=np.float32)
    return np.stack([ones, trineg, onesneg, rot, ident], 1).astype(NPBF), ident


def _rope_tables(pos):
    inv = (10000.0 ** (-(np.arange(0, 64, 2, dtype=np.float32)) / 64.0)).astype(np.float32)
    ang = pos.astype(np.float32)[None, :] * inv[np.arange(128) % 32][:, None]
    return np.cos(ang).astype(np.float32), np.sin(ang).astype(np.float32)


def make_in_maps(inputs, nslot):
    G = geom(nslot)
    T, TL, NQ, NB = G["T"], G["TL"], G["NQ"], G["NB"]
    x = np.asarray(inputs["x"], np.float32)
    cmat, ident = _const_mats()
    sbm, _, dm, _ = build_masks()
    f = lambda k: np.ascontiguousarray(np.asarray(inputs[k], np.float32))
    gains = np.zeros((128, 6, 8), np.float32)
    for gi, k in enumerate(["mix_norm_0", "ffn_norm_0", "mix_norm_1", "ffn_norm_1", "final_norm"]):
        gains[:, gi, :] = f(k).reshape(8, 128).T
    vecs = np.zeros((128, 24), np.float32)
    vecs[:, 0] = f("subln_0")
    vecs[:, 1:5] = f("pool_scale_1").reshape(4, 128).T
    vecs[:, 5:9] = f("dw_b_1").reshape(4, 128).T
    vecs[:, 9:13] = f("conv_norm_g_1").reshape(4, 128).T
    vecs[:, 13:17] = f("conv_norm_b_1").reshape(4, 128).T
    lamv = np.stack([f("lambda_q1_0"), f("lambda_k1_0"), f("lambda_q2_0"), f("lambda_k2_0")], 0)
    dwT = np.ascontiguousarray(f("dw_w_1").T.reshape(4, 128, 31).transpose(1, 0, 2))
    shared = dict(w_in_0=f("w_in_0"), w_out_0=f("w_out_0"), w_gate_0=f("w_gate_0"), w_up_0=f("w_up_0"), w_down_0=f("w_down_0"),
                  w_in_1=f("w_in_1"), w_out_1=f("w_out_1"), w_gate_1=f("w_gate_1"), w_up_1=f("w_up_1"), w_down_1=f("w_down_1"),
                  pool_w_1=f("pool_w_1"), gains=gains, vecs=vecs, lamv=lamv, dwT=dwT, cmat=cmat, identf=ident,
                  sbmask=sbm, dmask=dm)
    maps = []
    for c in range(8):
        b, j = c // 4, c % 4
        off = 1024 * (3 - j)
        xT = x[b, :T].T
        xkv = np.zeros((D, TL), np.float32)
        xkv[:, off:] = xT[:, :TL - off]
        posk = np.maximum(np.arange(TL) - off, 0)
        cosk, sink = _rope_tables(posk)
        xq = np.zeros((D, NQ), np.float32)
        posq = np.zeros(NQ, np.int64)
        rc = np.zeros((4, NQ), np.float32)
        for i in range(nslot):
            a0 = 1024 * (j + 4 * i) - HALO
            tt = np.arange(a0, a0 + SLOT)
            valid = tt >= 0
            xq[:, SLOT * i:SLOT * (i + 1)][:, valid] = xT[:, tt[valid]]
            posq[SLOT * i:SLOT * (i + 1)] = np.maximum(tt, 0)
            for g, w in enumerate((2, 4, 8, 16)):
                rc[g, SLOT * i:SLOT * (i + 1)] = 1.0 / np.minimum(np.maximum(tt, 0) + 1, w)
        cosq, sinq = _rope_tables(posq)
        vones = np.zeros((128, 3, 128), np.float32)
        vones[:, 3 - j:, :] = 1.0
        m = dict(shared)
        vt = np.zeros((128, G["NKT"]), np.float32)
        vt[:, 8 * (3 - j):] = 1.0
        m.update(xkv=xkv, xq=xq, cosk=cosk, sink=sink, cosq=cosq, sinq=sinq, rcnt=rc, vones=vones, vtile=vt)
        maps.append(m)
    return maps


_NC_CACHE = {}


def run(inputs, nslot, debug=False):
    key = (nslot, debug)
    if key not in _NC_CACHE:
        _NC_CACHE[key] = build_nc(nslot, debug)
    nc = _NC_CACHE[key]
    maps = make_in_maps(inputs, nslot)
    res = run_bass_kernel_spmd(nc, maps, core_ids=list(range(8)))
    T = 4096 * nslot
    out = np.zeros((2, T, D), np.float32)
    for c in range(8):
        b, j = c // 4, c % 4
        oT = res.results[c]["outT"]
        for i in range(nslot):
            a0 = 1024 * (j + 4 * i)
            out[b, a0:a0 + 1024, :] = oT[:, BQ * i:BQ * (i + 1)].T
    return out, res


def kernel(**inputs):
    out, _ = run(inputs, 4)
    return out
```
